# Optimizing a Trainium2 kernel written in Bass

```python
import jax, jax.numpy as jnp
from jax import lax
import numpy as np

D_MODEL = 1024
BATCH = 8
SEQ = 4096
DEPTH = 2

GRID_W = 64
CTX_LEN = 256
CONV_W = 1024
CONV_K = 3
N_HEADS = 8
N_KV_HEADS = 2
HEAD_DIM = 128
GROUP = N_HEADS // N_KV_HEADS
Q_BLOCK = 128
ROPE_THETA = 10000.0
ROPE_AXIS_DIM = HEAD_DIM // 2
ROPE_PAIRS = ROPE_AXIS_DIM // 2
ATTN_SCALE = HEAD_DIM ** -0.5
GLA_HEADS = 4
GLA_DK = D_MODEL // 2
GLA_DV = D_MODEL
GLA_DKH = GLA_DK // GLA_HEADS
GLA_DVH = GLA_DV // GLA_HEADS
GLA_RANK = 16
GLA_TAU = 16.0
GLA_CHUNK = 64
N_BRANCH = 3
EPS = 1e-6

IN_SPLITS = (CONV_W, CONV_W, CONV_W, CONV_W,
             N_HEADS * HEAD_DIM, N_KV_HEADS * HEAD_DIM, N_KV_HEADS * HEAD_DIM,
             N_HEADS * HEAD_DIM,
             GLA_DK, GLA_DK, GLA_DV, GLA_RANK, GLA_RANK, GLA_DV,
             N_BRANCH * D_MODEL)
IN_WIDTH = sum(IN_SPLITS)

kernel_name = "hybrid_conv_gqa_gla_prefix_dit"


def _rmsnorm(x, g):
    xf = x.astype(jnp.float32)
    y = xf * lax.rsqrt(jnp.mean(xf * xf, axis=-1, keepdims=True) + EPS)
    return (y * g.astype(jnp.float32)).astype(x.dtype)


def _split_proj(p):
    offsets = np.cumsum(IN_SPLITS)[:-1].tolist()
    return jnp.split(p, offsets, axis=-1)


def _short_conv(u, w):
    up = jnp.pad(u, ((0, 0), (1, 1), (0, 0)))
    return w[0] * up[:, :-2] + w[1] * up[:, 1:-1] + w[2] * up[:, 2:]


def _axial_rope_tables(n_tokens):
    n_rows = n_tokens // GRID_W
    row = jnp.repeat(jnp.arange(n_rows, dtype=jnp.float32), GRID_W)
    col = jnp.tile(jnp.arange(GRID_W, dtype=jnp.float32), n_rows)
    freqs = ROPE_THETA ** (-jnp.arange(ROPE_PAIRS, dtype=jnp.float32) * 2.0 / ROPE_AXIS_DIM)
    ang = jnp.stack([row[:, None] * freqs, col[:, None] * freqs], axis=1)
    return jnp.cos(ang), jnp.sin(ang)


def _apply_rope(x, cos, sin):
    b_, t_, h_, _ = x.shape
    xr = x.reshape(b_, t_, h_, 2, 2, ROPE_PAIRS)
    x1, x2 = xr[..., 0, :], xr[..., 1, :]
    c_, s_ = cos[None, :, None], sin[None, :, None]
    out = jnp.stack([x1 * c_ - x2 * s_, x2 * c_ + x1 * s_], axis=-2)
    return out.reshape(b_, t_, h_, HEAD_DIM).astype(x.dtype)


def _attn_heads(q, k, v, q_g, k_g, rope):
    b_, t_ = q.shape[:2]
    q = _rmsnorm(q.reshape(b_, t_, N_HEADS, HEAD_DIM), q_g)
    k = _rmsnorm(k.reshape(b_, t_, N_KV_HEADS, HEAD_DIM), k_g)
    v = v.reshape(b_, t_, N_KV_HEADS, HEAD_DIM)
    if rope is not None:
        q = _apply_rope(q, *rope)
        k = _apply_rope(k, *rope)
    return q, k, v


def _sdpa_blocks(q, keys, vals):
    b_, t_ = q.shape[:2]
    nb = t_ // Q_BLOCK
    qb = q.reshape(b_, nb, Q_BLOCK, N_KV_HEADS, GROUP, HEAD_DIM).transpose(1, 0, 2, 3, 4, 5)

    def one_block(qblk):
        s = jnp.einsum('bqkgd,bskd->bkgqs', qblk, keys).astype(jnp.float32) * ATTN_SCALE
        p = jax.nn.softmax(s, axis=-1).astype(vals.dtype)
        return jnp.einsum('bkgqs,bskd->bqkgd', p, vals)

    o = lax.map(one_block, qb)
    return o.transpose(1, 0, 2, 3, 4, 5).reshape(b_, t_, N_HEADS * HEAD_DIM)


def _gla_inputs(q, k, v, r_f, r_b, w_df, b_df, w_db, b_db):
    b_, t_ = q.shape[:2]
    q = q.reshape(b_, t_, GLA_HEADS, GLA_DKH) * (GLA_DKH ** -0.5)
    k = k.reshape(b_, t_, GLA_HEADS, GLA_DKH)
    v = v.reshape(b_, t_, GLA_HEADS, GLA_DVH)
    la_f = (jax.nn.log_sigmoid((r_f @ w_df + b_df).astype(jnp.float32)) / GLA_TAU).reshape(b_, t_, GLA_HEADS, GLA_DKH)
    la_b = (jax.nn.log_sigmoid((r_b @ w_db + b_db).astype(jnp.float32)) / GLA_TAU).reshape(b_, t_, GLA_HEADS, GLA_DKH)
    return q, k, v, la_f, la_b


def _gla_scan(q, k, v, log_a, s0):
    b_, t_, h_, _ = q.shape
    dv = v.shape[-1]
    nc = t_ // GLA_CHUNK

    def chunks(a):
        return a.astype(jnp.float32).reshape(b_, nc, GLA_CHUNK, h_, a.shape[-1]).transpose(1, 0, 3, 2, 4)

    mask = jnp.tril(jnp.ones((GLA_CHUNK, GLA_CHUNK), dtype=bool))

    def step(s, inp):
        qc, kc, vc, ac = inp
        bcum = jnp.cumsum(ac, axis=2)
        o_inter = jnp.einsum('bhtd,bhde->bhte', qc * jnp.exp(bcum), s)
        diff = bcum[:, :, :, None, :] - bcum[:, :, None, :, :]
        decay = jnp.exp(jnp.where(mask[:, :, None], diff, -jnp.inf))
        att = jnp.einsum('bhtd,bhsd,bhtsd->bhts', qc, kc, decay)
        o_intra = jnp.einsum('bhts,bhse->bhte', att, vc)
        b_last = bcum[:, :, -1:, :]
        s_new = jnp.exp(b_last[:, :, 0, :])[..., None] * s + jnp.einsum('bhsd,bhse->bhde', kc * jnp.exp(b_last - bcum), vc)
        return s_new, o_inter + o_intra

    s_f, o = lax.scan(step, s0, (chunks(q), chunks(k), chunks(v), chunks(log_a)))
    o = o.transpose(1, 0, 3, 2, 4).reshape(b_, t_, h_, dv)
    return o.astype(v.dtype), s_f


def _flip(a):
    return jnp.flip(a, axis=1)


def _gla_bidirectional(ctx_in, lat_in):
    qc, kc, vc, lfc, lbc = ctx_in
    ql, kl, vl, lfl, lbl = lat_in
    s0 = jnp.zeros((qc.shape[0], GLA_HEADS, GLA_DKH, GLA_DVH), jnp.float32)
    o_cf, s_f = _gla_scan(qc, kc, vc, lfc, s0)
    o_cb, s_b = _gla_scan(_flip(qc), _flip(kc), _flip(vc), _flip(lbc), s0)
    o_lf, _ = _gla_scan(ql, kl, vl, lfl, s_f)
    o_lb, _ = _gla_scan(_flip(ql), _flip(kl), _flip(vl), _flip(lbl), s_b)
    return o_lf + _flip(o_lb), o_cf + _flip(o_cb)


def _mixer_output(parts, att_o, gla_o, conv_w_l, gla_g_l, w_br_conv_l, w_br_attn_l, w_br_gla_l, b_gate_l, w_out_l):
    a_b, a_c, a_x, a_z = parts[0], parts[1], parts[2], parts[3]
    z_attn, z_gla, mg = parts[7], parts[13], parts[14]
    b_, t_ = a_b.shape[:2]
    br_a = ((a_b * _short_conv(a_c * a_x, conv_w_l)) * jax.nn.silu(a_z)) @ w_br_conv_l
    br_b = (att_o * jax.nn.silu(z_attn)) @ w_br_attn_l
    br_c = (_rmsnorm(gla_o, gla_g_l).reshape(b_, t_, GLA_DV) * jax.nn.silu(z_gla)) @ w_br_gla_l
    g_a, g_b, g_c = jnp.split(jax.nn.sigmoid(mg + b_gate_l), N_BRANCH, axis=-1)
    return (g_a * br_a + g_b * br_b + g_c * br_c) @ w_out_l


def setup_inputs(seed: int = 0) -> dict:
    key = jax.random.key(seed)
    ks = jax.random.split(key, 24)
    n = jax.random.normal
    f32 = jnp.float32
    return {
        "x": n(ks[0], (BATCH, SEQ, D_MODEL), f32),
        "c": n(ks[1], (BATCH, D_MODEL), f32),
        "ctx": n(ks[2], (BATCH, CTX_LEN, D_MODEL), f32),
        "c_ctx": n(ks[3], (D_MODEL,), f32),
        "w_ada": n(ks[4], (DEPTH, D_MODEL, 3 * D_MODEL), f32) * D_MODEL ** -0.5,
        "b_ada": 0.02 * n(ks[5], (DEPTH, 3 * D_MODEL), f32),
        "g_pre": 1.0 + 0.02 * n(ks[6], (DEPTH, D_MODEL), f32),
        "g_post": 1.0 + 0.02 * n(ks[7], (DEPTH, D_MODEL), f32),
        "w_in": n(ks[8], (DEPTH, D_MODEL, IN_WIDTH), f32) * D_MODEL ** -0.5,
        "conv_w": n(ks[9], (DEPTH, CONV_K, CONV_W), f32) * CONV_K ** -0.5,
        "q_norm_g": 1.0 + 0.02 * n(ks[10], (DEPTH, HEAD_DIM), f32),
        "k_norm_g": 1.0 + 0.02 * n(ks[11], (DEPTH, HEAD_DIM), f32),
        "w_decay_fwd": n(ks[12], (DEPTH, GLA_RANK, GLA_DK), f32) * GLA_RANK ** -0.5,
        "b_decay_fwd": 0.01 * n(ks[13], (DEPTH, GLA_DK), f32),
        "w_decay_bwd": n(ks[14], (DEPTH, GLA_RANK, GLA_DK), f32) * GLA_RANK ** -0.5,
        "b_decay_bwd": 0.01 * n(ks[15], (DEPTH, GLA_DK), f32),
        "gla_norm_g": 1.0 + 0.02 * n(ks[16], (DEPTH, GLA_DVH), f32),
        "w_br_conv": n(ks[17], (DEPTH, CONV_W, D_MODEL), f32) * CONV_W ** -0.5,
        "w_br_attn": n(ks[18], (DEPTH, N_HEADS * HEAD_DIM, D_MODEL), f32) * (N_HEADS * HEAD_DIM) ** -0.5,
        "w_br_gla": n(ks[19], (DEPTH, GLA_DV, D_MODEL), f32) * GLA_DV ** -0.5,
        "b_gate": 0.02 * n(ks[20], (DEPTH, N_BRANCH * D_MODEL), f32),
        "w_out": n(ks[21], (DEPTH, D_MODEL, D_MODEL), f32) * D_MODEL ** -0.5,
    }


def reference(x, c, ctx, c_ctx, w_ada, b_ada, g_pre, g_post, w_in, conv_w, q_norm_g, k_norm_g,
              w_decay_fwd, b_decay_fwd, w_decay_bwd, b_decay_bwd, gla_norm_g,
              w_br_conv, w_br_attn, w_br_gla, b_gate, w_out):
    n_tok = x.shape[1]
    rope = _axial_rope_tables(n_tok)
    xc = ctx
    for l in range(DEPTH):
        last = l == DEPTH - 1
        mod_l = jax.nn.silu(c) @ w_ada[l] + b_ada[l]
        sh_l, sc_l, gt_l = jnp.split(mod_l[:, None, :], 3, axis=-1)
        mod_c = jax.nn.silu(c_ctx) @ w_ada[l] + b_ada[l]
        sh_c, sc_c, gt_c = jnp.split(mod_c, 3, axis=-1)
        h_l = _rmsnorm(x, g_pre[l]) * (1.0 + sc_l) + sh_l
        h_c = _rmsnorm(xc, g_pre[l]) * (1.0 + sc_c) + sh_c
        pl = _split_proj(h_l @ w_in[l])
        pc = _split_proj(h_c @ w_in[l])
        q_l, k_l, v_l = _attn_heads(pl[4], pl[5], pl[6], q_norm_g[l], k_norm_g[l], rope)
        q_c, k_c, v_c = _attn_heads(pc[4], pc[5], pc[6], q_norm_g[l], k_norm_g[l], None)
        keys = jnp.concatenate([k_l, k_c], axis=1)
        vals = jnp.concatenate([v_l, v_c], axis=1)
        att_l = _sdpa_blocks(q_l, keys, vals)
        dec = (w_decay_fwd[l], b_decay_fwd[l], w_decay_bwd[l], b_decay_bwd[l])
        gla_c_in = _gla_inputs(pc[8], pc[9], pc[10], pc[11], pc[12], *dec)
        gla_l_in = _gla_inputs(pl[8], pl[9], pl[10], pl[11], pl[12], *dec)
        gla_l, gla_c = _gla_bidirectional(gla_c_in, gla_l_in)
        shared = (conv_w[l], gla_norm_g[l], w_br_conv[l], w_br_attn[l], w_br_gla[l], b_gate[l], w_out[l])
        out_l = _mixer_output(pl, att_l, gla_l, *shared)
        if not last:
            att_c = _sdpa_blocks(q_c, k_c, v_c)
            out_c = _mixer_output(pc, att_c, gla_c, *shared)
            xc = xc + gt_c * _rmsnorm(out_c, g_post[l])
        x = x + gt_l * _rmsnorm(out_l, g_post[l])
    return x
```

```python
import numpy as np
import ml_dtypes
from contextlib import ExitStack
import concourse.bass as bass
import concourse.mybir as mybir
from concourse.bass_utils import run_bass_kernel_spmd

F32 = mybir.dt.float32
BF16 = mybir.dt.bfloat16
AF = mybir.ActivationFunctionType
ALU = mybir.AluOpType

D = 1024
KC = 8
T = 4352
NCH = 34
EPS = 1e-6
IN_W = 12832
O_AB, O_AC, O_AX, O_AZ = 0, 1024, 2048, 3072
O_Q, O_K, O_V, O_ZA = 4096, 5120, 5376, 5632
O_GQ, O_GK, O_GV, O_RF, O_RB, O_ZG, O_MG = 6656, 7168, 7680, 8704, 8720, 8736, 9760
ATTN_SCALE = 128 ** -0.5
GLA_SCALE = 128 ** -0.5
GLA_TAU = 16.0


def tile_rng(tile):
    return (0, 256) if tile == 0 else (256 + (tile - 1) * 512, 512)


class Buf:
    __slots__ = ("name", "w", "r")

    def __init__(self, name=""):
        self.name = name
        self.w = {}
        self.r = {}


class Sched:
    def __init__(self, nc):
        self.nc = nc
        self.engs = {"pe": nc.tensor, "act": nc.scalar, "dve": nc.vector,
                     "pool": nc.gpsimd, "sp": nc.sync}
        self.sems = {}
        self.cnt = {}
        for k in self.engs:
            self.sems["e_" + k] = nc.alloc_semaphore("e_" + k)
            self.cnt["e_" + k] = 0
        self.known = {k: {} for k in self.engs}
        self.nwaits = 0
        self.nops = 0

    def _deps(self, reads, writes, pwrites):
        need = {}

        def add(d):
            for k, v in d.items():
                if v > need.get(k, 0):
                    need[k] = v
        for b in reads:
            add(b.w)
        for b in writes:
            add(b.w)
            add(b.r)
        for b in pwrites:
            add(b.r)
        return need

    def _wait(self, eng, need):
        kn = self.known[eng]
        for k, v in need.items():
            if kn.get(k, 0) >= v:
                continue
            self.engs[eng].wait_ge(self.sems[k], v)
            kn[k] = v
            self.nwaits += 1

    def _mark(self, k, v, reads, writes, pwrites):
        for b in reads:
            if b.r.get(k, 0) < v:
                b.r[k] = v
        for b in writes:
            b.w = {k: v}
            b.r = {}
        for b in pwrites:
            if b.w.get(k, 0) < v:
                b.w[k] = v

    def op(self, eng, fn, reads=(), writes=(), pwrites=()):
        own = "e_" + eng
        need = self._deps(reads, writes, pwrites)
        if eng == "pe":
            need.pop(own, None)
        self._wait(eng, need)
        inst = fn(self.engs[eng])
        self.cnt[own] += 1
        inst.then_inc(self.sems[own], 1)
        self._mark(own, self.cnt[own], reads, writes, pwrites)
        self.nops += 1

    def dma(self, eng, sem, out, in_, reads=(), writes=(), pwrites=(), **kw):
        if sem not in self.sems:
            self.sems[sem] = self.nc.alloc_semaphore(sem)
            self.cnt[sem] = 0
        need = self._deps(reads, writes, pwrites)
        if self.cnt[sem] > 0:
            need[sem] = max(need.get(sem, 0), self.cnt[sem])
        self._wait(eng, need)
        inst = self.engs[eng].dma_start(out=out, in_=in_, **kw)
        self.cnt[sem] += 16
        inst.then_inc(self.sems[sem], 16)
        self._mark(sem, self.cnt[sem], reads, writes, pwrites)
        self.nops += 1

    def barrier(self):
        need = {k: v for k, v in self.cnt.items() if v > 0}
        for eng in self.engs:
            self._wait(eng, dict(need))


class Rot:
    def __init__(self, items):
        self.items = items
        self.i = 0

    def next(self):
        it = self.items[self.i % len(self.items)]
        self.i += 1
        return it


def build(n_layers=2, debug=(), stop_after=None):
    nc = bass.Bass("TRN2", target_bir_lowering=False)
    S = Sched(nc)

    def din(name, shape, dty=F32):
        return nc.dram_tensor(name, shape, dty, kind="ExternalInput").ap()

    def scr(name, shape, dty):
        kind = "ExternalOutput" if name in debug else "Internal"
        return nc.dram_tensor(name, shape, dty, kind=kind).ap()

    x_in = din("x", [4096, D]); ctx_in = din("ctx", [256, D])
    c2_d = din("c2_d", [128, 8, 2])
    w_ada = din("w_ada", [2, D, 3 * D]); b_adaT = din("b_adaT", [2, 128, 24])
    g_preT = din("g_preT", [2, 128, 8]); g_postT = din("g_postT", [2, 128, 8])
    w_in = din("w_in", [2, D, IN_W]); w_qp = din("w_qp", [2, D, 1024]); w_kp = din("w_kp", [2, D, 256])
    conv_wT = din("conv_wT", [2, 128, 8, 3])
    qg_d = din("qg", [2, 128, 2]); kg_d = din("kg", [2, 128, 2])
    wd_f = din("wd_f", [2, 17, 512]); wd_b = din("wd_b", [2, 17, 512])
    gla_g = din("gla_g", [2, 256])
    w_bra = din("w_br_conv", [2, D, D]); w_brb = din("w_br_attn", [2, D, D]); w_brc = din("w_br_gla", [2, D, D])
    b_gateT = din("b_gateT", [2, 128, 24]); w_out = din("w_out", [2, D, D])
    ident_d = din("ident", [128, 128]); tri_d = din("tri4", [4, 128, 128])
    cos_d = din("cosT", [128, 4096], BF16); sin_d = din("sinT", [128, 4096], BF16)
    y_out = nc.dram_tensor("y", [4096, D], F32, kind="ExternalOutput").ap()

    yaT = scr("yaT", [D, T], BF16); ybT = scr("ybT", [D, T], BF16); ycT = scr("ycT", [D, T], BF16)
    mgT = scr("mgT", [3 * D, T], BF16); szT = scr("szT", [D, T], BF16)
    gqT = scr("gqT", [512, T], F32); gkT = scr("gkT", [512, T], F32); gktm = scr("gktm", [T, 512], F32)
    gv = scr("gv", [T, D], BF16); laf = scr("laf", [T, 512], F32); lab = scr("lab", [T, 512], F32)
    of_d = scr("of", [T, D], F32)
    x1 = scr("x1", [4096, D], F32); xc1 = scr("xc1", [256, D], F32)
    gtg_d = scr("gtg", [2, 2, D], F32)
    B = {n: Buf(n) for n in ["yaT", "ybT", "ycT", "mgT", "szT", "gqT", "gkT", "gktm", "gv", "laf", "lab",
                             "of", "x1", "xc1", "gtg", "y"]}

    uid = [0]
    with ExitStack() as es0:
        def mk(es):
            def sb(name, shape, dty):
                uid[0] += 1
                return es.enter_context(nc.sbuf_tensor("%s_%d" % (name, uid[0]), shape, dty)), Buf(name)

            def ps(name, shape, dty=F32):
                uid[0] += 1
                return es.enter_context(nc.psum_tensor("%s_%d" % (name, uid[0]), shape, dty)), Buf(name)
            return sb, ps
        sb0, ps0 = mk(es0)
        identf, identf_b = sb0("identf", [128, 128], F32)
        identb, identb_b = sb0("identb", [128, 128], BF16)
        onesb, onesb_b = sb0("onesb", [128, 128], BF16)
        mod, mod_b = sb0("mod", [128, 24, 2], F32)
        gsc, gsc_b = sb0("gsc", [128, 8, 2], F32)
        gtgT, gtgT_b = sb0("gtgT", [128, 8, 2], F32)
        bank = [ps0("bank%d" % i, [128, 512], F32) for i in range(8)]

        S.dma("sp", "c0", out=identf[:], in_=ident_d[:, :], writes=[identf_b])
        S.op("dve", lambda e: e.tensor_copy(out=identb[:], in_=identf[:]), reads=[identf_b], writes=[identb_b])
        S.op("dve", lambda e: e.memset(onesb[:], 1.0), writes=[onesb_b])

        rr = [0]

        def load_w(src, W, wst_rot, dst, dst_b, off, first, cast_engs=("pool",), dma_engs=("sp",)):
            done = 0
            while done < W:
                w = min(256, W - done)
                st, st_b, sem = wst_rot.next()
                de = dma_engs[rr[0] % len(dma_engs)]
                ce = cast_engs[rr[0] % len(cast_engs)]
                rr[0] += 1
                S.dma(de, sem, out=st[:, :, 0:w],
                      in_=src[:, done:done + w].rearrange("(kc p) w -> p kc w", p=128), writes=[st_b])
                o = off + done
                if ce == "act":
                    fn = lambda e, st=st, w=w, o=o: e.activation(out=dst[:, :, o:o + w], in_=st[:, :, 0:w], func=AF.Copy)
                else:
                    fn = lambda e, st=st, w=w, o=o: e.tensor_copy(out=dst[:, :, o:o + w], in_=st[:, :, 0:w])
                if first and done == 0:
                    S.op(ce, fn, reads=[st_b], writes=[dst_b])
                else:
                    S.op(ce, fn, reads=[st_b], pwrites=[dst_b])
                done += w

        for l in range(n_layers):
            last = (l == n_layers - 1)
            xs_lat = x_in if l == 0 else x1
            xs_ctx = ctx_in if l == 0 else xc1
            xs_bufs = [] if l == 0 else [B["x1"], B["xc1"]]
            xd_lat = y_out if last else x1
            xd_lat_b = B["y"] if last else B["x1"]

            with ExitStack() as esL:
                sbL, _ = mk(esL)
                hT, _ = sbL("hT", [128, KC, T], BF16)
                hT_b = [Buf("hT%d" % i) for i in range(9)]
                wst = [sbL("wst%d" % i, [128, KC, 256], F32) for i in range(2)]
                wst_rot = Rot([(wst[i][0], wst[i][1], "wst%d" % i) for i in range(2)])
                wbf = [sbL("wbf%d" % i, [128, KC, 512], BF16) for i in range(2)]
                wbf_rot = Rot(wbf)

                with ExitStack() as es:
                    sb, _ = mk(es)
                    c2, c2_b = sb("c2", [128, 8, 2], F32)
                    sc2, sc2_b = sb("sc2", [128, 8, 2], F32)
                    bada, bada_b = sb("bada", [128, 24], F32)
                    gpre, gpre_b = sb("gpre", [128, 8], F32)
                    gpost, gpost_b = sb("gpost", [128, 8], F32)
                    psA, psA_b = bank[0]
                    psAv = psA[:, 0:48].rearrange("p (j t) -> p j t", t=2)
                    S.dma("sp", "c1", out=c2[:], in_=c2_d[:, :, :], writes=[c2_b])
                    S.dma("sp", "c2", out=bada[:], in_=b_adaT[l], writes=[bada_b])
                    S.dma("sp", "c3", out=gpre[:], in_=g_preT[l], writes=[gpre_b])
                    S.dma("sp", "c4", out=gpost[:], in_=g_postT[l], writes=[gpost_b])
                    S.op("act", lambda e: e.activation(out=sc2[:], in_=c2[:], func=AF.Silu), reads=[c2_b], writes=[sc2_b])
                    for blk in range(12):
                        st, st_b, sem = wst_rot.next()
                        S.dma("sp", sem, out=st[:],
                              in_=w_ada[l, :, blk * 256:(blk + 1) * 256].rearrange("(kc p) w -> p kc w", p=128),
                              writes=[st_b])
                        for jj in range(2):
                            j = blk * 2 + jj
                            for kc in range(KC):
                                S.op("pe", lambda e, st=st, jj=jj, j=j, kc=kc: e.matmul(
                                    psAv[:, j, :], lhsT=st[:, kc, jj * 128:(jj + 1) * 128], rhs=sc2[:, kc, :],
                                    start=(kc == 0), stop=(kc == KC - 1)),
                                    reads=[st_b, sc2_b], writes=[psA_b])
                    for j in range(2):
                        S.op("dve", lambda e, j=j: e.tensor_tensor(out=mod[:, :, j], in0=psAv[:, :, j], in1=bada[:], op=ALU.add),
                             reads=[psA_b, bada_b], writes=[mod_b])
                    for j in range(2):
                        S.op("dve", lambda e, j=j: e.scalar_tensor_tensor(
                            out=gsc[:, :, j], in0=mod[:, 8:16, j], scalar=1.0, in1=gpre[:], op0=ALU.add, op1=ALU.mult),
                            reads=[mod_b, gpre_b], writes=[gsc_b])
                        S.op("dve", lambda e, j=j: e.tensor_tensor(out=gtgT[:, :, j], in0=mod[:, 16:24, j], in1=gpost[:], op=ALU.mult),
                             reads=[mod_b, gpost_b], writes=[gtgT_b])
                    for j in range(2):
                        S.dma("sp", "c5", out=gtg_d[l, j].rearrange("(kc p) -> p kc", p=128), in_=gtgT[:, :, j],
                              reads=[gtgT_b], writes=[B["gtg"]], allow_slow_non_contiguous=True)
                    S.barrier()

                with ExitStack() as es:
                    sb, _ = mk(es)
                    xin = Rot([sb("xin%d" % i, [128, D], F32) for i in range(2)])
                    xn = Rot([sb("xn%d" % i, [128, D], BF16) for i in range(2)])
                    ssr = Rot([sb("ss%d" % i, [128, 1], F32) for i in range(2)])
                    rsr = Rot([sb("rs%d" % i, [128, 1], F32) for i in range(2)])
                    psr = Rot(bank[0:2])
                    def p1_front(i):
                        src = xs_ctx[i * 128:(i + 1) * 128, :] if i < 2 else xs_lat[(i - 2) * 128:(i - 1) * 128, :]
                        xi, xi_b = xin.next(); xv, xv_b = xn.next(); ss, ss_b = ssr.next(); rs, rs_b = rsr.next()
                        S.dma("sp", "xin%d" % (i % 2), out=xi[:], in_=src, reads=xs_bufs, writes=[xi_b])
                        S.op("act", lambda e: e.activation(out=xv[:], in_=xi[:], func=AF.Square, accum_out=ss[:]),
                             reads=[xi_b], writes=[xv_b, ss_b])
                        S.op("dve", lambda e: e.tensor_scalar(out=rs[:], in0=ss[:], scalar1=1.0 / D, scalar2=EPS,
                                                              op0=ALU.mult, op1=ALU.add), reads=[ss_b], writes=[rs_b])
                        return dict(i=i, xi=xi, xi_b=xi_b, xv=xv, xv_b=xv_b, rs=rs, rs_b=rs_b)

                    def p1_back(cx):
                        i, xi, xi_b, xv, xv_b, rs, rs_b = cx["i"], cx["xi"], cx["xi_b"], cx["xv"], cx["xv_b"], cx["rs"], cx["rs_b"]
                        j = 1 if i < 2 else 0
                        tile = 0 if i < 2 else 1 + (i - 2) // 4
                        pT, pT_b = psr.next()
                        pTv = pT[:].bitcast(BF16).rearrange("p (c t) -> p c t", t=128)
                        S.op("act", lambda e: e.activation(out=rs[:], in_=rs[:], func=AF.Ln), reads=[rs_b], writes=[rs_b])
                        S.op("act", lambda e: e.activation(out=rs[:], in_=rs[:], func=AF.Exp, scale=-0.5), reads=[rs_b], writes=[rs_b])
                        S.op("act", lambda e: e.activation(out=xv[:], in_=xi[:], func=AF.Copy, scale=rs[:]),
                             reads=[xi_b, rs_b], writes=[xv_b])
                        for c in range(KC):
                            S.op("pe", lambda e, c=c: e.transpose(out=pTv[:, c, :], in_=xv[:, c * 128:(c + 1) * 128], identity=identb[:]),
                                 reads=[xv_b, identb_b], writes=[pT_b])
                        for c in range(KC):
                            S.op("dve", lambda e, c=c: e.tensor_scalar(
                                out=hT[:, c, i * 128:(i + 1) * 128], in0=pTv[:, c, :],
                                scalar1=gsc[:, c, j:j + 1], scalar2=mod[:, c, j:j + 1], op0=ALU.mult, op1=ALU.add),
                                reads=[pT_b, gsc_b, mod_b], pwrites=[hT_b[tile]])

                    cxp = p1_front(0)
                    for i in range(NCH):
                        nxp = p1_front(i + 1) if i + 1 < NCH else None
                        p1_back(cxp)
                        cxp = nxp
                    S.barrier()
                if stop_after == "P1":
                    hT_dbg = scr("hT_dbg", [128, KC, T], BF16)
                    S.dma("pool", "dbg", out=hT_dbg[:, :, :], in_=hT[:], reads=hT_b, writes=[Buf()])
                    S.barrier()
                    return nc, S

                def proj_fm(wt, wt_b, c0, M, tile, pb, pb_b):
                    t0, tw = tile_rng(tile)
                    for kc in range(KC):
                        S.op("pe", lambda e, kc=kc: e.matmul(pb[0:M, 0:tw], lhsT=wt[:, kc, c0:c0 + M], rhs=hT[:, kc, t0:t0 + tw],
                                                             start=(kc == 0), stop=(kc == KC - 1)),
                             reads=[wt_b, hT_b[tile]], writes=[pb_b])

                def proj_tm(wt, wt_b, c0, N, i, pb, pb_b):
                    tile = 0 if i < 2 else 1 + (i - 2) // 4
                    for kc in range(KC):
                        S.op("pe", lambda e, kc=kc: e.matmul(pb[:, 0:N], lhsT=hT[:, kc, i * 128:(i + 1) * 128], rhs=wt[:, kc, c0:c0 + N],
                                                             start=(kc == 0), stop=(kc == KC - 1)),
                             reads=[wt_b, hT_b[tile]], writes=[pb_b])

                with ExitStack() as es:
                    sb, _ = mk(es)
                    UW = T + 4
                    ufull, _ = sb("ufull", [128, UW], F32)
                    gbfull, _ = sb("gbfull", [128, T], F32)
                    u_b = [Buf("u%d" % i) for i in range(9)]
                    gb_b = [Buf("gb%d" % i) for i in range(9)]
                    upad_b = Buf("upad")
                    cw, cw_b = sb("cw", [128, 8, 3], F32)
                    bg, bg_b = sb("bg", [128, 24], F32)
                    wdf, wdf_b = sb("wdf", [17, 512], F32)
                    wdb, wdb_b = sb("wdb", [17, 512], F32)
                    tmpA = Rot([sb("tmpA%d" % i, [128, 512], F32) for i in range(2)])
                    tmpB = Rot([sb("tmpB%d" % i, [128, 512], F32) for i in range(2)])
                    ytmp = Rot([sb("ytmp%d" % i, [128, 512], F32) for i in range(2)])
                    ytmp2 = Rot([sb("ytmpb%d" % i, [128, 512], F32) for i in range(2)])
                    stf = Rot([sb("stf%d" % i, [128, 512], F32) + ("stf%d" % i,) for i in range(3)])
                    stb = Rot([sb("stb%d" % i, [128, 512], BF16) + ("stb%d" % i,) for i in range(3)])
                    rT = [sb("rT%d" % i, [32, 512], F32) for i in range(4)]
                    banks = Rot(bank)
                    S.dma("sp", "c1", out=cw[:], in_=conv_wT[l], writes=[cw_b])
                    S.dma("sp", "c2", out=bg[:], in_=b_gateT[l], writes=[bg_b])
                    S.dma("sp", "c3", out=wdf[:], in_=wd_f[l], writes=[wdf_b])
                    S.dma("sp", "c4", out=wdb[:], in_=wd_b[l], writes=[wdb_b])
                    for a in (0, 257, 4355):
                        wdt = 2 if a == 257 else 1
                        S.op("pool", lambda e, a=a, wdt=wdt: e.memset(ufull[:, a:a + wdt], 0.0), pwrites=[upad_b])
                    for t_, tb in rT:
                        S.op("pool", lambda e, t_=t_: e.memset(t_[:], 1.0), writes=[tb])

                    def ucol(tile):
                        t0, tw = tile_rng(tile)
                        return (1 if tile == 0 else 259 + (t0 - 256)), tw

                    wq, wq_ready, wq_pos = [], {}, [0]

                    def mk_loader(src, W):
                        def ld():
                            wt_, wt_b_ = wbf_rot.next()
                            load_w(src, W, wst_rot, wt_, wt_b_, 0, True)
                            return wt_, wt_b_
                        return ld

                    def w_issue(j):
                        if j < len(wq) and j not in wq_ready:
                            wq_ready[j] = wq[j]()

                    def next_w():
                        j = wq_pos[0]
                        w_issue(j)
                        r = wq_ready.pop(j)
                        wq_pos[0] += 1
                        w_issue(j + 1)
                        return r

                    for off_, ncols_ in ((O_GQ, 512), (O_GK, 512), (O_ZG, 1024), (O_MG, 3072)):
                        for c0_ in range(0, ncols_, 512):
                            wq.append(mk_loader(w_in[l, :, off_ + c0_: off_ + c0_ + min(512, ncols_ - c0_)], min(512, ncols_ - c0_)))
                    wq.append(mk_loader(w_in[l, :, O_GK:O_GK + 512], 512))
                    for half_ in range(2):
                        wq.append(mk_loader(w_in[l, :, O_GV + half_ * 512:O_GV + (half_ + 1) * 512], 512))
                    wq.append(mk_loader(w_in[l, :, O_RF:O_RF + 32], 32))

                    def conv_weights(cb_):
                        wt_, wt_b_ = wbf_rot.next()
                        for n, off in enumerate((O_AB, O_AC, O_AX, O_AZ)):
                            load_w(w_in[l, :, off + cb_ * 128: off + (cb_ + 1) * 128], 128, wst_rot, wt_, wt_b_, n * 128, n == 0)
                        return wt_, wt_b_

                    nxt_w = conv_weights(0)
                    for cb in range(8):
                        wt, wt_b = nxt_w
                        for tile in range(9):
                            if tile == 3 and cb + 1 < 8:
                                nxt_w = conv_weights(cb + 1)
                            if tile == 3 and cb == 7:
                                w_issue(0)
                            t0, tw = tile_rng(tile)
                            uc, _ = ucol(tile)
                            pbs = [banks.next() for _ in range(4)]
                            for n in range(4):
                                proj_fm(wt, wt_b, n * 128, 128, tile, pbs[n][0], pbs[n][1])
                            (pB, pB_b), (pC, pC_b), (pX, pX_b), (pZ, pZ_b) = pbs
                            ta, ta_b = tmpA.next(); tb_, tb_b = tmpB.next()
                            S.op("act", lambda e, ta=ta, pZ=pZ, tw=tw: e.activation(out=ta[:, 0:tw], in_=pZ[:, 0:tw], func=AF.Silu),
                                 reads=[pZ_b], writes=[ta_b])
                            S.op("dve", lambda e, ta=ta, pB=pB, t0=t0, tw=tw: e.tensor_tensor(
                                out=gbfull[:, t0:t0 + tw], in0=pB[:, 0:tw], in1=ta[:, 0:tw], op=ALU.mult),
                                reads=[pB_b, ta_b], writes=[gb_b[tile]])
                            S.op("act", lambda e, tb_=tb_, pX=pX, tw=tw: e.activation(out=tb_[:, 0:tw], in_=pX[:, 0:tw], func=AF.Copy),
                                 reads=[pX_b], writes=[tb_b])
                            S.op("dve", lambda e, tb_=tb_, pC=pC, uc=uc, tw=tw: e.tensor_tensor(
                                out=ufull[:, uc:uc + tw], in0=pC[:, 0:tw], in1=tb_[:, 0:tw], op=ALU.mult),
                                reads=[pC_b, tb_b], writes=[u_b[tile]])
                        for tile in range(9):
                            t0, tw = tile_rng(tile)
                            uc, _ = ucol(tile)
                            nb = [u_b[k] for k in (tile - 1, tile, tile + 1) if 0 <= k < 9] + [upad_b]
                            yt, yt_b = ytmp.next()
                            sbf, sbf_b, sbf_s = stb.next()
                            S.op("dve", lambda e, yt=yt, uc=uc, tw=tw, cb=cb: e.tensor_scalar(
                                out=yt[:, 0:tw], in0=ufull[:, uc - 1:uc - 1 + tw], scalar1=cw[:, cb, 0:1], scalar2=None, op0=ALU.mult),
                                reads=nb + [cw_b], writes=[yt_b])
                            S.op("dve", lambda e, yt=yt, uc=uc, tw=tw, cb=cb: e.scalar_tensor_tensor(
                                out=yt[:, 0:tw], in0=ufull[:, uc:uc + tw], scalar=cw[:, cb, 1:2], in1=yt[:, 0:tw],
                                op0=ALU.mult, op1=ALU.add), reads=nb + [cw_b, yt_b], writes=[yt_b])
                            S.op("dve", lambda e, yt=yt, uc=uc, tw=tw, cb=cb: e.scalar_tensor_tensor(
                                out=yt[:, 0:tw], in0=ufull[:, uc + 1:uc + 1 + tw], scalar=cw[:, cb, 2:3], in1=yt[:, 0:tw],
                                op0=ALU.mult, op1=ALU.add), reads=nb + [cw_b, yt_b], writes=[yt_b])
                            S.op("pool", lambda e, yt=yt, sbf=sbf, t0=t0, tw=tw: e.tensor_tensor(
                                out=sbf[:, 0:tw], in0=yt[:, 0:tw], in1=gbfull[:, t0:t0 + tw], op=ALU.mult),
                                reads=[yt_b, gb_b[tile]], writes=[sbf_b])
                            S.dma("pool", sbf_s, out=yaT[cb * 128:(cb + 1) * 128, t0:t0 + tw], in_=sbf[:, 0:tw],
                                  reads=[sbf_b], pwrites=[B["yaT"]])

                    def fm_slot(off, ncols, dst, dst_b, kind):
                        for c0 in range(0, ncols, 512):
                            W = min(512, ncols - c0)
                            wt, wt_b = next_w()
                            for cbi in range(W // 128):
                                row0 = c0 + cbi * 128
                                for tile in range(9):
                                    t0, tw = tile_rng(tile)
                                    pb, pb_b = banks.next()
                                    proj_fm(wt, wt_b, cbi * 128, 128, tile, pb, pb_b)
                                    if kind in ("gq", "gk"):
                                        st, st_b, st_s = stf.next()
                                        sc = GLA_SCALE if kind == "gq" else 1.0
                                        S.op("act", lambda e, st=st, pb=pb, tw=tw, sc=sc: e.activation(
                                            out=st[:, 0:tw], in_=pb[:, 0:tw], func=AF.Copy, scale=sc), reads=[pb_b], writes=[st_b])
                                    elif kind == "zg":
                                        st, st_b, st_s = stb.next()
                                        S.op("act", lambda e, st=st, pb=pb, tw=tw: e.activation(
                                            out=st[:, 0:tw], in_=pb[:, 0:tw], func=AF.Silu), reads=[pb_b], writes=[st_b])
                                    else:
                                        st, st_b, st_s = stb.next()
                                        jcol = row0 // 128
                                        S.op("act", lambda e, st=st, pb=pb, tw=tw, jcol=jcol: e.activation(
                                            out=st[:, 0:tw], in_=pb[:, 0:tw], func=AF.Sigmoid, bias=bg[:, jcol:jcol + 1]),
                                            reads=[pb_b, bg_b], writes=[st_b])
                                    S.dma("pool", st_s, out=dst[row0:row0 + 128, t0:t0 + tw], in_=st[:, 0:tw],
                                          reads=[st_b], pwrites=[dst_b])

                    fm_slot(O_GQ, 512, gqT, B["gqT"], "gq")
                    fm_slot(O_GK, 512, gkT, B["gkT"], "gk")
                    fm_slot(O_ZG, 1024, szT, B["szT"], "zg")
                    fm_slot(O_MG, 3072, mgT, B["mgT"], "mg")

                    wt, wt_b = next_w()
                    for i in range(NCH):
                        pb, pb_b = banks.next()
                        proj_tm(wt, wt_b, 0, 512, i, pb, pb_b)
                        st, st_b, st_s = stf.next()
                        S.op("act", lambda e, st=st, pb=pb: e.activation(out=st[:], in_=pb[:], func=AF.Copy), reads=[pb_b], writes=[st_b])
                        S.dma("pool", st_s, out=gktm[i * 128:(i + 1) * 128, :], in_=st[:], reads=[st_b], pwrites=[B["gktm"]])
                    for half in range(2):
                        wt, wt_b = next_w()
                        for i in range(NCH):
                            pb, pb_b = banks.next()
                            proj_tm(wt, wt_b, 0, 512, i, pb, pb_b)
                            st, st_b, st_s = stb.next()
                            S.op("act", lambda e, st=st, pb=pb: e.activation(out=st[:], in_=pb[:], func=AF.Copy), reads=[pb_b], writes=[st_b])
                            S.dma("pool", st_s, out=gv[i * 128:(i + 1) * 128, half * 512:(half + 1) * 512], in_=st[:],
                                  reads=[st_b], pwrites=[B["gv"]])

                    wt, wt_b = next_w()
                    assert wq_pos[0] == len(wq) and not wq_ready
                    for tile in range(9):
                        t0, tw = tile_rng(tile)
                        for dr in range(2):
                            r_, r_b = rT[dr * 2 + tile % 2]
                            pb, pb_b = banks.next()
                            proj_fm(wt, wt_b, dr * 16, 16, tile, pb, pb_b)
                            S.op("act", lambda e, r_=r_, pb=pb, tw=tw: e.activation(out=r_[0:16, 0:tw], in_=pb[0:16, 0:tw], func=AF.Copy),
                                 reads=[pb_b], writes=[r_b])
                            wd, wd_b_ = (wdf, wdf_b) if dr == 0 else (wdb, wdb_b)
                            dst, dst_b = (laf, B["laf"]) if dr == 0 else (lab, B["lab"])
                            for sub in range(tw // 128):
                                pz, pz_b = banks.next()
                                S.op("pe", lambda e, pz=pz, r_=r_, sub=sub, wd=wd: e.matmul(
                                    pz[:, :], lhsT=r_[0:17, sub * 128:(sub + 1) * 128], rhs=wd[0:17, :], start=True, stop=True),
                                    reads=[r_b, wd_b_], writes=[pz_b])
                                st, st_b, st_s = stf.next()
                                S.op("act", lambda e, st=st, pz=pz: e.activation(out=st[:], in_=pz[:], func=AF.Exp, scale=-1.0),
                                     reads=[pz_b], writes=[st_b])
                                S.op("act", lambda e, st=st: e.activation(out=st[:], in_=st[:], func=AF.Ln, bias=1.0),
                                     reads=[st_b], writes=[st_b])
                                S.op("dve", lambda e, st=st: e.tensor_scalar(out=st[:], in0=st[:], scalar1=-1.0 / GLA_TAU, scalar2=None, op0=ALU.mult),
                                     reads=[st_b], writes=[st_b])
                                r0 = t0 + sub * 128
                                S.dma("pool", st_s, out=dst[r0:r0 + 128, :], in_=st[:], reads=[st_b], pwrites=[dst_b])
                    S.barrier()
                if stop_after == "P2":
                    return nc, S

                with ExitStack() as es:
                    sb, _ = mk(es)
                    kT, _ = sb("kT", [128, 2, T], BF16)
                    kT_b = [Buf("kT0"), Buf("kT1")]
                    Vs, V_b = sb("Vs", [128, NCH, 256], BF16)
                    cosS, cos_b = sb("cosS", [128, 4096], BF16)
                    sinS, sin_b = sb("sinS", [128, 4096], BF16)
                    qgS, qgS_b = sb("qgS", [128, 2], F32)
                    kgS, kgS_b = sb("kgS", [128, 2], F32)
                    qaR = Rot([sb("qa%d" % i, [128, 512], F32) for i in range(2)])
                    qbR = Rot([sb("qb%d" % i, [128, 512], F32) for i in range(2)])
                    qsqR = Rot([sb("qsq%d" % i, [128, 512], BF16) for i in range(2)])
                    rstR = Rot([sb("rst%d" % i, [128, 512], F32) for i in range(2)])
                    qrR = Rot([sb("qr%d" % i, [128, 512], BF16) for i in range(2)])
                    szR = Rot([sb("szr%d" % i, [128, 512], F32) for i in range(2)])
                    pTR = Rot([sb("pT%d" % i, [128, 512], BF16) for i in range(3)])
                    rDR = Rot([sb("rD%d" % i, [128, 512], F32) for i in range(2)])
                    ybR = Rot([sb("ybs%d" % i, [128, 512], BF16) + ("ybs%d" % i,) for i in range(2)])
                    (pA, pA_b), (pBk, pBk_b), (pSS, pSS_b), (pZ, pZ_b) = bank[0], bank[1], bank[2], bank[3]
                    SbR = Rot([bank[4], bank[5]])
                    (pO, pO_b), (pD, pD_b) = bank[6], bank[7]
                    S.dma("sp", "c1", out=cosS[:], in_=cos_d[:, :], writes=[cos_b])
                    S.dma("sp", "c2", out=sinS[:], in_=sin_d[:, :], writes=[sin_b])
                    S.dma("sp", "c3", out=qgS[:], in_=qg_d[l], writes=[qgS_b])
                    S.dma("sp", "c4", out=kgS[:], in_=kg_d[l], writes=[kgS_b])

                    def normrope_a(gS, gS_b, tile, rope):
                        t0, tw = tile_rng(tile)
                        qa, qa_b = qaR.next(); qb, qb_b = qbR.next(); qsq, qsq_b = qsqR.next(); rst, rst_b = rstR.next()
                        S.op("act", lambda e: e.activation(out=qa[:, 0:tw], in_=pA[:, 0:tw], func=AF.Copy, scale=gS[:, 0:1]),
                             reads=[pA_b, gS_b], writes=[qa_b])
                        S.op("act", lambda e: e.activation(out=qsq[:, 0:tw], in_=pA[:, 0:tw], func=AF.Square),
                             reads=[pA_b], writes=[qsq_b])
                        if rope:
                            S.op("act", lambda e: e.activation(out=qb[:, 0:tw], in_=pBk[:, 0:tw], func=AF.Copy, scale=gS[:, 1:2]),
                                 reads=[pBk_b, gS_b], writes=[qb_b])
                        S.op("pe", lambda e: e.matmul(pSS[:, 0:tw], lhsT=onesb[:], rhs=qsq[:, 0:tw], start=True, stop=True),
                             reads=[onesb_b, qsq_b], writes=[pSS_b])
                        S.op("dve", lambda e: e.tensor_scalar(out=rst[:, 0:tw], in0=pSS[:, 0:tw], scalar1=1.0 / 128, scalar2=EPS,
                                                              op0=ALU.mult, op1=ALU.add), reads=[pSS_b], writes=[rst_b])
                        return dict(t0=t0, tw=tw, rope=rope, qa=qa, qa_b=qa_b, qb=qb, qb_b=qb_b, rst=rst, rst_b=rst_b)

                    def normrope_b(cx, out_ap, out_w, out_pw):
                        t0, tw, rope = cx["t0"], cx["tw"], cx["rope"]
                        qa, qa_b, qb, qb_b, rst, rst_b = cx["qa"], cx["qa_b"], cx["qb"], cx["qb_b"], cx["rst"], cx["rst_b"]
                        S.op("act", lambda e: e.activation(out=rst[:, 0:tw], in_=rst[:, 0:tw], func=AF.Ln), reads=[rst_b], writes=[rst_b])
                        S.op("act", lambda e: e.activation(out=rst[:, 0:tw], in_=rst[:, 0:tw], func=AF.Exp, scale=-0.5), reads=[rst_b], writes=[rst_b])
                        if rope:
                            lt0 = t0 - 256
                            S.op("dve", lambda e: e.tensor_tensor(out=qa[:, 0:tw], in0=qa[:, 0:tw], in1=cosS[:, lt0:lt0 + tw], op=ALU.mult),
                                 reads=[qa_b, cos_b], writes=[qa_b])
                            S.op("pool", lambda e: e.tensor_tensor(out=qb[:, 0:tw], in0=qb[:, 0:tw], in1=sinS[:, lt0:lt0 + tw], op=ALU.mult),
                                 reads=[qb_b, sin_b], writes=[qb_b])
                            S.op("dve", lambda e: e.tensor_tensor(out=qa[:, 0:tw], in0=qa[:, 0:tw], in1=qb[:, 0:tw], op=ALU.add),
                                 reads=[qa_b, qb_b], writes=[qa_b])
                        S.op("dve", lambda e: e.tensor_tensor(out=out_ap, in0=qa[:, 0:tw], in1=rst[:, 0:tw], op=ALU.mult),
                             reads=[qa_b, rst_b], writes=out_w, pwrites=out_pw)

                    def normrope(gS, gS_b, tile, rope, out_ap, out_w, out_pw):
                        normrope_b(normrope_a(gS, gS_b, tile, rope), out_ap, out_w, out_pw)

                    def PRE_HEAD0():
                        wt_, wt_b_ = wbf_rot.next()
                        load_w(w_in[l, :, O_Q:O_Q + 128], 128, wst_rot, wt_, wt_b_, 0, True)
                        load_w(w_qp[l, :, 0:128], 128, wst_rot, wt_, wt_b_, 128, False)
                        load_w(w_in[l, :, O_ZA:O_ZA + 128], 128, wst_rot, wt_, wt_b_, 256, False)
                        return wt_, wt_b_

                    wt, wt_b = wbf_rot.next()
                    load_w(w_in[l, :, O_K:O_K + 256], 256, wst_rot, wt, wt_b, 0, True)
                    load_w(w_kp[l, :, :], 256, wst_rot, wt, wt_b, 256, False)
                    wtV, wtV_b = wbf_rot.next()
                    load_w(w_in[l, :, O_V:O_V + 256], 256, wst_rot, wtV, wtV_b, 0, True)
                    for kv in range(2):
                        for tile in range(9):
                            t0, tw = tile_rng(tile)
                            proj_fm(wt, wt_b, kv * 128, 128, tile, pA, pA_b)
                            if tile > 0:
                                proj_fm(wt, wt_b, 256 + kv * 128, 128, tile, pBk, pBk_b)
                            normrope(kgS, kgS_b, tile, tile > 0, kT[:, kv, t0:t0 + tw], [], [kT_b[kv]])
                    wt, wt_b = wtV, wtV_b
                    HW0 = PRE_HEAD0()
                    for i in range(NCH):
                        pb, pb_b = SbR.next()
                        proj_tm(wt, wt_b, 0, 256, i, pb, pb_b)
                        S.op("act", lambda e, pb=pb, i=i: e.activation(out=Vs[:, i, :], in_=pb[:, 0:256], func=AF.Copy),
                             reads=[pb_b], pwrites=[V_b])

                    q_tiles = list(range(1, 9)) if last else list(range(0, 9))
                    def load_head_w(h):
                        wt, wt_b = wbf_rot.next()
                        load_w(w_in[l, :, O_Q + h * 128:O_Q + (h + 1) * 128], 128, wst_rot, wt, wt_b, 0, True)
                        load_w(w_qp[l, :, h * 128:(h + 1) * 128], 128, wst_rot, wt, wt_b, 128, False)
                        load_w(w_in[l, :, O_ZA + h * 128:O_ZA + (h + 1) * 128], 128, wst_rot, wt, wt_b, 256, False)
                        return wt, wt_b

                    def make_head(h, wt, wt_b):
                        kv = h // 4

                        def prologue_a(tile):
                            t0, tw = tile_rng(tile)
                            qr, qr_b = qrR.next(); sz, sz_b = szR.next()
                            proj_fm(wt, wt_b, 0, 128, tile, pA, pA_b)
                            if tile > 0:
                                proj_fm(wt, wt_b, 128, 128, tile, pBk, pBk_b)
                            proj_fm(wt, wt_b, 256, 128, tile, pZ, pZ_b)
                            cx = normrope_a(qgS, qgS_b, tile, tile > 0)
                            return dict(tile=tile, tw=tw, qr=qr, qr_b=qr_b, sz=sz, sz_b=sz_b, cx=cx)

                        def prologue_b(P):
                            tw, qr, qr_b, sz, sz_b = P["tw"], P["qr"], P["qr_b"], P["sz"], P["sz_b"]
                            normrope_b(P["cx"], qr[:, 0:tw], [qr_b], [])
                            S.op("act", lambda e: e.activation(out=sz[:, 0:tw], in_=pZ[:, 0:tw], func=AF.Exp, scale=-1.0),
                                 reads=[pZ_b], writes=[sz_b])
                            S.op("dve", lambda e: e.tensor_scalar(out=sz[:, 0:tw], in0=sz[:, 0:tw], scalar1=1.0, scalar2=None, op0=ALU.add),
                                 reads=[sz_b], writes=[sz_b])
                            S.op("dve", lambda e: e.reciprocal(out=sz[:, 0:tw], in_=sz[:, 0:tw]), reads=[sz_b], writes=[sz_b])
                            S.op("dve", lambda e: e.tensor_tensor(out=sz[:, 0:tw], in0=pZ[:, 0:tw], in1=sz[:, 0:tw], op=ALU.mult),
                                 reads=[pZ_b, sz_b], writes=[sz_b])

                        def inner(tile, qr, qr_b, sz, sz_b, hook=None):
                            t0, tw = tile_rng(tile)
                            kcs = [0, 1] if tile == 0 else list(range(NCH))
                            LA = 2
                            nk = len(kcs)
                            pslots = {}

                            def emit_s(idx):
                                kc = kcs[idx]
                                sbk, sbk_b = SbR.next()
                                p_, p_b = pTR.next()
                                pslots[idx] = (p_, p_b)
                                S.op("pe", lambda e, sbk=sbk, kc=kc: e.matmul(
                                    sbk[:, 0:tw], lhsT=kT[:, kv, kc * 128:(kc + 1) * 128], rhs=qr[:, 0:tw], start=True, stop=True),
                                    reads=[kT_b[kv], qr_b], writes=[sbk_b])
                                S.op("act", lambda e, sbk=sbk, p_=p_: e.activation(out=p_[:, 0:tw], in_=sbk[:, 0:tw], func=AF.Exp, scale=ATTN_SCALE),
                                     reads=[sbk_b], writes=[p_b])

                            for idx in range(min(LA, nk)):
                                emit_s(idx)
                            for idx in range(nk):
                                kc = kcs[idx]
                                p_, p_b = pslots.pop(idx)
                                first, lastk = (idx == 0), (idx == nk - 1)
                                S.op("pe", lambda e, p_=p_, kc=kc, first=first, lastk=lastk: e.matmul(
                                    pO[:, 0:tw], lhsT=Vs[:, kc, kv * 128:(kv + 1) * 128], rhs=p_[:, 0:tw], start=first, stop=lastk),
                                    reads=[V_b, p_b], writes=[pO_b])
                                S.op("pe", lambda e, p_=p_, first=first, lastk=lastk: e.matmul(
                                    pD[:, 0:tw], lhsT=onesb[:], rhs=p_[:, 0:tw], start=first, stop=lastk),
                                    reads=[onesb_b, p_b], writes=[pD_b])
                                if idx + LA < nk:
                                    emit_s(idx + LA)
                                if hook is not None and idx == min(6, nk - 1):
                                    hook()
                            rD, rD_b = rDR.next(); ybs, ybs_b, ybs_s = ybR.next()
                            S.op("dve", lambda e: e.reciprocal(out=rD[:, 0:tw], in_=pD[:, 0:tw]), reads=[pD_b], writes=[rD_b])
                            S.op("dve", lambda e: e.tensor_tensor(out=rD[:, 0:tw], in0=pO[:, 0:tw], in1=rD[:, 0:tw], op=ALU.mult),
                                 reads=[pO_b, rD_b], writes=[rD_b])
                            S.op("dve", lambda e: e.tensor_tensor(out=ybs[:, 0:tw], in0=rD[:, 0:tw], in1=sz[:, 0:tw], op=ALU.mult),
                                 reads=[rD_b, sz_b], writes=[ybs_b])
                            S.dma("pool", ybs_s, out=ybT[h * 128:(h + 1) * 128, t0:t0 + tw], in_=ybs[:, 0:tw],
                                  reads=[ybs_b], pwrites=[B["ybT"]])

                        return prologue_a, prologue_b, inner

                    HW = {0: HW0}
                    HF = {}

                    def head_fns(h):
                        if h not in HF:
                            HF[h] = make_head(h, *HW[h])
                        return HF[h]

                    items = [(h, tile) for h in range(8) for tile in q_tiles]
                    cur = head_fns(0)[0](items[0][1])
                    head_fns(0)[1](cur)
                    for n, (h, tile) in enumerate(items):
                        if tile == q_tiles[-2] and h + 1 < 8:
                            HW[h + 1] = load_head_w(h + 1)
                        if n + 1 < len(items):
                            nh, nt = items[n + 1]
                            nxt = head_fns(nh)[0](nt)
                            hook = (lambda nh=nh, nxt=nxt: head_fns(nh)[1](nxt))
                        else:
                            nxt, hook = None, None
                        head_fns(h)[2](tile, cur["qr"], cur["qr_b"], cur["sz"], cur["sz_b"], hook=hook)
                        cur = nxt
                    S.barrier()
            if stop_after == "P3":
                return nc, S

            with ExitStack() as es:
                sb, _ = mk(es)
                tri, tri_b = sb("tri", [128, 4, 128], F32)
                maskF, maskF_b = sb("maskF", [128, 4, 128], F32)
                maskB, maskB_b = sb("maskB", [128, 4, 128], F32)
                Sst, Sst_b = sb("Sst", [128, 4, 256], F32)
                SbfR = Rot([sb("Sbf%d" % i, [128, 4, 256], BF16) for i in range(2)])
                gbc, gbc_b = sb("gbc", [128, 256], F32)
                qTl = Rot([sb("qTl%d" % i, [128, 4, 128], F32) + ("qTl%d" % i,) for i in range(2)])
                kTl = Rot([sb("kTl%d" % i, [128, 4, 128], F32) + ("kTl%d" % i,) for i in range(2)])
                ktml = Rot([sb("ktml%d" % i, [128, 512], F32) + ("ktml%d" % i,) for i in range(2)])
                vl = Rot([sb("vl%d" % i, [128, 1024], BF16) + ("vl%d" % i,) for i in range(3)])
                lal = Rot([sb("lal%d" % i, [128, 512], F32) + ("lal%d" % i,) for i in range(2)])
                ofl = Rot([sb("ofl%d" % i, [128, 1024], F32) + ("ofl%d" % i,) for i in range(3)])
                szl = Rot([sb("szl%d" % i, [128, 8, 128], BF16) + ("szl%d" % i,) for i in range(3)])
                EqR = Rot([sb("Eq%d" % i, [128, 4, 128], F32) for i in range(3)])
                EkR = Rot([sb("Ek%d" % i, [128, 4, 128], F32) for i in range(2)])
                EkkR = Rot([sb("Ekk%d" % i, [128, 512], F32) for i in range(2)])
                qtR = Rot([sb("qt%d" % i, [128, 4, 128], BF16) for i in range(3)])
                ktR = Rot([sb("kt%d" % i, [128, 4, 128], BF16) for i in range(2)])
                khR = Rot([sb("kh%d" % i, [128, 512], BF16) for i in range(3)])
                amR = Rot([sb("am%d" % i, [128, 4, 128], BF16) for i in range(2)])
                ofsR = Rot([sb("ofs%d" % i, [128, 1024], F32) + ("ofs%d" % i,) for i in range(2)])
                onR = Rot([sb("on%d" % i, [128, 1024], BF16) for i in range(2)])
                ss4R = Rot([sb("ss4%d" % i, [128, 4], F32) for i in range(2)])
                ycR = Rot([sb("ycs%d" % i, [128, 8, 128], BF16) + ("ycs%d" % i,) for i in range(2)])
                (pB1, pB1_b), (pB2, pB2_b) = bank[0], bank[1]
                ATs = [bank[2], bank[5]]
                pOb = [bank[3], bank[4]]
                pSU = [bank[6], bank[7]]
                S.dma("sp", "c1", out=tri[:], in_=tri_d.rearrange("k s t -> s k t"), writes=[tri_b])
                for hh in range(4):
                    S.dma("sp", "c2", out=maskF[:, hh, :], in_=tri_d[0], pwrites=[maskF_b])
                    S.dma("sp", "c3", out=maskB[:, hh, :], in_=tri_d[2], pwrites=[maskB_b])
                S.dma("sp", "c4", out=gbc[:], in_=gla_g[l].partition_broadcast(128), writes=[gbc_b])

                steps = [(0, c) for c in range(NCH)] + [(1, c) for c in [1, 0] + list(range(NCH - 1, 1, -1))]
                cur_S = [None]

                def stage_a(g, dr, c):
                    r0 = c * 128
                    pB1v = pB1[:].rearrange("p (h t) -> p h t", t=128)
                    cumM, restM = (tri[:, 0, :], tri[:, 1, :]) if dr == 0 else (tri[:, 2, :], tri[:, 3, :])
                    la_d, la_db = (laf, B["laf"]) if dr == 0 else (lab, B["lab"])
                    q_, q_b, q_s = qTl.next(); k_, k_b, k_s = kTl.next(); km, km_b, km_s = ktml.next()
                    v_, v_b, v_s = vl.next(); la, la_b, la_s = lal.next()
                    S.dma("sp", q_s, out=q_[:], in_=gqT[:, r0:r0 + 128].rearrange("(h d) t -> d h t", d=128), reads=[B["gqT"]], writes=[q_b])
                    S.dma("sp", k_s, out=k_[:], in_=gkT[:, r0:r0 + 128].rearrange("(h d) t -> d h t", d=128), reads=[B["gkT"]], writes=[k_b])
                    S.dma("sp", km_s, out=km[:], in_=gktm[r0:r0 + 128, :], reads=[B["gktm"]], writes=[km_b])
                    S.dma("sp", v_s, out=v_[:], in_=gv[r0:r0 + 128, :], reads=[B["gv"]], writes=[v_b])
                    S.dma("sp", la_s, out=la[:], in_=la_d[r0:r0 + 128, :], reads=[la_db], writes=[la_b])
                    Eq, Eq_b = EqR.next(); Ek, Ek_b = EkR.next(); Ekk, Ekk_b = EkkR.next()
                    qt, qt_b = qtR.next(); kt, kt_b = ktR.next(); kh, kh_b = khR.next()
                    for hh in range(4):
                        S.op("pe", lambda e, hh=hh: e.matmul(pB1v[:, hh, :], lhsT=la[:, hh * 128:(hh + 1) * 128], rhs=cumM, start=True, stop=True),
                             reads=[la_b, tri_b], writes=[pB1_b])
                    S.op("pe", lambda e: e.matmul(pB2[:, :], lhsT=restM, rhs=la[:, :], start=True, stop=True),
                         reads=[la_b, tri_b], writes=[pB2_b])
                    S.op("act", lambda e: e.activation(out=Eq[:], in_=pB1v, func=AF.Exp), reads=[pB1_b], writes=[Eq_b])
                    S.op("act", lambda e: e.activation(out=Ek[:], in_=pB1v, func=AF.Exp, scale=-1.0), reads=[pB1_b], writes=[Ek_b])
                    S.op("act", lambda e: e.activation(out=Ekk[:], in_=pB2[:, :], func=AF.Exp), reads=[pB2_b], writes=[Ekk_b])
                    S.op("dve", lambda e: e.tensor_tensor(out=qt[:], in0=q_[:], in1=Eq[:], op=ALU.mult), reads=[q_b, Eq_b], writes=[qt_b])
                    S.op("pool", lambda e: e.tensor_tensor(out=kt[:], in0=k_[:], in1=Ek[:], op=ALU.mult), reads=[k_b, Ek_b], writes=[kt_b])
                    S.op("pool", lambda e: e.tensor_tensor(out=kh[:], in0=km[:], in1=Ekk[:], op=ALU.mult), reads=[km_b, Ekk_b], writes=[kh_b])
                    cx = dict(g=g, dr=dr, c=c, r0=r0, v_=v_, v_b=v_b, Eq=Eq, Eq_b=Eq_b, qt=qt, qt_b=qt_b, kt=kt, kt_b=kt_b, kh=kh, kh_b=kh_b)
                    if dr == 1:
                        of_, of_b, of_s = ofl.next(); sz_, sz_b, sz_s = szl.next()
                        S.dma("sp", of_s, out=of_[:], in_=of_d[r0:r0 + 128, :], reads=[B["of"]], writes=[of_b])
                        S.dma("sp", sz_s, out=sz_[:], in_=szT[:, r0:r0 + 128].rearrange("(c p) t -> p c t", p=128),
                              reads=[B["szT"]], writes=[sz_b])
                        cx.update(of_=of_, of_b=of_b, sz_=sz_, sz_b=sz_b)
                    return cx

                def stage_b(cx):
                    g, dr = cx["g"], cx["dr"]
                    qt, qt_b, kt, kt_b = cx["qt"], cx["qt_b"], cx["kt"], cx["kt_b"]
                    pAT, pAT_b = ATs[g % 2]
                    pATv = pAT[:].rearrange("p (h t) -> p h t", t=128)
                    mask, mask_b = (maskF, maskF_b) if dr == 0 else (maskB, maskB_b)
                    am, am_b = amR.next()
                    for hh in range(4):
                        S.op("pe", lambda e, hh=hh: e.matmul(pATv[:, hh, :], lhsT=kt[:, hh, :], rhs=qt[:, hh, :], start=True, stop=True),
                             reads=[kt_b, qt_b], writes=[pAT_b])
                    S.op("dve", lambda e: e.tensor_tensor(out=am[:], in0=pATv, in1=mask[:], op=ALU.mult), reads=[pAT_b, mask_b], writes=[am_b])
                    cx.update(am=am, am_b=am_b)
                    return cx

                def back(cx):
                    g, dr, c, r0 = cx["g"], cx["dr"], cx["c"], cx["r0"]
                    v_, v_b, Eq, Eq_b, qt, qt_b = cx["v_"], cx["v_b"], cx["Eq"], cx["Eq_b"], cx["qt"], cx["qt_b"]
                    kh, kh_b, am, am_b = cx["kh"], cx["kh_b"], cx["am"], cx["am_b"]
                    pAT, pAT_b = ATs[g % 2]
                    pTB_b = pAT_b
                    pTBv = pAT[:].bitcast(BF16).rearrange("p (c t) -> p c t", t=128)
                    tl = 127 if dr == 0 else 0
                    if (dr, c) in ((0, 0), (1, 1)):
                        S.op("pool", lambda e: e.memset(Sst[:], 0.0), writes=[Sst_b])
                        Sbf0, Sbf0_b = SbfR.next()
                        S.op("pool", lambda e: e.memset(Sbf0[:], 0.0), writes=[Sbf0_b])
                        cur_S[0] = (Sbf0, Sbf0_b)
                    Sbf, Sbf_b = cur_S[0]
                    for hh in range(4):
                        su, su_b = pSU[hh // 2]
                        oc = (hh % 2) * 256
                        S.op("pe", lambda e, hh=hh, su=su, oc=oc: e.matmul(su[:, oc:oc + 256], lhsT=kh[:, hh * 128:(hh + 1) * 128], rhs=v_[:, hh * 256:(hh + 1) * 256],
                                                                        start=True, stop=True), reads=[kh_b, v_b], writes=[su_b])
                    for hh in range(4):
                        ob, ob_b = pOb[hh // 2]
                        oc = (hh % 2) * 256
                        S.op("pe", lambda e, hh=hh, ob=ob, oc=oc: e.matmul(ob[:, oc:oc + 256], lhsT=am[:, hh, :], rhs=v_[:, hh * 256:(hh + 1) * 256],
                                                                        start=True, stop=False), reads=[am_b, v_b], writes=[ob_b])
                        S.op("pe", lambda e, hh=hh, ob=ob, oc=oc: e.matmul(ob[:, oc:oc + 256], lhsT=qt[:, hh, :], rhs=Sbf[:, hh, :],
                                                                        start=False, stop=True), reads=[qt_b, Sbf_b], writes=[ob_b])
                    for hh in range(4):
                        su, su_b = pSU[hh // 2]
                        oc = (hh % 2) * 256
                        S.op("dve", lambda e, hh=hh, su=su, oc=oc: e.scalar_tensor_tensor(
                            out=Sst[:, hh, :], in0=Sst[:, hh, :], scalar=Eq[:, hh, tl:tl + 1], in1=su[:, oc:oc + 256],
                            op0=ALU.mult, op1=ALU.add), reads=[Sst_b, Eq_b, su_b], writes=[Sst_b])
                    Sbfn, Sbfn_b = SbfR.next()
                    S.op("act", lambda e: e.activation(out=Sbfn[:], in_=Sst[:], func=AF.Copy), reads=[Sst_b], writes=[Sbfn_b])
                    cur_S[0] = (Sbfn, Sbfn_b)
                    ofs, ofs_b, ofs_s = ofsR.next()
                    if dr == 0:
                        for hf in range(2):
                            ob, ob_b = pOb[hf]
                            S.op("act", lambda e, ob=ob, hf=hf: e.activation(out=ofs[:, hf * 512:(hf + 1) * 512], in_=ob[:, :], func=AF.Copy),
                                 reads=[ob_b], pwrites=[ofs_b])
                        S.dma("pool", ofs_s, out=of_d[r0:r0 + 128, :], in_=ofs[:], reads=[ofs_b], pwrites=[B["of"]])
                    elif not (last and c < 2):
                        of_, of_b, sz_, sz_b = cx["of_"], cx["of_b"], cx["sz_"], cx["sz_b"]
                        on, on_b = onR.next(); ss4, ss4_b = ss4R.next(); ycs, ycs_b, ycs_s = ycR.next()
                        for hf in range(2):
                            ob, ob_b = pOb[hf]
                            S.op("dve", lambda e, ob=ob, hf=hf: e.tensor_tensor(out=ofs[:, hf * 512:(hf + 1) * 512], in0=ob[:, :],
                                                                             in1=of_[:, hf * 512:(hf + 1) * 512], op=ALU.add),
                                 reads=[ob_b, of_b], pwrites=[ofs_b])
                        for hh in range(4):
                            S.op("act", lambda e, hh=hh: e.activation(out=on[:, hh * 256:(hh + 1) * 256], in_=ofs[:, hh * 256:(hh + 1) * 256],
                                                                      func=AF.Square, accum_out=ss4[:, hh:hh + 1]),
                                 reads=[ofs_b], pwrites=[on_b, ss4_b])
                        S.op("dve", lambda e: e.tensor_scalar(out=ss4[:], in0=ss4[:], scalar1=1.0 / 256, scalar2=EPS, op0=ALU.mult, op1=ALU.add),
                             reads=[ss4_b], writes=[ss4_b])
                        S.op("act", lambda e: e.activation(out=ss4[:], in_=ss4[:], func=AF.Sqrt), reads=[ss4_b], writes=[ss4_b])
                        S.op("dve", lambda e: e.reciprocal(out=ss4[:], in_=ss4[:]), reads=[ss4_b], writes=[ss4_b])
                        for hh in range(4):
                            S.op("dve", lambda e, hh=hh: e.scalar_tensor_tensor(
                                out=on[:, hh * 256:(hh + 1) * 256], in0=ofs[:, hh * 256:(hh + 1) * 256], scalar=ss4[:, hh:hh + 1],
                                in1=gbc[:], op0=ALU.mult, op1=ALU.mult), reads=[ofs_b, ss4_b, gbc_b, on_b], writes=[on_b])
                        for cc in range(8):
                            S.op("pe", lambda e, cc=cc: e.transpose(out=pTBv[:, cc, :], in_=on[:, cc * 128:(cc + 1) * 128], identity=identb[:]),
                                 reads=[on_b, identb_b], writes=[pTB_b])
                        S.op("dve", lambda e: e.tensor_tensor(out=ycs[:], in0=pTBv, in1=sz_[:], op=ALU.mult),
                             reads=[pTB_b, sz_b], writes=[ycs_b])
                        S.dma("pool", ycs_s, out=ycT[:, r0:r0 + 128].rearrange("(c p) t -> p c t", p=128), in_=ycs[:],
                              reads=[ycs_b], pwrites=[B["ycT"]])

                NS = len(steps)
                cxa = {0: stage_a(0, *steps[0])}
                if NS > 1:
                    cxa[1] = stage_a(1, *steps[1])
                cxb = {0: stage_b(cxa.pop(0))}
                for g in range(NS):
                    if g + 2 < NS:
                        cxa[g + 2] = stage_a(g + 2, *steps[g + 2])
                    if g + 1 < NS:
                        cxb[g + 1] = stage_b(cxa.pop(g + 1))
                    back(cxb.pop(g))
                S.barrier()
            if stop_after == "P4":
                return nc, S

            with ExitStack() as es:
                sb, _ = mk(es)
                wst = [sb("wstO%d" % i, [128, KC, 256], F32) for i in range(3)]
                wst_rot = Rot([(wst[i][0], wst[i][1], "wstO%d" % i) for i in range(3)])
                wA, wA_b = sb("wA", [128, KC, 1024], BF16)
                wB, wB_b = sb("wB", [128, KC, 1024], BF16)
                wC, wC_b = sb("wC", [128, KC, 1024], BF16)
                wO, wO_b = sb("wO", [128, KC, 1024], BF16)
                gtg, gtg_b = sb("gtgS", [128, 2, 1024], F32)
                yaR = Rot([sb("yal%d" % i, [128, 8, 512], BF16) + ("yal%d" % i,) for i in range(2)])
                ybR2 = Rot([sb("ybl%d" % i, [128, 8, 512], BF16) + ("ybl%d" % i,) for i in range(2)])
                ycR2 = Rot([sb("ycl%d" % i, [128, 8, 512], BF16) + ("ycl%d" % i,) for i in range(2)])
                mgR = Rot([sb("mgl%d" % i, [128, 3, 512], BF16) + ("mgl%d" % i,) for i in range(2)])
                mrgR = Rot([sb("mrg%d" % i, [128, 8, 512], BF16) for i in range(2)])
                m1R = Rot([sb("m1%d" % i, [128, 512], F32) for i in range(2)])
                m2R = Rot([sb("m2%d" % i, [128, 512], F32) for i in range(2)])
                m3R = Rot([sb("m3%d" % i, [128, 512], F32) for i in range(2)])
                xlR = Rot([sb("xl%d" % i, [128, 1024], F32) + ("xl%d" % i,) for i in range(2)])
                ostR = Rot([sb("ost%d" % i, [128, 1024], F32) + ("ost%d" % i,) for i in range(2)])
                ttR = Rot([sb("tt%d" % i, [128, 1024], F32) for i in range(2)])
                ss2R = Rot([sb("ss2%d" % i, [128, 2], F32) for i in range(2)])
                rsR = Rot([sb("rso%d" % i, [128, 1], F32) for i in range(2)])
                brR = Rot(bank[0:6])
                (pO0, pO0_b), (pO1, pO1_b) = bank[6], bank[7]
                load_w(w_bra[l], 1024, wst_rot, wA, wA_b, 0, True, cast_engs=("act", "dve"), dma_engs=("sp", "act"))
                load_w(w_brb[l], 1024, wst_rot, wB, wB_b, 0, True, cast_engs=("act", "dve"), dma_engs=("sp", "act"))
                load_w(w_brc[l], 1024, wst_rot, wC, wC_b, 0, True, cast_engs=("act", "dve"), dma_engs=("sp", "act"))
                load_w(w_out[l], 1024, wst_rot, wO, wO_b, 0, True, cast_engs=("act", "dve"), dma_engs=("sp", "act"))
                for j in range(2):
                    S.dma("sp", "c1", out=gtg[:, j, :], in_=gtg_d[l, j].partition_broadcast(128), reads=[B["gtg"]], pwrites=[gtg_b])
                mgTv = mgT.rearrange("(g j p) t -> j p g t", g=3, j=8, p=128)
                tiles = list(range(1, 9)) if last else list(range(0, 9))
                for tile in tiles:
                    t0, tw = tile_rng(tile)
                    jm = 1 if tile == 0 else 0
                    ya, ya_b, ya_s = yaR.next(); yb, yb_b, yb_s = ybR2.next(); yc, yc_b, yc_s = ycR2.next()
                    for (dst, dst_b, sem, src, src_b) in ((ya, ya_b, ya_s, yaT, B["yaT"]), (yb, yb_b, yb_s, ybT, B["ybT"]), (yc, yc_b, yc_s, ycT, B["ycT"])):
                        S.dma("sp", sem, out=dst[:, :, 0:tw], in_=src[:, t0:t0 + tw].rearrange("(c p) t -> p c t", p=128),
                              reads=[src_b], writes=[dst_b])
                    mrg, mrg_b = mrgR.next()
                    for j in range(8):
                        mg, mg_b, mg_s = mgR.next()
                        S.dma("sp", mg_s, out=mg[:, :, 0:tw], in_=mgTv[j][:, :, t0:t0 + tw], reads=[B["mgT"]], writes=[mg_b])
                        pbs = []
                        for (wt, wt_b, yy, yy_b) in ((wA, wA_b, ya, ya_b), (wB, wB_b, yb, yb_b), (wC, wC_b, yc, yc_b)):
                            pb, pb_b = brR.next()
                            for kc in range(KC):
                                S.op("pe", lambda e, pb=pb, wt=wt, yy=yy, kc=kc, j=j: e.matmul(
                                    pb[:, 0:tw], lhsT=wt[:, kc, j * 128:(j + 1) * 128], rhs=yy[:, kc, 0:tw],
                                    start=(kc == 0), stop=(kc == KC - 1)), reads=[wt_b, yy_b], writes=[pb_b])
                            pbs.append((pb, pb_b))
                        m1, m1_b = m1R.next(); m2, m2_b = m2R.next(); m3, m3_b = m3R.next()
                        for n, (mm, mm_b) in enumerate(((m1, m1_b), (m2, m2_b), (m3, m3_b))):
                            pb, pb_b = pbs[n]
                            S.op("dve", lambda e, mm=mm, pb=pb, n=n, mg=mg: e.tensor_tensor(out=mm[:, 0:tw], in0=pb[:, 0:tw], in1=mg[:, n, 0:tw], op=ALU.mult),
                                 reads=[pb_b, mg_b], writes=[mm_b])
                        S.op("pool", lambda e, m1=m1, m2=m2: e.tensor_tensor(out=m1[:, 0:tw], in0=m1[:, 0:tw], in1=m2[:, 0:tw], op=ALU.add),
                             reads=[m1_b, m2_b], writes=[m1_b])
                        S.op("pool", lambda e, m1=m1, m3=m3, mrg=mrg, j=j: e.tensor_tensor(out=mrg[:, j, 0:tw], in0=m1[:, 0:tw], in1=m3[:, 0:tw], op=ALU.add),
                             reads=[m1_b, m3_b], pwrites=[mrg_b])
                    for s_ in range(tw // 128):
                        tok0 = t0 + s_ * 128
                        src = xs_ctx[tok0:tok0 + 128, :] if tile == 0 else xs_lat[tok0 - 256:tok0 - 128, :]
                        dstd = xc1[tok0:tok0 + 128, :] if tile == 0 else xd_lat[tok0 - 256:tok0 - 128, :]
                        dstd_b = B["xc1"] if tile == 0 else xd_lat_b
                        xl, xl_b, xl_s = xlR.next(); ost, ost_b, ost_s = ostR.next(); tt, tt_b = ttR.next()
                        ss2, ss2_b = ss2R.next(); rs, rs_b = rsR.next()
                        S.dma("sp", xl_s, out=xl[:], in_=src, reads=xs_bufs, writes=[xl_b])
                        for hf, (po, po_b) in enumerate(((pO0, pO0_b), (pO1, pO1_b))):
                            for kc in range(KC):
                                S.op("pe", lambda e, po=po, kc=kc, hf=hf: e.matmul(
                                    po[:, :], lhsT=mrg[:, kc, s_ * 128:(s_ + 1) * 128], rhs=wO[:, kc, hf * 512:(hf + 1) * 512],
                                    start=(kc == 0), stop=(kc == KC - 1)), reads=[mrg_b, wO_b], writes=[po_b])
                        for hf, (po, po_b) in enumerate(((pO0, pO0_b), (pO1, pO1_b))):
                            S.op("act", lambda e, po=po, hf=hf: e.activation(out=tt[:, hf * 512:(hf + 1) * 512], in_=po[:, :], func=AF.Square,
                                                                            accum_out=ss2[:, hf:hf + 1]),
                                 reads=[po_b], pwrites=[tt_b, ss2_b])
                        S.op("dve", lambda e: e.tensor_tensor(out=rs[:], in0=ss2[:, 0:1], in1=ss2[:, 1:2], op=ALU.add), reads=[ss2_b], writes=[rs_b])
                        S.op("dve", lambda e: e.tensor_scalar(out=rs[:], in0=rs[:], scalar1=1.0 / D, scalar2=EPS, op0=ALU.mult, op1=ALU.add),
                             reads=[rs_b], writes=[rs_b])
                        S.op("act", lambda e: e.activation(out=rs[:], in_=rs[:], func=AF.Sqrt), reads=[rs_b], writes=[rs_b])
                        S.op("dve", lambda e: e.reciprocal(out=rs[:], in_=rs[:]), reads=[rs_b], writes=[rs_b])
                        for hf, (po, po_b) in enumerate(((pO0, pO0_b), (pO1, pO1_b))):
                            S.op("dve", lambda e, po=po, hf=hf: e.tensor_tensor(out=tt[:, hf * 512:(hf + 1) * 512], in0=po[:, :],
                                                                             in1=gtg[:, jm, hf * 512:(hf + 1) * 512], op=ALU.mult),
                                 reads=[po_b, gtg_b, tt_b], pwrites=[tt_b])
                        S.op("dve", lambda e: e.scalar_tensor_tensor(out=ost[:], in0=tt[:], scalar=rs[:], in1=xl[:], op0=ALU.mult, op1=ALU.add),
                             reads=[tt_b, rs_b, xl_b], writes=[ost_b])
                        S.dma("pool", ost_s, out=dstd, in_=ost[:], reads=[ost_b], pwrites=[dstd_b])
                S.barrier()
    S.barrier()
    return nc, S


def _lay(v):
    return np.ascontiguousarray(np.asarray(v).reshape(-1, 128).T)


def _rope_perm():
    d = np.arange(128)
    return np.where((d % 64) < 32, d + 32, d - 32)


def _consts():
    f32 = np.float32
    s = np.arange(128)[:, None]; t = np.arange(128)[None, :]
    tri4 = np.stack([(s <= t), (s > t), (s >= t), (s < t)]).astype(f32)
    tok = np.arange(4096)
    row = (tok // 64).astype(f32); col = (tok % 64).astype(f32)
    freqs = (f32(10000.0) ** (-np.arange(32, dtype=f32) * f32(2.0) / f32(64))).astype(f32)
    d = np.arange(128)
    axis = d // 64; half = (d % 64) // 32; p = d % 32
    pos = np.where(axis[:, None] == 0, row[None, :], col[None, :]).astype(f32)
    ang = (pos * freqs[p][:, None]).astype(f32)
    cosT = np.cos(ang).astype(f32)
    sinT = (np.sin(ang) * np.where(half == 0, -1.0, 1.0)[:, None]).astype(f32)
    return dict(ident=np.eye(128, dtype=f32), tri4=tri4,
                cosT=cosT.astype(ml_dtypes.bfloat16), sinT=sinT.astype(ml_dtypes.bfloat16))


def prep_inputs(inp):
    f32 = np.float32
    perm = _rope_perm()
    A = lambda v: np.ascontiguousarray(np.asarray(v, dtype=f32))
    w_in = A(inp["w_in"])
    qcols = np.concatenate([O_Q + h * 128 + perm for h in range(8)])
    kcols = np.concatenate([O_K + h * 128 + perm for h in range(2)])
    shared = dict(
        w_ada=A(inp["w_ada"]), b_adaT=np.stack([_lay(inp["b_ada"][i]) for i in range(2)]),
        g_preT=np.stack([_lay(inp["g_pre"][i]) for i in range(2)]),
        g_postT=np.stack([_lay(inp["g_post"][i]) for i in range(2)]),
        w_in=w_in, w_qp=np.ascontiguousarray(w_in[:, :, qcols]), w_kp=np.ascontiguousarray(w_in[:, :, kcols]),
        conv_wT=np.ascontiguousarray(np.stack([A(inp["conv_w"][i]).T.reshape(8, 128, 3).transpose(1, 0, 2) for i in range(2)])),
        qg=np.ascontiguousarray(np.stack([np.stack([A(inp["q_norm_g"][i]), A(inp["q_norm_g"][i])[perm]], -1) for i in range(2)])),
        kg=np.ascontiguousarray(np.stack([np.stack([A(inp["k_norm_g"][i]), A(inp["k_norm_g"][i])[perm]], -1) for i in range(2)])),
        wd_f=np.ascontiguousarray(np.concatenate([A(inp["w_decay_fwd"]), A(inp["b_decay_fwd"])[:, None, :]], 1)),
        wd_b=np.ascontiguousarray(np.concatenate([A(inp["w_decay_bwd"]), A(inp["b_decay_bwd"])[:, None, :]], 1)),
        gla_g=A(inp["gla_norm_g"]),
        w_br_conv=A(inp["w_br_conv"]), w_br_attn=A(inp["w_br_attn"]), w_br_gla=A(inp["w_br_gla"]),
        b_gateT=np.stack([_lay(inp["b_gate"][i]) for i in range(2)]), w_out=A(inp["w_out"]),
        **_consts())
    in_maps = []
    for b in range(8):
        m = dict(shared)
        m["x"] = A(inp["x"][b]); m["ctx"] = A(inp["ctx"][b])
        m["c2_d"] = np.ascontiguousarray(np.stack([_lay(inp["c"][b]), _lay(inp["c_ctx"])], -1).astype(f32))
        in_maps.append(m)
    return in_maps


def kernel(**inputs):
    in_maps = prep_inputs(inputs)
    nc, _ = build(2)
    res = run_bass_kernel_spmd(nc, in_maps, core_ids=list(range(8)))
    return np.stack([np.asarray(r["y"], dtype=np.float32) for r in res.results], 0)
```

```python
import numpy as np
import ml_dtypes
from contextlib import ExitStack
import concourse.bass as bass
import concourse.mybir as mybir
from concourse.bass_utils import run_bass_kernel_spmd

F32 = mybir.dt.float32
BF16 = mybir.dt.bfloat16
AF = mybir.ActivationFunctionType
ALU = mybir.AluOpType

D = 1024
KC = 8
T = 4352
NCH = 34
EPS = 1e-6
IN_W = 12832
O_AB, O_AC, O_AX, O_AZ = 0, 1024, 2048, 3072
O_Q, O_K, O_V, O_ZA = 4096, 5120, 5376, 5632
O_GQ, O_GK, O_GV, O_RF, O_RB, O_ZG, O_MG = 6656, 7168, 7680, 8704, 8720, 8736, 9760
ATTN_SCALE = 128 ** -0.5
GLA_SCALE = 128 ** -0.5
GLA_TAU = 16.0


def tile_rng(tile):
    return (0, 256) if tile == 0 else (256 + (tile - 1) * 512, 512)


class Buf:
    __slots__ = ("name", "w", "r")

    def __init__(self, name=""):
        self.name = name
        self.w = {}
        self.r = {}


class Sched:
    def __init__(self, nc):
        self.nc = nc
        self.engs = {"pe": nc.tensor, "act": nc.scalar, "dve": nc.vector,
                     "pool": nc.gpsimd, "sp": nc.sync}
        self.sems = {}
        self.cnt = {}
        for k in self.engs:
            self.sems["e_" + k] = nc.alloc_semaphore("e_" + k)
            self.cnt["e_" + k] = 0
        self.known = {k: {} for k in self.engs}
        self.nwaits = 0
        self.nops = 0

    def _deps(self, reads, writes, pwrites):
        need = {}

        def add(d):
            for k, v in d.items():
                if v > need.get(k, 0):
                    need[k] = v
        for b in reads:
            add(b.w)
        for b in writes:
            add(b.w)
            add(b.r)
        for b in pwrites:
            add(b.r)
        return need

    def _wait(self, eng, need):
        kn = self.known[eng]
        for k, v in need.items():
            if kn.get(k, 0) >= v:
                continue
            self.engs[eng].wait_ge(self.sems[k], v)
            kn[k] = v
            self.nwaits += 1

    def _mark(self, k, v, reads, writes, pwrites):
        for b in reads:
            if b.r.get(k, 0) < v:
                b.r[k] = v
        for b in writes:
            b.w = {k: v}
            b.r = {}
        for b in pwrites:
            if b.w.get(k, 0) < v:
                b.w[k] = v

    def op(self, eng, fn, reads=(), writes=(), pwrites=()):
        own = "e_" + eng
        need = self._deps(reads, writes, pwrites)
        if eng == "pe":
            need.pop(own, None)
        self._wait(eng, need)
        inst = fn(self.engs[eng])
        self.cnt[own] += 1
        inst.then_inc(self.sems[own], 1)
        self._mark(own, self.cnt[own], reads, writes, pwrites)
        self.nops += 1

    def dma(self, eng, sem, out, in_, reads=(), writes=(), pwrites=(), **kw):
        if sem not in self.sems:
            self.sems[sem] = self.nc.alloc_semaphore(sem)
            self.cnt[sem] = 0
        need = self._deps(reads, writes, pwrites)
        if self.cnt[sem] > 0:
            need[sem] = max(need.get(sem, 0), self.cnt[sem])
        self._wait(eng, need)
        inst = self.engs[eng].dma_start(out=out, in_=in_, **kw)
        self.cnt[sem] += 16
        inst.then_inc(self.sems[sem], 16)
        self._mark(sem, self.cnt[sem], reads, writes, pwrites)
        self.nops += 1

    def barrier(self):
        need = {k: v for k, v in self.cnt.items() if v > 0}
        for eng in self.engs:
            self._wait(eng, dict(need))


class Rot:
    def __init__(self, items):
        self.items = items
        self.i = 0

    def next(self):
        it = self.items[self.i % len(self.items)]
        self.i += 1
        return it


def build(n_layers=2, debug=(), stop_after=None):
    nc = bass.Bass("TRN2", target_bir_lowering=False)
    S = Sched(nc)

    def din(name, shape, dty=F32):
        return nc.dram_tensor(name, shape, dty, kind="ExternalInput").ap()

    def scr(name, shape, dty):
        kind = "ExternalOutput" if name in debug else "Internal"
        return nc.dram_tensor(name, shape, dty, kind=kind).ap()

    x_in = din("x", [4096, D]); ctx_in = din("ctx", [256, D])
    c2_d = din("c2_d", [128, 8, 2])
    w_ada = din("w_ada", [2, D, 3 * D]); b_adaT = din("b_adaT", [2, 128, 24])
    g_preT = din("g_preT", [2, 128, 8]); g_postT = din("g_postT", [2, 128, 8])
    w_in = din("w_in", [2, D, IN_W]); w_qp = din("w_qp", [2, D, 1024]); w_kp = din("w_kp", [2, D, 256])
    conv_wT = din("conv_wT", [2, 128, 8, 3])
    qg_d = din("qg", [2, 128, 2]); kg_d = din("kg", [2, 128, 2])
    wd_f = din("wd_f", [2, 17, 512]); wd_b = din("wd_b", [2, 17, 512])
    gla_g = din("gla_g", [2, 256])
    w_bra = din("w_br_conv", [2, D, D]); w_brb = din("w_br_attn", [2, D, D]); w_brc = din("w_br_gla", [2, D, D])
    b_gateT = din("b_gateT", [2, 128, 24]); w_out = din("w_out", [2, D, D])
    ident_d = din("ident", [128, 128]); tri_d = din("tri4", [4, 128, 128])
    cos_d = din("cosT", [128, 4096], BF16); sin_d = din("sinT", [128, 4096], BF16)
    y_out = nc.dram_tensor("y", [4096, D], F32, kind="ExternalOutput").ap()

    yaT = scr("yaT", [D, T], BF16); ybT = scr("ybT", [D, T], BF16); ycT = scr("ycT", [D, T], BF16)
    mgT = scr("mgT", [3 * D, T], BF16); szT = scr("szT", [D, T], BF16)
    gqT = scr("gqT", [512, T], F32); gkT = scr("gkT", [512, T], F32); gktm = scr("gktm", [T, 512], F32)
    gv = scr("gv", [T, D], BF16); laf = scr("laf", [T, 512], F32); lab = scr("lab", [T, 512], F32)
    of_d = scr("of", [T, D], F32)
    x1 = scr("x1", [4096, D], F32); xc1 = scr("xc1", [256, D], F32)
    gtg_d = scr("gtg", [2, 2, D], F32)
    B = {n: Buf(n) for n in ["yaT", "ybT", "ycT", "mgT", "szT", "gqT", "gkT", "gktm", "gv", "laf", "lab",
                             "of", "x1", "xc1", "gtg", "y"]}

    uid = [0]
    with ExitStack() as es0:
        def mk(es):
            def sb(name, shape, dty):
                uid[0] += 1
                return es.enter_context(nc.sbuf_tensor("%s_%d" % (name, uid[0]), shape, dty)), Buf(name)

            def ps(name, shape, dty=F32):
                uid[0] += 1
                return es.enter_context(nc.psum_tensor("%s_%d" % (name, uid[0]), shape, dty)), Buf(name)
            return sb, ps
        sb0, ps0 = mk(es0)
        identf, identf_b = sb0("identf", [128, 128], F32)
        identb, identb_b = sb0("identb", [128, 128], BF16)
        onesb, onesb_b = sb0("onesb", [128, 128], BF16)
        mod, mod_b = sb0("mod", [128, 24, 2], F32)
        gsc, gsc_b = sb0("gsc", [128, 8, 2], F32)
        gtgT, gtgT_b = sb0("gtgT", [128, 8, 2], F32)
        bank = [ps0("bank%d" % i, [128, 512], F32) for i in range(8)]

        S.dma("sp", "c0", out=identf[:], in_=ident_d[:, :], writes=[identf_b])
        S.op("dve", lambda e: e.tensor_copy(out=identb[:], in_=identf[:]), reads=[identf_b], writes=[identb_b])
        S.op("dve", lambda e: e.memset(onesb[:], 1.0), writes=[onesb_b])

        rr = [0]

        def load_w(src, W, wst_rot, dst, dst_b, off, first, cast_engs=("pool",), dma_engs=("sp",)):
            done = 0
            while done < W:
                w = min(256, W - done)
                st, st_b, sem = wst_rot.next()
                de = dma_engs[rr[0] % len(dma_engs)]
                ce = cast_engs[rr[0] % len(cast_engs)]
                rr[0] += 1
                S.dma(de, sem, out=st[:, :, 0:w],
                      in_=src[:, done:done + w].rearrange("(kc p) w -> p kc w", p=128), writes=[st_b])
                o = off + done
                if ce == "act":
                    fn = lambda e, st=st, w=w, o=o: e.activation(out=dst[:, :, o:o + w], in_=st[:, :, 0:w], func=AF.Copy)
                else:
                    fn = lambda e, st=st, w=w, o=o: e.tensor_copy(out=dst[:, :, o:o + w], in_=st[:, :, 0:w])
                if first and done == 0:
                    S.op(ce, fn, reads=[st_b], writes=[dst_b])
                else:
                    S.op(ce, fn, reads=[st_b], pwrites=[dst_b])
                done += w

        for l in range(n_layers):
            last = (l == n_layers - 1)
            xs_lat = x_in if l == 0 else x1
            xs_ctx = ctx_in if l == 0 else xc1
            xs_bufs = [] if l == 0 else [B["x1"], B["xc1"]]
            xd_lat = y_out if last else x1
            xd_lat_b = B["y"] if last else B["x1"]

            with ExitStack() as esL:
                sbL, _ = mk(esL)
                hT, _ = sbL("hT", [128, KC, T], BF16)
                hT_b = [Buf("hT%d" % i) for i in range(9)]
                wst = [sbL("wst%d" % i, [128, KC, 256], F32) for i in range(2)]
                wst_rot = Rot([(wst[i][0], wst[i][1], "wst%d" % i) for i in range(2)])
                wbf = [sbL("wbf%d" % i, [128, KC, 512], BF16) for i in range(2)]
                wbf_rot = Rot(wbf)

                with ExitStack() as es:
                    sb, _ = mk(es)
                    c2, c2_b = sb("c2", [128, 8, 2], F32)
                    sc2, sc2_b = sb("sc2", [128, 8, 2], F32)
                    bada, bada_b = sb("bada", [128, 24], F32)
                    gpre, gpre_b = sb("gpre", [128, 8], F32)
                    gpost, gpost_b = sb("gpost", [128, 8], F32)
                    psA, psA_b = bank[0]
                    psAv = psA[:, 0:48].rearrange("p (j t) -> p j t", t=2)
                    S.dma("sp", "c1", out=c2[:], in_=c2_d[:, :, :], writes=[c2_b])
                    S.dma("sp", "c2", out=bada[:], in_=b_adaT[l], writes=[bada_b])
                    S.dma("sp", "c3", out=gpre[:], in_=g_preT[l], writes=[gpre_b])
                    S.dma("sp", "c4", out=gpost[:], in_=g_postT[l], writes=[gpost_b])
                    S.op("act", lambda e: e.activation(out=sc2[:], in_=c2[:], func=AF.Silu), reads=[c2_b], writes=[sc2_b])
                    for blk in range(12):
                        st, st_b, sem = wst_rot.next()
                        S.dma("sp", sem, out=st[:],
                              in_=w_ada[l, :, blk * 256:(blk + 1) * 256].rearrange("(kc p) w -> p kc w", p=128),
                              writes=[st_b])
                        for jj in range(2):
                            j = blk * 2 + jj
                            for kc in range(KC):
                                S.op("pe", lambda e, st=st, jj=jj, j=j, kc=kc: e.matmul(
                                    psAv[:, j, :], lhsT=st[:, kc, jj * 128:(jj + 1) * 128], rhs=sc2[:, kc, :],
                                    start=(kc == 0), stop=(kc == KC - 1)),
                                    reads=[st_b, sc2_b], writes=[psA_b])
                    for j in range(2):
                        S.op("dve", lambda e, j=j: e.tensor_tensor(out=mod[:, :, j], in0=psAv[:, :, j], in1=bada[:], op=ALU.add),
                             reads=[psA_b, bada_b], writes=[mod_b])
                    for j in range(2):
                        S.op("dve", lambda e, j=j: e.scalar_tensor_tensor(
                            out=gsc[:, :, j], in0=mod[:, 8:16, j], scalar=1.0, in1=gpre[:], op0=ALU.add, op1=ALU.mult),
                            reads=[mod_b, gpre_b], writes=[gsc_b])
                        S.op("dve", lambda e, j=j: e.tensor_tensor(out=gtgT[:, :, j], in0=mod[:, 16:24, j], in1=gpost[:], op=ALU.mult),
                             reads=[mod_b, gpost_b], writes=[gtgT_b])
                    for j in range(2):
                        S.dma("sp", "c5", out=gtg_d[l, j].rearrange("(kc p) -> p kc", p=128), in_=gtgT[:, :, j],
                              reads=[gtgT_b], writes=[B["gtg"]], allow_slow_non_contiguous=True)
                    S.barrier()

                with ExitStack() as es:
                    sb, _ = mk(es)
                    xin = Rot([sb("xin%d" % i, [128, D], F32) for i in range(2)])
                    xn = Rot([sb("xn%d" % i, [128, D], BF16) for i in range(2)])
                    ssr = Rot([sb("ss%d" % i, [128, 1], F32) for i in range(2)])
                    rsr = Rot([sb("rs%d" % i, [128, 1], F32) for i in range(2)])
                    psr = Rot(bank[0:2])
                    for i in range(NCH):
                        j = 1 if i < 2 else 0
                        src = xs_ctx[i * 128:(i + 1) * 128, :] if i < 2 else xs_lat[(i - 2) * 128:(i - 1) * 128, :]
                        tile = 0 if i < 2 else 1 + (i - 2) // 4
                        xi, xi_b = xin.next(); xv, xv_b = xn.next(); ss, ss_b = ssr.next(); rs, rs_b = rsr.next()
                        pT, pT_b = psr.next()
                        pTv = pT[:].bitcast(BF16).rearrange("p (c t) -> p c t", t=128)
                        S.dma("sp", "xin%d" % (i % 2), out=xi[:], in_=src, reads=xs_bufs, writes=[xi_b])
                        S.op("act", lambda e, xi=xi, xv=xv, ss=ss: e.activation(out=xv[:], in_=xi[:], func=AF.Square, accum_out=ss[:]),
                             reads=[xi_b], writes=[xv_b, ss_b])
                        S.op("dve", lambda e, ss=ss, rs=rs: e.tensor_scalar(out=rs[:], in0=ss[:], scalar1=1.0 / D, scalar2=EPS,
                                                                           op0=ALU.mult, op1=ALU.add), reads=[ss_b], writes=[rs_b])
                        S.op("act", lambda e, rs=rs: e.activation(out=rs[:], in_=rs[:], func=AF.Sqrt), reads=[rs_b], writes=[rs_b])
                        S.op("dve", lambda e, rs=rs: e.reciprocal(out=rs[:], in_=rs[:]), reads=[rs_b], writes=[rs_b])
                        S.op("act", lambda e, xi=xi, xv=xv, rs=rs: e.activation(out=xv[:], in_=xi[:], func=AF.Copy, scale=rs[:]),
                             reads=[xi_b, rs_b], writes=[xv_b])
                        for c in range(KC):
                            S.op("pe", lambda e, xv=xv, c=c, pTv=pTv: e.transpose(out=pTv[:, c, :], in_=xv[:, c * 128:(c + 1) * 128],
                                                                                  identity=identb[:]),
                                 reads=[xv_b, identb_b], writes=[pT_b])
                        for c in range(KC):
                            S.op("dve", lambda e, c=c, i=i, j=j, pTv=pTv: e.tensor_scalar(
                                out=hT[:, c, i * 128:(i + 1) * 128], in0=pTv[:, c, :],
                                scalar1=gsc[:, c, j:j + 1], scalar2=mod[:, c, j:j + 1], op0=ALU.mult, op1=ALU.add),
                                reads=[pT_b, gsc_b, mod_b], pwrites=[hT_b[tile]])
                    S.barrier()
                if stop_after == "P1":
                    hT_dbg = scr("hT_dbg", [128, KC, T], BF16)
                    S.dma("pool", "dbg", out=hT_dbg[:, :, :], in_=hT[:], reads=hT_b, writes=[Buf()])
                    S.barrier()
                    return nc, S

                def proj_fm(wt, wt_b, c0, M, tile, pb, pb_b):
                    t0, tw = tile_rng(tile)
                    for kc in range(KC):
                        S.op("pe", lambda e, kc=kc: e.matmul(pb[0:M, 0:tw], lhsT=wt[:, kc, c0:c0 + M], rhs=hT[:, kc, t0:t0 + tw],
                                                             start=(kc == 0), stop=(kc == KC - 1)),
                             reads=[wt_b, hT_b[tile]], writes=[pb_b])

                def proj_tm(wt, wt_b, c0, N, i, pb, pb_b):
                    tile = 0 if i < 2 else 1 + (i - 2) // 4
                    for kc in range(KC):
                        S.op("pe", lambda e, kc=kc: e.matmul(pb[:, 0:N], lhsT=hT[:, kc, i * 128:(i + 1) * 128], rhs=wt[:, kc, c0:c0 + N],
                                                             start=(kc == 0), stop=(kc == KC - 1)),
                             reads=[wt_b, hT_b[tile]], writes=[pb_b])

                with ExitStack() as es:
                    sb, _ = mk(es)
                    UW = T + 4
                    ufull, _ = sb("ufull", [128, UW], F32)
                    gbfull, _ = sb("gbfull", [128, T], F32)
                    u_b = [Buf("u%d" % i) for i in range(9)]
                    gb_b = [Buf("gb%d" % i) for i in range(9)]
                    upad_b = Buf("upad")
                    cw, cw_b = sb("cw", [128, 8, 3], F32)
                    bg, bg_b = sb("bg", [128, 24], F32)
                    wdf, wdf_b = sb("wdf", [17, 512], F32)
                    wdb, wdb_b = sb("wdb", [17, 512], F32)
                    tmpA = Rot([sb("tmpA%d" % i, [128, 512], F32) for i in range(2)])
                    tmpB = Rot([sb("tmpB%d" % i, [128, 512], F32) for i in range(2)])
                    ytmp = Rot([sb("ytmp%d" % i, [128, 512], F32) for i in range(2)])
                    ytmp2 = Rot([sb("ytmpb%d" % i, [128, 512], F32) for i in range(2)])
                    stf = Rot([sb("stf%d" % i, [128, 512], F32) + ("stf%d" % i,) for i in range(3)])
                    stb = Rot([sb("stb%d" % i, [128, 512], BF16) + ("stb%d" % i,) for i in range(3)])
                    rT = [sb("rT%d" % i, [32, 512], F32) for i in range(4)]
                    banks = Rot(bank)
                    S.dma("sp", "c1", out=cw[:], in_=conv_wT[l], writes=[cw_b])
                    S.dma("sp", "c2", out=bg[:], in_=b_gateT[l], writes=[bg_b])
                    S.dma("sp", "c3", out=wdf[:], in_=wd_f[l], writes=[wdf_b])
                    S.dma("sp", "c4", out=wdb[:], in_=wd_b[l], writes=[wdb_b])
                    for a in (0, 257, 4355):
                        wdt = 2 if a == 257 else 1
                        S.op("pool", lambda e, a=a, wdt=wdt: e.memset(ufull[:, a:a + wdt], 0.0), pwrites=[upad_b])
                    for t_, tb in rT:
                        S.op("pool", lambda e, t_=t_: e.memset(t_[:], 1.0), writes=[tb])

                    def ucol(tile):
                        t0, tw = tile_rng(tile)
                        return (1 if tile == 0 else 259 + (t0 - 256)), tw

                    wq, wq_ready, wq_pos = [], {}, [0]

                    def mk_loader(src, W):
                        def ld():
                            wt_, wt_b_ = wbf_rot.next()
                            load_w(src, W, wst_rot, wt_, wt_b_, 0, True)
                            return wt_, wt_b_
                        return ld

                    def w_issue(j):
                        if j < len(wq) and j not in wq_ready:
                            wq_ready[j] = wq[j]()

                    def next_w():
                        j = wq_pos[0]
                        w_issue(j)
                        r = wq_ready.pop(j)
                        wq_pos[0] += 1
                        w_issue(j + 1)
                        return r

                    for off_, ncols_ in ((O_GQ, 512), (O_GK, 512), (O_ZG, 1024), (O_MG, 3072)):
                        for c0_ in range(0, ncols_, 512):
                            wq.append(mk_loader(w_in[l, :, off_ + c0_: off_ + c0_ + min(512, ncols_ - c0_)], min(512, ncols_ - c0_)))
                    wq.append(mk_loader(w_in[l, :, O_GK:O_GK + 512], 512))
                    for half_ in range(2):
                        wq.append(mk_loader(w_in[l, :, O_GV + half_ * 512:O_GV + (half_ + 1) * 512], 512))
                    wq.append(mk_loader(w_in[l, :, O_RF:O_RF + 32], 32))

                    def conv_weights(cb_):
                        wt_, wt_b_ = wbf_rot.next()
                        for n, off in enumerate((O_AB, O_AC, O_AX, O_AZ)):
                            load_w(w_in[l, :, off + cb_ * 128: off + (cb_ + 1) * 128], 128, wst_rot, wt_, wt_b_, n * 128, n == 0)
                        return wt_, wt_b_

                    nxt_w = conv_weights(0)
                    for cb in range(8):
                        wt, wt_b = nxt_w

                        def conv_piece(tile):
                            t0, tw = tile_rng(tile)
                            uc, _ = ucol(tile)
                            nb = [u_b[k] for k in (tile - 1, tile, tile + 1) if 0 <= k < 9] + [upad_b]
                            yt, yt_b = ytmp.next()
                            sbf, sbf_b, sbf_s = stb.next()
                            S.op("dve", lambda e, yt=yt, uc=uc, tw=tw, cb=cb: e.tensor_scalar(
                                out=yt[:, 0:tw], in0=ufull[:, uc - 1:uc - 1 + tw], scalar1=cw[:, cb, 0:1], scalar2=None, op0=ALU.mult),
                                reads=nb + [cw_b], writes=[yt_b])
                            S.op("dve", lambda e, yt=yt, uc=uc, tw=tw, cb=cb: e.scalar_tensor_tensor(
                                out=yt[:, 0:tw], in0=ufull[:, uc:uc + tw], scalar=cw[:, cb, 1:2], in1=yt[:, 0:tw],
                                op0=ALU.mult, op1=ALU.add), reads=nb + [cw_b, yt_b], writes=[yt_b])
                            S.op("dve", lambda e, yt=yt, uc=uc, tw=tw, cb=cb: e.scalar_tensor_tensor(
                                out=yt[:, 0:tw], in0=ufull[:, uc + 1:uc + 1 + tw], scalar=cw[:, cb, 2:3], in1=yt[:, 0:tw],
                                op0=ALU.mult, op1=ALU.add), reads=nb + [cw_b, yt_b], writes=[yt_b])
                            S.op("pool", lambda e, yt=yt, sbf=sbf, t0=t0, tw=tw: e.tensor_tensor(
                                out=sbf[:, 0:tw], in0=yt[:, 0:tw], in1=gbfull[:, t0:t0 + tw], op=ALU.mult),
                                reads=[yt_b, gb_b[tile]], writes=[sbf_b])
                            S.dma("pool", sbf_s, out=yaT[cb * 128:(cb + 1) * 128, t0:t0 + tw], in_=sbf[:, 0:tw],
                                  reads=[sbf_b], pwrites=[B["yaT"]])

                        for tile in range(9):
                            if tile == 3 and cb + 1 < 8:
                                nxt_w = conv_weights(cb + 1)
                            if tile == 3 and cb == 7:
                                w_issue(0)
                            t0, tw = tile_rng(tile)
                            uc, _ = ucol(tile)
                            pbs = [banks.next() for _ in range(4)]
                            for n in range(4):
                                proj_fm(wt, wt_b, n * 128, 128, tile, pbs[n][0], pbs[n][1])
                            (pB, pB_b), (pC, pC_b), (pX, pX_b), (pZ, pZ_b) = pbs
                            ta, ta_b = tmpA.next(); tb_, tb_b = tmpB.next()
                            S.op("act", lambda e, ta=ta, pZ=pZ, tw=tw: e.activation(out=ta[:, 0:tw], in_=pZ[:, 0:tw], func=AF.Silu),
                                 reads=[pZ_b], writes=[ta_b])
                            S.op("dve", lambda e, ta=ta, pB=pB, t0=t0, tw=tw: e.tensor_tensor(
                                out=gbfull[:, t0:t0 + tw], in0=pB[:, 0:tw], in1=ta[:, 0:tw], op=ALU.mult),
                                reads=[pB_b, ta_b], writes=[gb_b[tile]])
                            S.op("act", lambda e, tb_=tb_, pX=pX, tw=tw: e.activation(out=tb_[:, 0:tw], in_=pX[:, 0:tw], func=AF.Copy),
                                 reads=[pX_b], writes=[tb_b])
                            S.op("dve", lambda e, tb_=tb_, pC=pC, uc=uc, tw=tw: e.tensor_tensor(
                                out=ufull[:, uc:uc + tw], in0=pC[:, 0:tw], in1=tb_[:, 0:tw], op=ALU.mult),
                                reads=[pC_b, tb_b], writes=[u_b[tile]])
                            if tile >= 1:
                                conv_piece(tile - 1)
                        conv_piece(8)

                    def fm_slot(off, ncols, dst, dst_b, kind):
                        for c0 in range(0, ncols, 512):
                            W = min(512, ncols - c0)
                            wt, wt_b = next_w()
                            for cbi in range(W // 128):
                                row0 = c0 + cbi * 128
                                for tile in range(9):
                                    t0, tw = tile_rng(tile)
                                    pb, pb_b = banks.next()
                                    proj_fm(wt, wt_b, cbi * 128, 128, tile, pb, pb_b)
                                    if kind in ("gq", "gk"):
                                        st, st_b, st_s = stf.next()
                                        sc = GLA_SCALE if kind == "gq" else 1.0
                                        S.op("act", lambda e, st=st, pb=pb, tw=tw, sc=sc: e.activation(
                                            out=st[:, 0:tw], in_=pb[:, 0:tw], func=AF.Copy, scale=sc), reads=[pb_b], writes=[st_b])
                                    elif kind == "zg":
                                        st, st_b, st_s = stb.next()
                                        S.op("act", lambda e, st=st, pb=pb, tw=tw: e.activation(
                                            out=st[:, 0:tw], in_=pb[:, 0:tw], func=AF.Silu), reads=[pb_b], writes=[st_b])
                                    else:
                                        st, st_b, st_s = stb.next()
                                        jcol = row0 // 128
                                        S.op("act", lambda e, st=st, pb=pb, tw=tw, jcol=jcol: e.activation(
                                            out=st[:, 0:tw], in_=pb[:, 0:tw], func=AF.Sigmoid, bias=bg[:, jcol:jcol + 1]),
                                            reads=[pb_b, bg_b], writes=[st_b])
                                    S.dma("pool", st_s, out=dst[row0:row0 + 128, t0:t0 + tw], in_=st[:, 0:tw],
                                          reads=[st_b], pwrites=[dst_b])

                    fm_slot(O_GQ, 512, gqT, B["gqT"], "gq")
                    fm_slot(O_GK, 512, gkT, B["gkT"], "gk")
                    fm_slot(O_ZG, 1024, szT, B["szT"], "zg")
                    fm_slot(O_MG, 3072, mgT, B["mgT"], "mg")

                    wt, wt_b = next_w()
                    for i in range(NCH):
                        pb, pb_b = banks.next()
                        proj_tm(wt, wt_b, 0, 512, i, pb, pb_b)
                        st, st_b, st_s = stf.next()
                        S.op("act", lambda e, st=st, pb=pb: e.activation(out=st[:], in_=pb[:], func=AF.Copy), reads=[pb_b], writes=[st_b])
                        S.dma("pool", st_s, out=gktm[i * 128:(i + 1) * 128, :], in_=st[:], reads=[st_b], pwrites=[B["gktm"]])
                    for half in range(2):
                        wt, wt_b = next_w()
                        for i in range(NCH):
                            pb, pb_b = banks.next()
                            proj_tm(wt, wt_b, 0, 512, i, pb, pb_b)
                            st, st_b, st_s = stb.next()
                            S.op("act", lambda e, st=st, pb=pb: e.activation(out=st[:], in_=pb[:], func=AF.Copy), reads=[pb_b], writes=[st_b])
                            S.dma("pool", st_s, out=gv[i * 128:(i + 1) * 128, half * 512:(half + 1) * 512], in_=st[:],
                                  reads=[st_b], pwrites=[B["gv"]])

                    wt, wt_b = next_w()
                    assert wq_pos[0] == len(wq) and not wq_ready
                    for tile in range(9):
                        t0, tw = tile_rng(tile)
                        for dr in range(2):
                            r_, r_b = rT[dr * 2 + tile % 2]
                            pb, pb_b = banks.next()
                            proj_fm(wt, wt_b, dr * 16, 16, tile, pb, pb_b)
                            S.op("act", lambda e, r_=r_, pb=pb, tw=tw: e.activation(out=r_[0:16, 0:tw], in_=pb[0:16, 0:tw], func=AF.Copy),
                                 reads=[pb_b], writes=[r_b])
                            wd, wd_b_ = (wdf, wdf_b) if dr == 0 else (wdb, wdb_b)
                            dst, dst_b = (laf, B["laf"]) if dr == 0 else (lab, B["lab"])
                            for sub in range(tw // 128):
                                pz, pz_b = banks.next()
                                S.op("pe", lambda e, pz=pz, r_=r_, sub=sub, wd=wd: e.matmul(
                                    pz[:, :], lhsT=r_[0:17, sub * 128:(sub + 1) * 128], rhs=wd[0:17, :], start=True, stop=True),
                                    reads=[r_b, wd_b_], writes=[pz_b])
                                st, st_b, st_s = stf.next()
                                S.op("act", lambda e, st=st, pz=pz: e.activation(out=st[:], in_=pz[:], func=AF.Exp, scale=-1.0),
                                     reads=[pz_b], writes=[st_b])
                                S.op("act", lambda e, st=st: e.activation(out=st[:], in_=st[:], func=AF.Ln, bias=1.0),
                                     reads=[st_b], writes=[st_b])
                                S.op("dve", lambda e, st=st: e.tensor_scalar(out=st[:], in0=st[:], scalar1=-1.0 / GLA_TAU, scalar2=None, op0=ALU.mult),
                                     reads=[st_b], writes=[st_b])
                                r0 = t0 + sub * 128
                                S.dma("pool", st_s, out=dst[r0:r0 + 128, :], in_=st[:], reads=[st_b], pwrites=[dst_b])
                    S.barrier()
                if stop_after == "P2":
                    return nc, S

                with ExitStack() as es:
                    sb, _ = mk(es)
                    kT, _ = sb("kT", [128, 2, T], BF16)
                    kT_b = [Buf("kT0"), Buf("kT1")]
                    Vs, V_b = sb("Vs", [128, NCH, 256], BF16)
                    cosS, cos_b = sb("cosS", [128, 4096], BF16)
                    sinS, sin_b = sb("sinS", [128, 4096], BF16)
                    qgS, qgS_b = sb("qgS", [128, 2], F32)
                    kgS, kgS_b = sb("kgS", [128, 2], F32)
                    qaR = Rot([sb("qa%d" % i, [128, 512], F32) for i in range(2)])
                    qbR = Rot([sb("qb%d" % i, [128, 512], F32) for i in range(2)])
                    qsqR = Rot([sb("qsq%d" % i, [128, 512], BF16) for i in range(2)])
                    rstR = Rot([sb("rst%d" % i, [128, 512], F32) for i in range(2)])
                    qrR = Rot([sb("qr%d" % i, [128, 512], BF16) for i in range(2)])
                    szR = Rot([sb("szr%d" % i, [128, 512], F32) for i in range(2)])
                    pTR = Rot([sb("pT%d" % i, [128, 512], BF16) for i in range(3)])
                    rDR = Rot([sb("rD%d" % i, [128, 512], F32) for i in range(2)])
                    ybR = Rot([sb("ybs%d" % i, [128, 512], BF16) + ("ybs%d" % i,) for i in range(2)])
                    (pA, pA_b), (pBk, pBk_b), (pSS, pSS_b), (pZ, pZ_b) = bank[0], bank[1], bank[2], bank[3]
                    SbR = Rot([bank[4], bank[5]])
                    (pO, pO_b), (pD, pD_b) = bank[6], bank[7]
                    S.dma("sp", "c1", out=cosS[:], in_=cos_d[:, :], writes=[cos_b])
                    S.dma("sp", "c2", out=sinS[:], in_=sin_d[:, :], writes=[sin_b])
                    S.dma("sp", "c3", out=qgS[:], in_=qg_d[l], writes=[qgS_b])
                    S.dma("sp", "c4", out=kgS[:], in_=kg_d[l], writes=[kgS_b])

                    def normrope_a(gS, gS_b, tile, rope):
                        t0, tw = tile_rng(tile)
                        qa, qa_b = qaR.next(); qb, qb_b = qbR.next(); qsq, qsq_b = qsqR.next(); rst, rst_b = rstR.next()
                        S.op("act", lambda e: e.activation(out=qa[:, 0:tw], in_=pA[:, 0:tw], func=AF.Copy, scale=gS[:, 0:1]),
                             reads=[pA_b, gS_b], writes=[qa_b])
                        S.op("act", lambda e: e.activation(out=qsq[:, 0:tw], in_=pA[:, 0:tw], func=AF.Square),
                             reads=[pA_b], writes=[qsq_b])
                        if rope:
                            S.op("act", lambda e: e.activation(out=qb[:, 0:tw], in_=pBk[:, 0:tw], func=AF.Copy, scale=gS[:, 1:2]),
                                 reads=[pBk_b, gS_b], writes=[qb_b])
                        S.op("pe", lambda e: e.matmul(pSS[:, 0:tw], lhsT=onesb[:], rhs=qsq[:, 0:tw], start=True, stop=True),
                             reads=[onesb_b, qsq_b], writes=[pSS_b])
                        S.op("dve", lambda e: e.tensor_scalar(out=rst[:, 0:tw], in0=pSS[:, 0:tw], scalar1=1.0 / 128, scalar2=EPS,
                                                              op0=ALU.mult, op1=ALU.add), reads=[pSS_b], writes=[rst_b])
                        return dict(t0=t0, tw=tw, rope=rope, qa=qa, qa_b=qa_b, qb=qb, qb_b=qb_b, rst=rst, rst_b=rst_b)

                    def normrope_b(cx, out_ap, out_w, out_pw):
                        t0, tw, rope = cx["t0"], cx["tw"], cx["rope"]
                        qa, qa_b, qb, qb_b, rst, rst_b = cx["qa"], cx["qa_b"], cx["qb"], cx["qb_b"], cx["rst"], cx["rst_b"]
                        S.op("act", lambda e: e.activation(out=rst[:, 0:tw], in_=rst[:, 0:tw], func=AF.Ln), reads=[rst_b], writes=[rst_b])
                        S.op("act", lambda e: e.activation(out=rst[:, 0:tw], in_=rst[:, 0:tw], func=AF.Exp, scale=-0.5), reads=[rst_b], writes=[rst_b])
                        if rope:
                            lt0 = t0 - 256
                            S.op("dve", lambda e: e.tensor_tensor(out=qa[:, 0:tw], in0=qa[:, 0:tw], in1=cosS[:, lt0:lt0 + tw], op=ALU.mult),
                                 reads=[qa_b, cos_b], writes=[qa_b])
                            S.op("pool", lambda e: e.tensor_tensor(out=qb[:, 0:tw], in0=qb[:, 0:tw], in1=sinS[:, lt0:lt0 + tw], op=ALU.mult),
                                 reads=[qb_b, sin_b], writes=[qb_b])
                            S.op("dve", lambda e: e.tensor_tensor(out=qa[:, 0:tw], in0=qa[:, 0:tw], in1=qb[:, 0:tw], op=ALU.add),
                                 reads=[qa_b, qb_b], writes=[qa_b])
                        S.op("dve", lambda e: e.tensor_tensor(out=out_ap, in0=qa[:, 0:tw], in1=rst[:, 0:tw], op=ALU.mult),
                             reads=[qa_b, rst_b], writes=out_w, pwrites=out_pw)

                    def normrope(gS, gS_b, tile, rope, out_ap, out_w, out_pw):
                        normrope_b(normrope_a(gS, gS_b, tile, rope), out_ap, out_w, out_pw)

                    def PRE_HEAD0():
                        wt_, wt_b_ = wbf_rot.next()
                        load_w(w_in[l, :, O_Q:O_Q + 128], 128, wst_rot, wt_, wt_b_, 0, True)
                        load_w(w_qp[l, :, 0:128], 128, wst_rot, wt_, wt_b_, 128, False)
                        load_w(w_in[l, :, O_ZA:O_ZA + 128], 128, wst_rot, wt_, wt_b_, 256, False)
                        return wt_, wt_b_

                    wt, wt_b = wbf_rot.next()
                    load_w(w_in[l, :, O_K:O_K + 256], 256, wst_rot, wt, wt_b, 0, True)
                    load_w(w_kp[l, :, :], 256, wst_rot, wt, wt_b, 256, False)
                    wtV, wtV_b = wbf_rot.next()
                    load_w(w_in[l, :, O_V:O_V + 256], 256, wst_rot, wtV, wtV_b, 0, True)
                    for kv in range(2):
                        for tile in range(9):
                            t0, tw = tile_rng(tile)
                            proj_fm(wt, wt_b, kv * 128, 128, tile, pA, pA_b)
                            if tile > 0:
                                proj_fm(wt, wt_b, 256 + kv * 128, 128, tile, pBk, pBk_b)
                            normrope(kgS, kgS_b, tile, tile > 0, kT[:, kv, t0:t0 + tw], [], [kT_b[kv]])
                    wt, wt_b = wtV, wtV_b
                    HW0 = PRE_HEAD0()
                    for i in range(NCH):
                        pb, pb_b = SbR.next()
                        proj_tm(wt, wt_b, 0, 256, i, pb, pb_b)
                        S.op("act", lambda e, pb=pb, i=i: e.activation(out=Vs[:, i, :], in_=pb[:, 0:256], func=AF.Copy),
                             reads=[pb_b], pwrites=[V_b])

                    q_tiles = list(range(1, 9)) if last else list(range(0, 9))
                    def load_head_w(h):
                        wt, wt_b = wbf_rot.next()
                        load_w(w_in[l, :, O_Q + h * 128:O_Q + (h + 1) * 128], 128, wst_rot, wt, wt_b, 0, True)
                        load_w(w_qp[l, :, h * 128:(h + 1) * 128], 128, wst_rot, wt, wt_b, 128, False)
                        load_w(w_in[l, :, O_ZA + h * 128:O_ZA + (h + 1) * 128], 128, wst_rot, wt, wt_b, 256, False)
                        return wt, wt_b

                    def make_head(h, wt, wt_b):
                        kv = h // 4

                        def prologue_a(tile):
                            t0, tw = tile_rng(tile)
                            qr, qr_b = qrR.next(); sz, sz_b = szR.next()
                            proj_fm(wt, wt_b, 0, 128, tile, pA, pA_b)
                            if tile > 0:
                                proj_fm(wt, wt_b, 128, 128, tile, pBk, pBk_b)
                            proj_fm(wt, wt_b, 256, 128, tile, pZ, pZ_b)
                            cx = normrope_a(qgS, qgS_b, tile, tile > 0)
                            return dict(tile=tile, tw=tw, qr=qr, qr_b=qr_b, sz=sz, sz_b=sz_b, cx=cx)

                        def prologue_b(P):
                            tw, qr, qr_b, sz, sz_b = P["tw"], P["qr"], P["qr_b"], P["sz"], P["sz_b"]
                            normrope_b(P["cx"], qr[:, 0:tw], [qr_b], [])
                            S.op("act", lambda e: e.activation(out=sz[:, 0:tw], in_=pZ[:, 0:tw], func=AF.Exp, scale=-1.0),
                                 reads=[pZ_b], writes=[sz_b])
                            S.op("dve", lambda e: e.tensor_scalar(out=sz[:, 0:tw], in0=sz[:, 0:tw], scalar1=1.0, scalar2=None, op0=ALU.add),
                                 reads=[sz_b], writes=[sz_b])
                            S.op("dve", lambda e: e.reciprocal(out=sz[:, 0:tw], in_=sz[:, 0:tw]), reads=[sz_b], writes=[sz_b])
                            S.op("dve", lambda e: e.tensor_tensor(out=sz[:, 0:tw], in0=pZ[:, 0:tw], in1=sz[:, 0:tw], op=ALU.mult),
                                 reads=[pZ_b, sz_b], writes=[sz_b])

                        def inner(tile, qr, qr_b, sz, sz_b, hook=None):
                            t0, tw = tile_rng(tile)
                            kcs = [0, 1] if tile == 0 else list(range(NCH))
                            LA = 2
                            nk = len(kcs)
                            pslots = {}

                            def emit_s(idx):
                                kc = kcs[idx]
                                sbk, sbk_b = SbR.next()
                                p_, p_b = pTR.next()
                                pslots[idx] = (p_, p_b)
                                S.op("pe", lambda e, sbk=sbk, kc=kc: e.matmul(
                                    sbk[:, 0:tw], lhsT=kT[:, kv, kc * 128:(kc + 1) * 128], rhs=qr[:, 0:tw], start=True, stop=True),
                                    reads=[kT_b[kv], qr_b], writes=[sbk_b])
                                S.op("act", lambda e, sbk=sbk, p_=p_: e.activation(out=p_[:, 0:tw], in_=sbk[:, 0:tw], func=AF.Exp, scale=ATTN_SCALE),
                                     reads=[sbk_b], writes=[p_b])

                            for idx in range(min(LA, nk)):
                                emit_s(idx)
                            for idx in range(nk):
                                kc = kcs[idx]
                                p_, p_b = pslots.pop(idx)
                                first, lastk = (idx == 0), (idx == nk - 1)
                                S.op("pe", lambda e, p_=p_, kc=kc, first=first, lastk=lastk: e.matmul(
                                    pO[:, 0:tw], lhsT=Vs[:, kc, kv * 128:(kv + 1) * 128], rhs=p_[:, 0:tw], start=first, stop=lastk),
                                    reads=[V_b, p_b], writes=[pO_b])
                                S.op("pe", lambda e, p_=p_, first=first, lastk=lastk: e.matmul(
                                    pD[:, 0:tw], lhsT=onesb[:], rhs=p_[:, 0:tw], start=first, stop=lastk),
                                    reads=[onesb_b, p_b], writes=[pD_b])
                                if idx + LA < nk:
                                    emit_s(idx + LA)
                                if hook is not None and idx == min(6, nk - 1):
                                    hook()
                            rD, rD_b = rDR.next(); ybs, ybs_b, ybs_s = ybR.next()
                            S.op("dve", lambda e: e.reciprocal(out=rD[:, 0:tw], in_=pD[:, 0:tw]), reads=[pD_b], writes=[rD_b])
                            S.op("dve", lambda e: e.tensor_tensor(out=rD[:, 0:tw], in0=pO[:, 0:tw], in1=rD[:, 0:tw], op=ALU.mult),
                                 reads=[pO_b, rD_b], writes=[rD_b])
                            S.op("dve", lambda e: e.tensor_tensor(out=ybs[:, 0:tw], in0=rD[:, 0:tw], in1=sz[:, 0:tw], op=ALU.mult),
                                 reads=[rD_b, sz_b], writes=[ybs_b])
                            S.dma("pool", ybs_s, out=ybT[h * 128:(h + 1) * 128, t0:t0 + tw], in_=ybs[:, 0:tw],
                                  reads=[ybs_b], pwrites=[B["ybT"]])

                        return prologue_a, prologue_b, inner

                    HW = {0: HW0}
                    HF = {}

                    def head_fns(h):
                        if h not in HF:
                            HF[h] = make_head(h, *HW[h])
                        return HF[h]

                    items = [(h, tile) for h in range(8) for tile in q_tiles]
                    cur = head_fns(0)[0](items[0][1])
                    head_fns(0)[1](cur)
                    for n, (h, tile) in enumerate(items):
                        if tile == q_tiles[-2] and h + 1 < 8:
                            HW[h + 1] = load_head_w(h + 1)
                        if n + 1 < len(items):
                            nh, nt = items[n + 1]
                            nxt = head_fns(nh)[0](nt)
                            hook = (lambda nh=nh, nxt=nxt: head_fns(nh)[1](nxt))
                        else:
                            nxt, hook = None, None
                        head_fns(h)[2](tile, cur["qr"], cur["qr_b"], cur["sz"], cur["sz_b"], hook=hook)
                        cur = nxt
                    S.barrier()
            if stop_after == "P3":
                return nc, S

            with ExitStack() as es:
                sb, _ = mk(es)
                tri, tri_b = sb("tri", [128, 4, 128], F32)
                maskF, maskF_b = sb("maskF", [128, 4, 128], F32)
                maskB, maskB_b = sb("maskB", [128, 4, 128], F32)
                Sst, Sst_b = sb("Sst", [128, 4, 256], F32)
                SbfR = Rot([sb("Sbf%d" % i, [128, 4, 256], BF16) for i in range(2)])
                gbc, gbc_b = sb("gbc", [128, 256], F32)
                qTl = Rot([sb("qTl%d" % i, [128, 4, 128], F32) + ("qTl%d" % i,) for i in range(2)])
                kTl = Rot([sb("kTl%d" % i, [128, 4, 128], F32) + ("kTl%d" % i,) for i in range(2)])
                ktml = Rot([sb("ktml%d" % i, [128, 512], F32) + ("ktml%d" % i,) for i in range(2)])
                vl = Rot([sb("vl%d" % i, [128, 1024], BF16) + ("vl%d" % i,) for i in range(3)])
                lal = Rot([sb("lal%d" % i, [128, 512], F32) + ("lal%d" % i,) for i in range(2)])
                ofl = Rot([sb("ofl%d" % i, [128, 1024], F32) + ("ofl%d" % i,) for i in range(3)])
                szl = Rot([sb("szl%d" % i, [128, 8, 128], BF16) + ("szl%d" % i,) for i in range(3)])
                EqR = Rot([sb("Eq%d" % i, [128, 4, 128], F32) for i in range(3)])
                EkR = Rot([sb("Ek%d" % i, [128, 4, 128], F32) for i in range(2)])
                EkkR = Rot([sb("Ekk%d" % i, [128, 512], F32) for i in range(2)])
                qtR = Rot([sb("qt%d" % i, [128, 4, 128], BF16) for i in range(3)])
                ktR = Rot([sb("kt%d" % i, [128, 4, 128], BF16) for i in range(2)])
                khR = Rot([sb("kh%d" % i, [128, 512], BF16) for i in range(3)])
                amR = Rot([sb("am%d" % i, [128, 4, 128], BF16) for i in range(2)])
                ofsR = Rot([sb("ofs%d" % i, [128, 1024], F32) + ("ofs%d" % i,) for i in range(2)])
                onR = Rot([sb("on%d" % i, [128, 1024], BF16) for i in range(2)])
                ss4R = Rot([sb("ss4%d" % i, [128, 4], F32) for i in range(2)])
                ycR = Rot([sb("ycs%d" % i, [128, 8, 128], BF16) + ("ycs%d" % i,) for i in range(2)])
                (pB1, pB1_b), (pB2, pB2_b) = bank[0], bank[1]
                ATs = [bank[2], bank[5]]
                pOb = [bank[3], bank[4]]
                pSU = [bank[6], bank[7]]
                S.dma("sp", "c1", out=tri[:], in_=tri_d.rearrange("k s t -> s k t"), writes=[tri_b])
                for hh in range(4):
                    S.dma("sp", "c2", out=maskF[:, hh, :], in_=tri_d[0], pwrites=[maskF_b])
                    S.dma("sp", "c3", out=maskB[:, hh, :], in_=tri_d[2], pwrites=[maskB_b])
                S.dma("sp", "c4", out=gbc[:], in_=gla_g[l].partition_broadcast(128), writes=[gbc_b])

                steps = [(0, c) for c in range(NCH)] + [(1, c) for c in [1, 0] + list(range(NCH - 1, 1, -1))]
                cur_S = [None]

                def stage_a(g, dr, c):
                    r0 = c * 128
                    pB1v = pB1[:].rearrange("p (h t) -> p h t", t=128)
                    cumM, restM = (tri[:, 0, :], tri[:, 1, :]) if dr == 0 else (tri[:, 2, :], tri[:, 3, :])
                    la_d, la_db = (laf, B["laf"]) if dr == 0 else (lab, B["lab"])
                    q_, q_b, q_s = qTl.next(); k_, k_b, k_s = kTl.next(); km, km_b, km_s = ktml.next()
                    v_, v_b, v_s = vl.next(); la, la_b, la_s = lal.next()
                    S.dma("sp", q_s, out=q_[:], in_=gqT[:, r0:r0 + 128].rearrange("(h d) t -> d h t", d=128), reads=[B["gqT"]], writes=[q_b])
                    S.dma("sp", k_s, out=k_[:], in_=gkT[:, r0:r0 + 128].rearrange("(h d) t -> d h t", d=128), reads=[B["gkT"]], writes=[k_b])
                    S.dma("sp", km_s, out=km[:], in_=gktm[r0:r0 + 128, :], reads=[B["gktm"]], writes=[km_b])
                    S.dma("sp", v_s, out=v_[:], in_=gv[r0:r0 + 128, :], reads=[B["gv"]], writes=[v_b])
                    S.dma("sp", la_s, out=la[:], in_=la_d[r0:r0 + 128, :], reads=[la_db], writes=[la_b])
                    Eq, Eq_b = EqR.next(); Ek, Ek_b = EkR.next(); Ekk, Ekk_b = EkkR.next()
                    qt, qt_b = qtR.next(); kt, kt_b = ktR.next(); kh, kh_b = khR.next()
                    for hh in range(4):
                        S.op("pe", lambda e, hh=hh: e.matmul(pB1v[:, hh, :], lhsT=la[:, hh * 128:(hh + 1) * 128], rhs=cumM, start=True, stop=True),
                             reads=[la_b, tri_b], writes=[pB1_b])
                    S.op("pe", lambda e: e.matmul(pB2[:, :], lhsT=restM, rhs=la[:, :], start=True, stop=True),
                         reads=[la_b, tri_b], writes=[pB2_b])
                    S.op("act", lambda e: e.activation(out=Eq[:], in_=pB1v, func=AF.Exp), reads=[pB1_b], writes=[Eq_b])
                    S.op("act", lambda e: e.activation(out=Ek[:], in_=pB1v, func=AF.Exp, scale=-1.0), reads=[pB1_b], writes=[Ek_b])
                    S.op("act", lambda e: e.activation(out=Ekk[:], in_=pB2[:, :], func=AF.Exp), reads=[pB2_b], writes=[Ekk_b])
                    S.op("dve", lambda e: e.tensor_tensor(out=qt[:], in0=q_[:], in1=Eq[:], op=ALU.mult), reads=[q_b, Eq_b], writes=[qt_b])
                    S.op("pool", lambda e: e.tensor_tensor(out=kt[:], in0=k_[:], in1=Ek[:], op=ALU.mult), reads=[k_b, Ek_b], writes=[kt_b])
                    S.op("pool", lambda e: e.tensor_tensor(out=kh[:], in0=km[:], in1=Ekk[:], op=ALU.mult), reads=[km_b, Ekk_b], writes=[kh_b])
                    cx = dict(g=g, dr=dr, c=c, r0=r0, v_=v_, v_b=v_b, Eq=Eq, Eq_b=Eq_b, qt=qt, qt_b=qt_b, kt=kt, kt_b=kt_b, kh=kh, kh_b=kh_b)
                    if dr == 1:
                        of_, of_b, of_s = ofl.next(); sz_, sz_b, sz_s = szl.next()
                        S.dma("sp", of_s, out=of_[:], in_=of_d[r0:r0 + 128, :], reads=[B["of"]], writes=[of_b])
                        S.dma("sp", sz_s, out=sz_[:], in_=szT[:, r0:r0 + 128].rearrange("(c p) t -> p c t", p=128),
                              reads=[B["szT"]], writes=[sz_b])
                        cx.update(of_=of_, of_b=of_b, sz_=sz_, sz_b=sz_b)
                    return cx

                def stage_b(cx):
                    g, dr = cx["g"], cx["dr"]
                    qt, qt_b, kt, kt_b = cx["qt"], cx["qt_b"], cx["kt"], cx["kt_b"]
                    pAT, pAT_b = ATs[g % 2]
                    pATv = pAT[:].rearrange("p (h t) -> p h t", t=128)
                    mask, mask_b = (maskF, maskF_b) if dr == 0 else (maskB, maskB_b)
                    am, am_b = amR.next()
                    for hh in range(4):
                        S.op("pe", lambda e, hh=hh: e.matmul(pATv[:, hh, :], lhsT=kt[:, hh, :], rhs=qt[:, hh, :], start=True, stop=True),
                             reads=[kt_b, qt_b], writes=[pAT_b])
                    S.op("dve", lambda e: e.tensor_tensor(out=am[:], in0=pATv, in1=mask[:], op=ALU.mult), reads=[pAT_b, mask_b], writes=[am_b])
                    cx.update(am=am, am_b=am_b)
                    return cx

                def back(cx):
                    g, dr, c, r0 = cx["g"], cx["dr"], cx["c"], cx["r0"]
                    v_, v_b, Eq, Eq_b, qt, qt_b = cx["v_"], cx["v_b"], cx["Eq"], cx["Eq_b"], cx["qt"], cx["qt_b"]
                    kh, kh_b, am, am_b = cx["kh"], cx["kh_b"], cx["am"], cx["am_b"]
                    pAT, pAT_b = ATs[g % 2]
                    pTB_b = pAT_b
                    pTBv = pAT[:].bitcast(BF16).rearrange("p (c t) -> p c t", t=128)
                    tl = 127 if dr == 0 else 0
                    if (dr, c) in ((0, 0), (1, 1)):
                        S.op("pool", lambda e: e.memset(Sst[:], 0.0), writes=[Sst_b])
                        Sbf0, Sbf0_b = SbfR.next()
                        S.op("pool", lambda e: e.memset(Sbf0[:], 0.0), writes=[Sbf0_b])
                        cur_S[0] = (Sbf0, Sbf0_b)
                    Sbf, Sbf_b = cur_S[0]
                    for hh in range(4):
                        su, su_b = pSU[hh // 2]
                        oc = (hh % 2) * 256
                        S.op("pe", lambda e, hh=hh, su=su, oc=oc: e.matmul(su[:, oc:oc + 256], lhsT=kh[:, hh * 128:(hh + 1) * 128], rhs=v_[:, hh * 256:(hh + 1) * 256],
                                                                        start=True, stop=True), reads=[kh_b, v_b], writes=[su_b])
                    for hh in range(4):
                        ob, ob_b = pOb[hh // 2]
                        oc = (hh % 2) * 256
                        S.op("pe", lambda e, hh=hh, ob=ob, oc=oc: e.matmul(ob[:, oc:oc + 256], lhsT=am[:, hh, :], rhs=v_[:, hh * 256:(hh + 1) * 256],
                                                                        start=True, stop=False), reads=[am_b, v_b], writes=[ob_b])
                        S.op("pe", lambda e, hh=hh, ob=ob, oc=oc: e.matmul(ob[:, oc:oc + 256], lhsT=qt[:, hh, :], rhs=Sbf[:, hh, :],
                                                                        start=False, stop=True), reads=[qt_b, Sbf_b], writes=[ob_b])
                    for hh in range(4):
                        su, su_b = pSU[hh // 2]
                        oc = (hh % 2) * 256
                        S.op("dve", lambda e, hh=hh, su=su, oc=oc: e.scalar_tensor_tensor(
                            out=Sst[:, hh, :], in0=Sst[:, hh, :], scalar=Eq[:, hh, tl:tl + 1], in1=su[:, oc:oc + 256],
                            op0=ALU.mult, op1=ALU.add), reads=[Sst_b, Eq_b, su_b], writes=[Sst_b])
                    Sbfn, Sbfn_b = SbfR.next()
                    S.op("act", lambda e: e.activation(out=Sbfn[:], in_=Sst[:], func=AF.Copy), reads=[Sst_b], writes=[Sbfn_b])
                    cur_S[0] = (Sbfn, Sbfn_b)
                    ofs, ofs_b, ofs_s = ofsR.next()
                    if dr == 0:
                        for hf in range(2):
                            ob, ob_b = pOb[hf]
                            S.op("act", lambda e, ob=ob, hf=hf: e.activation(out=ofs[:, hf * 512:(hf + 1) * 512], in_=ob[:, :], func=AF.Copy),
                                 reads=[ob_b], pwrites=[ofs_b])
                        S.dma("pool", ofs_s, out=of_d[r0:r0 + 128, :], in_=ofs[:], reads=[ofs_b], pwrites=[B["of"]])
                    elif not (last and c < 2):
                        of_, of_b, sz_, sz_b = cx["of_"], cx["of_b"], cx["sz_"], cx["sz_b"]
                        on, on_b = onR.next(); ss4, ss4_b = ss4R.next(); ycs, ycs_b, ycs_s = ycR.next()
                        for hf in range(2):
                            ob, ob_b = pOb[hf]
                            S.op("dve", lambda e, ob=ob, hf=hf: e.tensor_tensor(out=ofs[:, hf * 512:(hf + 1) * 512], in0=ob[:, :],
                                                                             in1=of_[:, hf * 512:(hf + 1) * 512], op=ALU.add),
                                 reads=[ob_b, of_b], pwrites=[ofs_b])
                        for hh in range(4):
                            S.op("act", lambda e, hh=hh: e.activation(out=on[:, hh * 256:(hh + 1) * 256], in_=ofs[:, hh * 256:(hh + 1) * 256],
                                                                      func=AF.Square, accum_out=ss4[:, hh:hh + 1]),
                                 reads=[ofs_b], pwrites=[on_b, ss4_b])
                        S.op("dve", lambda e: e.tensor_scalar(out=ss4[:], in0=ss4[:], scalar1=1.0 / 256, scalar2=EPS, op0=ALU.mult, op1=ALU.add),
                             reads=[ss4_b], writes=[ss4_b])
                        S.op("act", lambda e: e.activation(out=ss4[:], in_=ss4[:], func=AF.Sqrt), reads=[ss4_b], writes=[ss4_b])
                        S.op("dve", lambda e: e.reciprocal(out=ss4[:], in_=ss4[:]), reads=[ss4_b], writes=[ss4_b])
                        for hh in range(4):
                            S.op("dve", lambda e, hh=hh: e.scalar_tensor_tensor(
                                out=on[:, hh * 256:(hh + 1) * 256], in0=ofs[:, hh * 256:(hh + 1) * 256], scalar=ss4[:, hh:hh + 1],
                                in1=gbc[:], op0=ALU.mult, op1=ALU.mult), reads=[ofs_b, ss4_b, gbc_b, on_b], writes=[on_b])
                        for cc in range(8):
                            S.op("pe", lambda e, cc=cc: e.transpose(out=pTBv[:, cc, :], in_=on[:, cc * 128:(cc + 1) * 128], identity=identb[:]),
                                 reads=[on_b, identb_b], writes=[pTB_b])
                        S.op("dve", lambda e: e.tensor_tensor(out=ycs[:], in0=pTBv, in1=sz_[:], op=ALU.mult),
                             reads=[pTB_b, sz_b], writes=[ycs_b])
                        S.dma("pool", ycs_s, out=ycT[:, r0:r0 + 128].rearrange("(c p) t -> p c t", p=128), in_=ycs[:],
                              reads=[ycs_b], pwrites=[B["ycT"]])

                NS = len(steps)
                cxa = {0: stage_a(0, *steps[0])}
                if NS > 1:
                    cxa[1] = stage_a(1, *steps[1])
                cxb = {0: stage_b(cxa.pop(0))}
                for g in range(NS):
                    if g + 2 < NS:
                        cxa[g + 2] = stage_a(g + 2, *steps[g + 2])
                    if g + 1 < NS:
                        cxb[g + 1] = stage_b(cxa.pop(g + 1))
                    back(cxb.pop(g))
                S.barrier()
            if stop_after == "P4":
                return nc, S

            with ExitStack() as es:
                sb, _ = mk(es)
                wst = [sb("wstO%d" % i, [128, KC, 256], F32) for i in range(3)]
                wst_rot = Rot([(wst[i][0], wst[i][1], "wstO%d" % i) for i in range(3)])
                wA, wA_b = sb("wA", [128, KC, 1024], BF16)
                wB, wB_b = sb("wB", [128, KC, 1024], BF16)
                wC, wC_b = sb("wC", [128, KC, 1024], BF16)
                wO, wO_b = sb("wO", [128, KC, 1024], BF16)
                gtg, gtg_b = sb("gtgS", [128, 2, 1024], F32)
                yaR = Rot([sb("yal%d" % i, [128, 8, 512], BF16) + ("yal%d" % i,) for i in range(2)])
                ybR2 = Rot([sb("ybl%d" % i, [128, 8, 512], BF16) + ("ybl%d" % i,) for i in range(2)])
                ycR2 = Rot([sb("ycl%d" % i, [128, 8, 512], BF16) + ("ycl%d" % i,) for i in range(2)])
                mgR = Rot([sb("mgl%d" % i, [128, 3, 512], BF16) + ("mgl%d" % i,) for i in range(2)])
                mrgR = Rot([sb("mrg%d" % i, [128, 8, 512], BF16) for i in range(2)])
                m1R = Rot([sb("m1%d" % i, [128, 512], F32) for i in range(2)])
                m2R = Rot([sb("m2%d" % i, [128, 512], F32) for i in range(2)])
                m3R = Rot([sb("m3%d" % i, [128, 512], F32) for i in range(2)])
                xlR = Rot([sb("xl%d" % i, [128, 1024], F32) + ("xl%d" % i,) for i in range(2)])
                ostR = Rot([sb("ost%d" % i, [128, 1024], F32) + ("ost%d" % i,) for i in range(2)])
                ttR = Rot([sb("tt%d" % i, [128, 1024], F32) for i in range(2)])
                ss2R = Rot([sb("ss2%d" % i, [128, 2], F32) for i in range(2)])
                rsR = Rot([sb("rso%d" % i, [128, 1], F32) for i in range(2)])
                brR = Rot(bank[0:6])
                (pO0, pO0_b), (pO1, pO1_b) = bank[6], bank[7]
                load_w(w_bra[l], 1024, wst_rot, wA, wA_b, 0, True, cast_engs=("act", "dve"), dma_engs=("sp", "act"))
                load_w(w_brb[l], 1024, wst_rot, wB, wB_b, 0, True, cast_engs=("act", "dve"), dma_engs=("sp", "act"))
                load_w(w_brc[l], 1024, wst_rot, wC, wC_b, 0, True, cast_engs=("act", "dve"), dma_engs=("sp", "act"))
                load_w(w_out[l], 1024, wst_rot, wO, wO_b, 0, True, cast_engs=("act", "dve"), dma_engs=("sp", "act"))
                for j in range(2):
                    S.dma("sp", "c1", out=gtg[:, j, :], in_=gtg_d[l, j].partition_broadcast(128), reads=[B["gtg"]], pwrites=[gtg_b])
                mgTv = mgT.rearrange("(g j p) t -> j p g t", g=3, j=8, p=128)
                tiles = list(range(1, 9)) if last else list(range(0, 9))
                for tile in tiles:
                    t0, tw = tile_rng(tile)
                    jm = 1 if tile == 0 else 0
                    ya, ya_b, ya_s = yaR.next(); yb, yb_b, yb_s = ybR2.next(); yc, yc_b, yc_s = ycR2.next()
                    for (dst, dst_b, sem, src, src_b) in ((ya, ya_b, ya_s, yaT, B["yaT"]), (yb, yb_b, yb_s, ybT, B["ybT"]), (yc, yc_b, yc_s, ycT, B["ycT"])):
                        S.dma("sp", sem, out=dst[:, :, 0:tw], in_=src[:, t0:t0 + tw].rearrange("(c p) t -> p c t", p=128),
                              reads=[src_b], writes=[dst_b])
                    mrg, mrg_b = mrgR.next()
                    for j in range(8):
                        mg, mg_b, mg_s = mgR.next()
                        S.dma("sp", mg_s, out=mg[:, :, 0:tw], in_=mgTv[j][:, :, t0:t0 + tw], reads=[B["mgT"]], writes=[mg_b])
                        pbs = []
                        for (wt, wt_b, yy, yy_b) in ((wA, wA_b, ya, ya_b), (wB, wB_b, yb, yb_b), (wC, wC_b, yc, yc_b)):
                            pb, pb_b = brR.next()
                            for kc in range(KC):
                                S.op("pe", lambda e, pb=pb, wt=wt, yy=yy, kc=kc, j=j: e.matmul(
                                    pb[:, 0:tw], lhsT=wt[:, kc, j * 128:(j + 1) * 128], rhs=yy[:, kc, 0:tw],
                                    start=(kc == 0), stop=(kc == KC - 1)), reads=[wt_b, yy_b], writes=[pb_b])
                            pbs.append((pb, pb_b))
                        m1, m1_b = m1R.next(); m2, m2_b = m2R.next(); m3, m3_b = m3R.next()
                        for n, (mm, mm_b) in enumerate(((m1, m1_b), (m2, m2_b), (m3, m3_b))):
                            pb, pb_b = pbs[n]
                            S.op("dve", lambda e, mm=mm, pb=pb, n=n, mg=mg: e.tensor_tensor(out=mm[:, 0:tw], in0=pb[:, 0:tw], in1=mg[:, n, 0:tw], op=ALU.mult),
                                 reads=[pb_b, mg_b], writes=[mm_b])
                        S.op("pool", lambda e, m1=m1, m2=m2: e.tensor_tensor(out=m1[:, 0:tw], in0=m1[:, 0:tw], in1=m2[:, 0:tw], op=ALU.add),
                             reads=[m1_b, m2_b], writes=[m1_b])
                        S.op("pool", lambda e, m1=m1, m3=m3, mrg=mrg, j=j: e.tensor_tensor(out=mrg[:, j, 0:tw], in0=m1[:, 0:tw], in1=m3[:, 0:tw], op=ALU.add),
                             reads=[m1_b, m3_b], pwrites=[mrg_b])
                    for s_ in range(tw // 128):
                        tok0 = t0 + s_ * 128
                        src = xs_ctx[tok0:tok0 + 128, :] if tile == 0 else xs_lat[tok0 - 256:tok0 - 128, :]
                        dstd = xc1[tok0:tok0 + 128, :] if tile == 0 else xd_lat[tok0 - 256:tok0 - 128, :]
                        dstd_b = B["xc1"] if tile == 0 else xd_lat_b
                        xl, xl_b, xl_s = xlR.next(); ost, ost_b, ost_s = ostR.next(); tt, tt_b = ttR.next()
                        ss2, ss2_b = ss2R.next(); rs, rs_b = rsR.next()
                        S.dma("sp", xl_s, out=xl[:], in_=src, reads=xs_bufs, writes=[xl_b])
                        for hf, (po, po_b) in enumerate(((pO0, pO0_b), (pO1, pO1_b))):
                            for kc in range(KC):
                                S.op("pe", lambda e, po=po, kc=kc, hf=hf: e.matmul(
                                    po[:, :], lhsT=mrg[:, kc, s_ * 128:(s_ + 1) * 128], rhs=wO[:, kc, hf * 512:(hf + 1) * 512],
                                    start=(kc == 0), stop=(kc == KC - 1)), reads=[mrg_b, wO_b], writes=[po_b])
                        for hf, (po, po_b) in enumerate(((pO0, pO0_b), (pO1, pO1_b))):
                            S.op("act", lambda e, po=po, hf=hf: e.activation(out=tt[:, hf * 512:(hf + 1) * 512], in_=po[:, :], func=AF.Square,
                                                                            accum_out=ss2[:, hf:hf + 1]),
                                 reads=[po_b], pwrites=[tt_b, ss2_b])
                        S.op("dve", lambda e: e.tensor_tensor(out=rs[:], in0=ss2[:, 0:1], in1=ss2[:, 1:2], op=ALU.add), reads=[ss2_b], writes=[rs_b])
                        S.op("dve", lambda e: e.tensor_scalar(out=rs[:], in0=rs[:], scalar1=1.0 / D, scalar2=EPS, op0=ALU.mult, op1=ALU.add),
                             reads=[rs_b], writes=[rs_b])
                        S.op("act", lambda e: e.activation(out=rs[:], in_=rs[:], func=AF.Sqrt), reads=[rs_b], writes=[rs_b])
                        S.op("dve", lambda e: e.reciprocal(out=rs[:], in_=rs[:]), reads=[rs_b], writes=[rs_b])
                        for hf, (po, po_b) in enumerate(((pO0, pO0_b), (pO1, pO1_b))):
                            S.op("dve", lambda e, po=po, hf=hf: e.tensor_tensor(out=tt[:, hf * 512:(hf + 1) * 512], in0=po[:, :],
                                                                             in1=gtg[:, jm, hf * 512:(hf + 1) * 512], op=ALU.mult),
                                 reads=[po_b, gtg_b, tt_b], pwrites=[tt_b])
                        S.op("dve", lambda e: e.scalar_tensor_tensor(out=ost[:], in0=tt[:], scalar=rs[:], in1=xl[:], op0=ALU.mult, op1=ALU.add),
                             reads=[tt_b, rs_b, xl_b], writes=[ost_b])
                        S.dma("pool", ost_s, out=dstd, in_=ost[:], reads=[ost_b], pwrites=[dstd_b])
                S.barrier()
    S.barrier()
    return nc, S


def _lay(v):
    return np.ascontiguousarray(np.asarray(v).reshape(-1, 128).T)


def _rope_perm():
    d = np.arange(128)
    return np.where((d % 64) < 32, d + 32, d - 32)


def _consts():
    f32 = np.float32
    s = np.arange(128)[:, None]; t = np.arange(128)[None, :]
    tri4 = np.stack([(s <= t), (s > t), (s >= t), (s < t)]).astype(f32)
    tok = np.arange(4096)
    row = (tok // 64).astype(f32); col = (tok % 64).astype(f32)
    freqs = (f32(10000.0) ** (-np.arange(32, dtype=f32) * f32(2.0) / f32(64))).astype(f32)
    d = np.arange(128)
    axis = d // 64; half = (d % 64) // 32; p = d % 32
    pos = np.where(axis[:, None] == 0, row[None, :], col[None, :]).astype(f32)
    ang = (pos * freqs[p][:, None]).astype(f32)
    cosT = np.cos(ang).astype(f32)
    sinT = (np.sin(ang) * np.where(half == 0, -1.0, 1.0)[:, None]).astype(f32)
    return dict(ident=np.eye(128, dtype=f32), tri4=tri4,
                cosT=cosT.astype(ml_dtypes.bfloat16), sinT=sinT.astype(ml_dtypes.bfloat16))


def prep_inputs(inp):
    f32 = np.float32
    perm = _rope_perm()
    A = lambda v: np.ascontiguousarray(np.asarray(v, dtype=f32))
    w_in = A(inp["w_in"])
    qcols = np.concatenate([O_Q + h * 128 + perm for h in range(8)])
    kcols = np.concatenate([O_K + h * 128 + perm for h in range(2)])
    shared = dict(
        w_ada=A(inp["w_ada"]), b_adaT=np.stack([_lay(inp["b_ada"][i]) for i in range(2)]),
        g_preT=np.stack([_lay(inp["g_pre"][i]) for i in range(2)]),
        g_postT=np.stack([_lay(inp["g_post"][i]) for i in range(2)]),
        w_in=w_in, w_qp=np.ascontiguousarray(w_in[:, :, qcols]), w_kp=np.ascontiguousarray(w_in[:, :, kcols]),
        conv_wT=np.ascontiguousarray(np.stack([A(inp["conv_w"][i]).T.reshape(8, 128, 3).transpose(1, 0, 2) for i in range(2)])),
        qg=np.ascontiguousarray(np.stack([np.stack([A(inp["q_norm_g"][i]), A(inp["q_norm_g"][i])[perm]], -1) for i in range(2)])),
        kg=np.ascontiguousarray(np.stack([np.stack([A(inp["k_norm_g"][i]), A(inp["k_norm_g"][i])[perm]], -1) for i in range(2)])),
        wd_f=np.ascontiguousarray(np.concatenate([A(inp["w_decay_fwd"]), A(inp["b_decay_fwd"])[:, None, :]], 1)),
        wd_b=np.ascontiguousarray(np.concatenate([A(inp["w_decay_bwd"]), A(inp["b_decay_bwd"])[:, None, :]], 1)),
        gla_g=A(inp["gla_norm_g"]),
        w_br_conv=A(inp["w_br_conv"]), w_br_attn=A(inp["w_br_attn"]), w_br_gla=A(inp["w_br_gla"]),
        b_gateT=np.stack([_lay(inp["b_gate"][i]) for i in range(2)]), w_out=A(inp["w_out"]),
        **_consts())
    in_maps = []
    for b in range(8):
        m = dict(shared)
        m["x"] = A(inp["x"][b]); m["ctx"] = A(inp["ctx"][b])
        m["c2_d"] = np.ascontiguousarray(np.stack([_lay(inp["c"][b]), _lay(inp["c_ctx"])], -1).astype(f32))
        in_maps.append(m)
    return in_maps


def kernel(**inputs):
    in_maps = prep_inputs(inputs)
    nc, _ = build(2)
    res = run_bass_kernel_spmd(nc, in_maps, core_ids=list(range(8)))
    return np.stack([np.asarray(r["y"], dtype=np.float32) for r in res.results], 0)
```

```python
import numpy as np
import ml_dtypes
from contextlib import ExitStack
import concourse.bass as bass
import concourse.mybir as mybir
from concourse.bass_utils import run_bass_kernel_spmd

F32 = mybir.dt.float32
BF16 = mybir.dt.bfloat16
AF = mybir.ActivationFunctionType
ALU = mybir.AluOpType

D = 1024
KC = 8
T = 4352
NCH = 34
EPS = 1e-6
IN_W = 12832
O_AB, O_AC, O_AX, O_AZ = 0, 1024, 2048, 3072
O_Q, O_K, O_V, O_ZA = 4096, 5120, 5376, 5632
O_GQ, O_GK, O_GV, O_RF, O_RB, O_ZG, O_MG = 6656, 7168, 7680, 8704, 8720, 8736, 9760
ATTN_SCALE = 128 ** -0.5
GLA_SCALE = 128 ** -0.5
GLA_TAU = 16.0


def tile_rng(tile):
    return (0, 256) if tile == 0 else (256 + (tile - 1) * 512, 512)


class Buf:
    __slots__ = ("name", "w", "r")

    def __init__(self, name=""):
        self.name = name
        self.w = {}
        self.r = {}


class Sched:
    def __init__(self, nc):
        self.nc = nc
        self.engs = {"pe": nc.tensor, "act": nc.scalar, "dve": nc.vector,
                     "pool": nc.gpsimd, "sp": nc.sync}
        self.sems = {}
        self.cnt = {}
        for k in self.engs:
            self.sems["e_" + k] = nc.alloc_semaphore("e_" + k)
            self.cnt["e_" + k] = 0
        self.known = {k: {} for k in self.engs}
        self.nwaits = 0
        self.nops = 0

    def _deps(self, reads, writes, pwrites):
        need = {}

        def add(d):
            for k, v in d.items():
                if v > need.get(k, 0):
                    need[k] = v
        for b in reads:
            add(b.w)
        for b in writes:
            add(b.w)
            add(b.r)
        for b in pwrites:
            add(b.r)
        return need

    def _wait(self, eng, need):
        kn = self.known[eng]
        for k, v in need.items():
            if kn.get(k, 0) >= v:
                continue
            self.engs[eng].wait_ge(self.sems[k], v)
            kn[k] = v
            self.nwaits += 1

    def _mark(self, k, v, reads, writes, pwrites):
        for b in reads:
            if b.r.get(k, 0) < v:
                b.r[k] = v
        for b in writes:
            b.w = {k: v}
            b.r = {}
        for b in pwrites:
            if b.w.get(k, 0) < v:
                b.w[k] = v

    def op(self, eng, fn, reads=(), writes=(), pwrites=()):
        own = "e_" + eng
        need = self._deps(reads, writes, pwrites)
        if eng == "pe":
            need.pop(own, None)
        self._wait(eng, need)
        inst = fn(self.engs[eng])
        self.cnt[own] += 1
        inst.then_inc(self.sems[own], 1)
        self._mark(own, self.cnt[own], reads, writes, pwrites)
        self.nops += 1

    def dma(self, eng, sem, out, in_, reads=(), writes=(), pwrites=(), **kw):
        if sem not in self.sems:
            self.sems[sem] = self.nc.alloc_semaphore(sem)
            self.cnt[sem] = 0
        need = self._deps(reads, writes, pwrites)
        if self.cnt[sem] > 0:
            need[sem] = max(need.get(sem, 0), self.cnt[sem])
        self._wait(eng, need)
        inst = self.engs[eng].dma_start(out=out, in_=in_, **kw)
        self.cnt[sem] += 16
        inst.then_inc(self.sems[sem], 16)
        self._mark(sem, self.cnt[sem], reads, writes, pwrites)
        self.nops += 1

    def barrier(self):
        need = {k: v for k, v in self.cnt.items() if v > 0}
        for eng in self.engs:
            self._wait(eng, dict(need))


class Rot:
    def __init__(self, items):
        self.items = items
        self.i = 0

    def next(self):
        it = self.items[self.i % len(self.items)]
        self.i += 1
        return it


def build(n_layers=2, debug=(), stop_after=None):
    nc = bass.Bass("TRN2", target_bir_lowering=False)
    S = Sched(nc)

    def din(name, shape, dty=F32):
        return nc.dram_tensor(name, shape, dty, kind="ExternalInput").ap()

    def scr(name, shape, dty):
        kind = "ExternalOutput" if name in debug else "Internal"
        return nc.dram_tensor(name, shape, dty, kind=kind).ap()

    x_in = din("x", [4096, D]); ctx_in = din("ctx", [256, D])
    c2_d = din("c2_d", [128, 8, 2])
    w_ada = din("w_ada", [2, D, 3 * D]); b_adaT = din("b_adaT", [2, 128, 24])
    g_preT = din("g_preT", [2, 128, 8]); g_postT = din("g_postT", [2, 128, 8])
    w_in = din("w_in", [2, D, IN_W]); w_qp = din("w_qp", [2, D, 1024]); w_kp = din("w_kp", [2, D, 256])
    conv_wT = din("conv_wT", [2, 128, 8, 3])
    qg_d = din("qg", [2, 128, 2]); kg_d = din("kg", [2, 128, 2])
    wd_f = din("wd_f", [2, 17, 512]); wd_b = din("wd_b", [2, 17, 512])
    gla_g = din("gla_g", [2, 256])
    w_bra = din("w_br_conv", [2, D, D]); w_brb = din("w_br_attn", [2, D, D]); w_brc = din("w_br_gla", [2, D, D])
    b_gateT = din("b_gateT", [2, 128, 24]); w_out = din("w_out", [2, D, D])
    ident_d = din("ident", [128, 128]); tri_d = din("tri4", [4, 128, 128])
    cos_d = din("cosT", [128, 4096], BF16); sin_d = din("sinT", [128, 4096], BF16)
    y_out = nc.dram_tensor("y", [4096, D], F32, kind="ExternalOutput").ap()

    yaT = scr("yaT", [D, T], BF16); ybT = scr("ybT", [D, T], BF16); ycT = scr("ycT", [D, T], BF16)
    mgT = scr("mgT", [3 * D, T], BF16); szT = scr("szT", [D, T], BF16)
    gqT = scr("gqT", [512, T], F32); gkT = scr("gkT", [512, T], F32); gktm = scr("gktm", [T, 512], F32)
    gv = scr("gv", [T, D], BF16); laf = scr("laf", [T, 512], F32); lab = scr("lab", [T, 512], F32)
    of_d = scr("of", [T, D], F32)
    x1 = scr("x1", [4096, D], F32); xc1 = scr("xc1", [256, D], F32)
    gtg_d = scr("gtg", [2, 2, D], F32)
    B = {n: Buf(n) for n in ["yaT", "ybT", "ycT", "mgT", "szT", "gqT", "gkT", "gktm", "gv", "laf", "lab",
                             "of", "x1", "xc1", "gtg", "y"]}

    uid = [0]
    with ExitStack() as es0:
        def mk(es):
            def sb(name, shape, dty):
                uid[0] += 1
                return es.enter_context(nc.sbuf_tensor("%s_%d" % (name, uid[0]), shape, dty)), Buf(name)

            def ps(name, shape, dty=F32):
                uid[0] += 1
                return es.enter_context(nc.psum_tensor("%s_%d" % (name, uid[0]), shape, dty)), Buf(name)
            return sb, ps
        sb0, ps0 = mk(es0)
        identf, identf_b = sb0("identf", [128, 128], F32)
        identb, identb_b = sb0("identb", [128, 128], BF16)
        onesb, onesb_b = sb0("onesb", [128, 128], BF16)
        mod, mod_b = sb0("mod", [128, 24, 2], F32)
        gsc, gsc_b = sb0("gsc", [128, 8, 2], F32)
        gtgT, gtgT_b = sb0("gtgT", [128, 8, 2], F32)
        bank = [ps0("bank%d" % i, [128, 512], F32) for i in range(8)]

        S.dma("sp", "c0", out=identf[:], in_=ident_d[:, :], writes=[identf_b])
        S.op("dve", lambda e: e.tensor_copy(out=identb[:], in_=identf[:]), reads=[identf_b], writes=[identb_b])
        S.op("dve", lambda e: e.memset(onesb[:], 1.0), writes=[onesb_b])

        rr = [0]

        def load_w(src, W, wst_rot, dst, dst_b, off, first, cast_engs=("pool",), dma_engs=("sp",)):
            done = 0
            while done < W:
                w = min(256, W - done)
                st, st_b, sem = wst_rot.next()
                de = dma_engs[rr[0] % len(dma_engs)]
                ce = cast_engs[rr[0] % len(cast_engs)]
                rr[0] += 1
                S.dma(de, sem, out=st[:, :, 0:w],
                      in_=src[:, done:done + w].rearrange("(kc p) w -> p kc w", p=128), writes=[st_b])
                o = off + done
                if ce == "act":
                    fn = lambda e, st=st, w=w, o=o: e.activation(out=dst[:, :, o:o + w], in_=st[:, :, 0:w], func=AF.Copy)
                else:
                    fn = lambda e, st=st, w=w, o=o: e.tensor_copy(out=dst[:, :, o:o + w], in_=st[:, :, 0:w])
                if first and done == 0:
                    S.op(ce, fn, reads=[st_b], writes=[dst_b])
                else:
                    S.op(ce, fn, reads=[st_b], pwrites=[dst_b])
                done += w

        for l in range(n_layers):
            last = (l == n_layers - 1)
            xs_lat = x_in if l == 0 else x1
            xs_ctx = ctx_in if l == 0 else xc1
            xs_bufs = [] if l == 0 else [B["x1"], B["xc1"]]
            xd_lat = y_out if last else x1
            xd_lat_b = B["y"] if last else B["x1"]

            with ExitStack() as esL:
                sbL, _ = mk(esL)
                hT, _ = sbL("hT", [128, KC, T], BF16)
                hT_b = [Buf("hT%d" % i) for i in range(9)]
                wst = [sbL("wst%d" % i, [128, KC, 256], F32) for i in range(2)]
                wst_rot = Rot([(wst[i][0], wst[i][1], "wst%d" % i) for i in range(2)])
                wbf = [sbL("wbf%d" % i, [128, KC, 512], BF16) for i in range(2)]
                wbf_rot = Rot(wbf)

                def conv_weights(cb_):
                    wt_, wt_b_ = wbf_rot.next()
                    for n, off in enumerate((O_AB, O_AC, O_AX, O_AZ)):
                        load_w(w_in[l, :, off + cb_ * 128: off + (cb_ + 1) * 128], 128, wst_rot, wt_, wt_b_, n * 128, n == 0)
                    return wt_, wt_b_
                PRE_CW = [None]

                with ExitStack() as es:
                    sb, _ = mk(es)
                    c2, c2_b = sb("c2", [128, 8, 2], F32)
                    sc2, sc2_b = sb("sc2", [128, 8, 2], F32)
                    bada, bada_b = sb("bada", [128, 24], F32)
                    gpre, gpre_b = sb("gpre", [128, 8], F32)
                    gpost, gpost_b = sb("gpost", [128, 8], F32)
                    psA, psA_b = bank[0]
                    psAv = psA[:, 0:48].rearrange("p (j t) -> p j t", t=2)
                    S.dma("sp", "c1", out=c2[:], in_=c2_d[:, :, :], writes=[c2_b])
                    S.dma("sp", "c2", out=bada[:], in_=b_adaT[l], writes=[bada_b])
                    S.dma("sp", "c3", out=gpre[:], in_=g_preT[l], writes=[gpre_b])
                    S.dma("sp", "c4", out=gpost[:], in_=g_postT[l], writes=[gpost_b])
                    S.op("act", lambda e: e.activation(out=sc2[:], in_=c2[:], func=AF.Silu), reads=[c2_b], writes=[sc2_b])
                    for blk in range(12):
                        st, st_b, sem = wst_rot.next()
                        S.dma("sp", sem, out=st[:],
                              in_=w_ada[l, :, blk * 256:(blk + 1) * 256].rearrange("(kc p) w -> p kc w", p=128),
                              writes=[st_b])
                        for jj in range(2):
                            j = blk * 2 + jj
                            for kc in range(KC):
                                S.op("pe", lambda e, st=st, jj=jj, j=j, kc=kc: e.matmul(
                                    psAv[:, j, :], lhsT=st[:, kc, jj * 128:(jj + 1) * 128], rhs=sc2[:, kc, :],
                                    start=(kc == 0), stop=(kc == KC - 1)),
                                    reads=[st_b, sc2_b], writes=[psA_b])
                    for j in range(2):
                        S.op("dve", lambda e, j=j: e.tensor_tensor(out=mod[:, :, j], in0=psAv[:, :, j], in1=bada[:], op=ALU.add),
                             reads=[psA_b, bada_b], writes=[mod_b])
                    for j in range(2):
                        S.op("dve", lambda e, j=j: e.scalar_tensor_tensor(
                            out=gsc[:, :, j], in0=mod[:, 8:16, j], scalar=1.0, in1=gpre[:], op0=ALU.add, op1=ALU.mult),
                            reads=[mod_b, gpre_b], writes=[gsc_b])
                        S.op("dve", lambda e, j=j: e.tensor_tensor(out=gtgT[:, :, j], in0=mod[:, 16:24, j], in1=gpost[:], op=ALU.mult),
                             reads=[mod_b, gpost_b], writes=[gtgT_b])
                    for j in range(2):
                        S.dma("sp", "c5", out=gtg_d[l, j].rearrange("(kc p) -> p kc", p=128), in_=gtgT[:, :, j],
                              reads=[gtgT_b], writes=[B["gtg"]], allow_slow_non_contiguous=True)
                    S.barrier()

                with ExitStack() as es:
                    sb, _ = mk(es)
                    xin = Rot([sb("xin%d" % i, [128, D], F32) for i in range(2)])
                    xn = Rot([sb("xn%d" % i, [128, D], BF16) for i in range(2)])
                    ssr = Rot([sb("ss%d" % i, [128, 1], F32) for i in range(2)])
                    rsr = Rot([sb("rs%d" % i, [128, 1], F32) for i in range(2)])
                    psr = Rot(bank[0:2])
                    for i in range(NCH):
                        if i == 16:
                            PRE_CW[0] = conv_weights(0)
                        j = 1 if i < 2 else 0
                        src = xs_ctx[i * 128:(i + 1) * 128, :] if i < 2 else xs_lat[(i - 2) * 128:(i - 1) * 128, :]
                        tile = 0 if i < 2 else 1 + (i - 2) // 4
                        xi, xi_b = xin.next(); xv, xv_b = xn.next(); ss, ss_b = ssr.next(); rs, rs_b = rsr.next()
                        pT, pT_b = psr.next()
                        pTv = pT[:].bitcast(BF16).rearrange("p (c t) -> p c t", t=128)
                        S.dma("sp", "xin%d" % (i % 2), out=xi[:], in_=src, reads=xs_bufs, writes=[xi_b])
                        S.op("act", lambda e, xi=xi, xv=xv, ss=ss: e.activation(out=xv[:], in_=xi[:], func=AF.Square, accum_out=ss[:]),
                             reads=[xi_b], writes=[xv_b, ss_b])
                        S.op("dve", lambda e, ss=ss, rs=rs: e.tensor_scalar(out=rs[:], in0=ss[:], scalar1=1.0 / D, scalar2=EPS,
                                                                           op0=ALU.mult, op1=ALU.add), reads=[ss_b], writes=[rs_b])
                        S.op("act", lambda e, rs=rs: e.activation(out=rs[:], in_=rs[:], func=AF.Sqrt), reads=[rs_b], writes=[rs_b])
                        S.op("dve", lambda e, rs=rs: e.reciprocal(out=rs[:], in_=rs[:]), reads=[rs_b], writes=[rs_b])
                        S.op("act", lambda e, xi=xi, xv=xv, rs=rs: e.activation(out=xv[:], in_=xi[:], func=AF.Copy, scale=rs[:]),
                             reads=[xi_b, rs_b], writes=[xv_b])
                        for c in range(KC):
                            S.op("pe", lambda e, xv=xv, c=c, pTv=pTv: e.transpose(out=pTv[:, c, :], in_=xv[:, c * 128:(c + 1) * 128],
                                                                                  identity=identb[:]),
                                 reads=[xv_b, identb_b], writes=[pT_b])
                        for c in range(KC):
                            S.op("dve", lambda e, c=c, i=i, j=j, pTv=pTv: e.tensor_scalar(
                                out=hT[:, c, i * 128:(i + 1) * 128], in0=pTv[:, c, :],
                                scalar1=gsc[:, c, j:j + 1], scalar2=mod[:, c, j:j + 1], op0=ALU.mult, op1=ALU.add),
                                reads=[pT_b, gsc_b, mod_b], pwrites=[hT_b[tile]])
                    S.barrier()
                if stop_after == "P1":
                    hT_dbg = scr("hT_dbg", [128, KC, T], BF16)
                    S.dma("pool", "dbg", out=hT_dbg[:, :, :], in_=hT[:], reads=hT_b, writes=[Buf()])
                    S.barrier()
                    return nc, S

                def proj_fm(wt, wt_b, c0, M, tile, pb, pb_b):
                    t0, tw = tile_rng(tile)
                    for kc in range(KC):
                        S.op("pe", lambda e, kc=kc: e.matmul(pb[0:M, 0:tw], lhsT=wt[:, kc, c0:c0 + M], rhs=hT[:, kc, t0:t0 + tw],
                                                             start=(kc == 0), stop=(kc == KC - 1)),
                             reads=[wt_b, hT_b[tile]], writes=[pb_b])

                def proj_tm(wt, wt_b, c0, N, i, pb, pb_b):
                    tile = 0 if i < 2 else 1 + (i - 2) // 4
                    for kc in range(KC):
                        S.op("pe", lambda e, kc=kc: e.matmul(pb[:, 0:N], lhsT=hT[:, kc, i * 128:(i + 1) * 128], rhs=wt[:, kc, c0:c0 + N],
                                                             start=(kc == 0), stop=(kc == KC - 1)),
                             reads=[wt_b, hT_b[tile]], writes=[pb_b])

                with ExitStack() as es:
                    sb, _ = mk(es)
                    UW = T + 4
                    ufull, _ = sb("ufull", [128, UW], F32)
                    gbfull, _ = sb("gbfull", [128, T], F32)
                    u_b = [Buf("u%d" % i) for i in range(9)]
                    gb_b = [Buf("gb%d" % i) for i in range(9)]
                    upad_b = Buf("upad")
                    cw, cw_b = sb("cw", [128, 8, 3], F32)
                    bg, bg_b = sb("bg", [128, 24], F32)
                    wdf, wdf_b = sb("wdf", [17, 512], F32)
                    wdb, wdb_b = sb("wdb", [17, 512], F32)
                    tmpA = Rot([sb("tmpA%d" % i, [128, 512], F32) for i in range(2)])
                    tmpB = Rot([sb("tmpB%d" % i, [128, 512], F32) for i in range(2)])
                    ytmp = Rot([sb("ytmp%d" % i, [128, 512], F32) for i in range(2)])
                    ytmp2 = Rot([sb("ytmpb%d" % i, [128, 512], F32) for i in range(2)])
                    stf = Rot([sb("stf%d" % i, [128, 512], F32) + ("stf%d" % i,) for i in range(3)])
                    stb = Rot([sb("stb%d" % i, [128, 512], BF16) + ("stb%d" % i,) for i in range(3)])
                    rT = [sb("rT%d" % i, [32, 512], F32) for i in range(4)]
                    banks = Rot(bank)
                    S.dma("sp", "c1", out=cw[:], in_=conv_wT[l], writes=[cw_b])
                    S.dma("sp", "c2", out=bg[:], in_=b_gateT[l], writes=[bg_b])
                    S.dma("sp", "c3", out=wdf[:], in_=wd_f[l], writes=[wdf_b])
                    S.dma("sp", "c4", out=wdb[:], in_=wd_b[l], writes=[wdb_b])
                    for a in (0, 257, 4355):
                        wdt = 2 if a == 257 else 1
                        S.op("pool", lambda e, a=a, wdt=wdt: e.memset(ufull[:, a:a + wdt], 0.0), pwrites=[upad_b])
                    for t_, tb in rT:
                        S.op("pool", lambda e, t_=t_: e.memset(t_[:], 1.0), writes=[tb])

                    def ucol(tile):
                        t0, tw = tile_rng(tile)
                        return (1 if tile == 0 else 259 + (t0 - 256)), tw

                    wq, wq_ready, wq_pos = [], {}, [0]

                    def mk_loader(src, W):
                        def ld():
                            wt_, wt_b_ = wbf_rot.next()
                            load_w(src, W, wst_rot, wt_, wt_b_, 0, True)
                            return wt_, wt_b_
                        return ld

                    def w_issue(j):
                        if j < len(wq) and j not in wq_ready:
                            wq_ready[j] = wq[j]()

                    def next_w():
                        j = wq_pos[0]
                        w_issue(j)
                        r = wq_ready.pop(j)
                        wq_pos[0] += 1
                        w_issue(j + 1)
                        return r

                    for off_, ncols_ in ((O_GQ, 512), (O_GK, 512), (O_ZG, 1024), (O_MG, 3072)):
                        for c0_ in range(0, ncols_, 512):
                            wq.append(mk_loader(w_in[l, :, off_ + c0_: off_ + c0_ + min(512, ncols_ - c0_)], min(512, ncols_ - c0_)))
                    wq.append(mk_loader(w_in[l, :, O_GK:O_GK + 512], 512))
                    for half_ in range(2):
                        wq.append(mk_loader(w_in[l, :, O_GV + half_ * 512:O_GV + (half_ + 1) * 512], 512))
                    wq.append(mk_loader(w_in[l, :, O_RF:O_RF + 32], 32))

                    nxt_w = PRE_CW[0]
                    for cb in range(8):
                        wt, wt_b = nxt_w

                        def conv_piece(tile):
                            t0, tw = tile_rng(tile)
                            uc, _ = ucol(tile)
                            nb = [u_b[k] for k in (tile - 1, tile, tile + 1) if 0 <= k < 9] + [upad_b]
                            yt, yt_b = ytmp.next()
                            sbf, sbf_b, sbf_s = stb.next()
                            S.op("dve", lambda e, yt=yt, uc=uc, tw=tw, cb=cb: e.tensor_scalar(
                                out=yt[:, 0:tw], in0=ufull[:, uc - 1:uc - 1 + tw], scalar1=cw[:, cb, 0:1], scalar2=None, op0=ALU.mult),
                                reads=nb + [cw_b], writes=[yt_b])
                            S.op("dve", lambda e, yt=yt, uc=uc, tw=tw, cb=cb: e.scalar_tensor_tensor(
                                out=yt[:, 0:tw], in0=ufull[:, uc:uc + tw], scalar=cw[:, cb, 1:2], in1=yt[:, 0:tw],
                                op0=ALU.mult, op1=ALU.add), reads=nb + [cw_b, yt_b], writes=[yt_b])
                            S.op("dve", lambda e, yt=yt, uc=uc, tw=tw, cb=cb: e.scalar_tensor_tensor(
                                out=yt[:, 0:tw], in0=ufull[:, uc + 1:uc + 1 + tw], scalar=cw[:, cb, 2:3], in1=yt[:, 0:tw],
                                op0=ALU.mult, op1=ALU.add), reads=nb + [cw_b, yt_b], writes=[yt_b])
                            S.op("pool", lambda e, yt=yt, sbf=sbf, t0=t0, tw=tw: e.tensor_tensor(
                                out=sbf[:, 0:tw], in0=yt[:, 0:tw], in1=gbfull[:, t0:t0 + tw], op=ALU.mult),
                                reads=[yt_b, gb_b[tile]], writes=[sbf_b])
                            S.dma("pool", sbf_s, out=yaT[cb * 128:(cb + 1) * 128, t0:t0 + tw], in_=sbf[:, 0:tw],
                                  reads=[sbf_b], pwrites=[B["yaT"]])

                        for tile in range(9):
                            if tile == 3 and cb + 1 < 8:
                                nxt_w = conv_weights(cb + 1)
                            if tile == 3 and cb == 7:
                                w_issue(0)
                            t0, tw = tile_rng(tile)
                            uc, _ = ucol(tile)
                            pbs = [banks.next() for _ in range(4)]
                            for n in range(4):
                                proj_fm(wt, wt_b, n * 128, 128, tile, pbs[n][0], pbs[n][1])
                            (pB, pB_b), (pC, pC_b), (pX, pX_b), (pZ, pZ_b) = pbs
                            ta, ta_b = tmpA.next(); tb_, tb_b = tmpB.next()
                            S.op("act", lambda e, ta=ta, pZ=pZ, tw=tw: e.activation(out=ta[:, 0:tw], in_=pZ[:, 0:tw], func=AF.Silu),
                                 reads=[pZ_b], writes=[ta_b])
                            S.op("dve", lambda e, ta=ta, pB=pB, t0=t0, tw=tw: e.tensor_tensor(
                                out=gbfull[:, t0:t0 + tw], in0=pB[:, 0:tw], in1=ta[:, 0:tw], op=ALU.mult),
                                reads=[pB_b, ta_b], writes=[gb_b[tile]])
                            S.op("act", lambda e, tb_=tb_, pX=pX, tw=tw: e.activation(out=tb_[:, 0:tw], in_=pX[:, 0:tw], func=AF.Copy),
                                 reads=[pX_b], writes=[tb_b])
                            S.op("dve", lambda e, tb_=tb_, pC=pC, uc=uc, tw=tw: e.tensor_tensor(
                                out=ufull[:, uc:uc + tw], in0=pC[:, 0:tw], in1=tb_[:, 0:tw], op=ALU.mult),
                                reads=[pC_b, tb_b], writes=[u_b[tile]])
                            if tile >= 1:
                                conv_piece(tile - 1)
                        conv_piece(8)

                    def fm_slot(off, ncols, dst, dst_b, kind):
                        for c0 in range(0, ncols, 512):
                            W = min(512, ncols - c0)
                            wt, wt_b = next_w()
                            for cbi in range(W // 128):
                                row0 = c0 + cbi * 128
                                for tile in range(9):
                                    t0, tw = tile_rng(tile)
                                    pb, pb_b = banks.next()
                                    proj_fm(wt, wt_b, cbi * 128, 128, tile, pb, pb_b)
                                    if kind in ("gq", "gk"):
                                        st, st_b, st_s = stf.next()
                                        sc = GLA_SCALE if kind == "gq" else 1.0
                                        S.op("act", lambda e, st=st, pb=pb, tw=tw, sc=sc: e.activation(
                                            out=st[:, 0:tw], in_=pb[:, 0:tw], func=AF.Copy, scale=sc), reads=[pb_b], writes=[st_b])
                                    elif kind == "zg":
                                        st, st_b, st_s = stb.next()
                                        S.op("act", lambda e, st=st, pb=pb, tw=tw: e.activation(
                                            out=st[:, 0:tw], in_=pb[:, 0:tw], func=AF.Silu), reads=[pb_b], writes=[st_b])
                                    else:
                                        st, st_b, st_s = stb.next()
                                        jcol = row0 // 128
                                        S.op("act", lambda e, st=st, pb=pb, tw=tw, jcol=jcol: e.activation(
                                            out=st[:, 0:tw], in_=pb[:, 0:tw], func=AF.Sigmoid, bias=bg[:, jcol:jcol + 1]),
                                            reads=[pb_b, bg_b], writes=[st_b])
                                    S.dma("pool", st_s, out=dst[row0:row0 + 128, t0:t0 + tw], in_=st[:, 0:tw],
                                          reads=[st_b], pwrites=[dst_b])

                    fm_slot(O_GQ, 512, gqT, B["gqT"], "gq")
                    fm_slot(O_GK, 512, gkT, B["gkT"], "gk")
                    fm_slot(O_ZG, 1024, szT, B["szT"], "zg")
                    fm_slot(O_MG, 3072, mgT, B["mgT"], "mg")

                    wt, wt_b = next_w()
                    for i in range(NCH):
                        pb, pb_b = banks.next()
                        proj_tm(wt, wt_b, 0, 512, i, pb, pb_b)
                        st, st_b, st_s = stf.next()
                        S.op("act", lambda e, st=st, pb=pb: e.activation(out=st[:], in_=pb[:], func=AF.Copy), reads=[pb_b], writes=[st_b])
                        S.dma("pool", st_s, out=gktm[i * 128:(i + 1) * 128, :], in_=st[:], reads=[st_b], pwrites=[B["gktm"]])
                    for half in range(2):
                        wt, wt_b = next_w()
                        for i in range(NCH):
                            pb, pb_b = banks.next()
                            proj_tm(wt, wt_b, 0, 512, i, pb, pb_b)
                            st, st_b, st_s = stb.next()
                            S.op("act", lambda e, st=st, pb=pb: e.activation(out=st[:], in_=pb[:], func=AF.Copy), reads=[pb_b], writes=[st_b])
                            S.dma("pool", st_s, out=gv[i * 128:(i + 1) * 128, half * 512:(half + 1) * 512], in_=st[:],
                                  reads=[st_b], pwrites=[B["gv"]])

                    wt, wt_b = next_w()
                    assert wq_pos[0] == len(wq) and not wq_ready
                    for tile in range(9):
                        t0, tw = tile_rng(tile)
                        for dr in range(2):
                            r_, r_b = rT[dr * 2 + tile % 2]
                            pb, pb_b = banks.next()
                            proj_fm(wt, wt_b, dr * 16, 16, tile, pb, pb_b)
                            S.op("act", lambda e, r_=r_, pb=pb, tw=tw: e.activation(out=r_[0:16, 0:tw], in_=pb[0:16, 0:tw], func=AF.Copy),
                                 reads=[pb_b], writes=[r_b])
                            wd, wd_b_ = (wdf, wdf_b) if dr == 0 else (wdb, wdb_b)
                            dst, dst_b = (laf, B["laf"]) if dr == 0 else (lab, B["lab"])
                            for sub in range(tw // 128):
                                pz, pz_b = banks.next()
                                S.op("pe", lambda e, pz=pz, r_=r_, sub=sub, wd=wd: e.matmul(
                                    pz[:, :], lhsT=r_[0:17, sub * 128:(sub + 1) * 128], rhs=wd[0:17, :], start=True, stop=True),
                                    reads=[r_b, wd_b_], writes=[pz_b])
                                st, st_b, st_s = stf.next()
                                S.op("act", lambda e, st=st, pz=pz: e.activation(out=st[:], in_=pz[:], func=AF.Exp, scale=-1.0),
                                     reads=[pz_b], writes=[st_b])
                                S.op("act", lambda e, st=st: e.activation(out=st[:], in_=st[:], func=AF.Ln, bias=1.0),
                                     reads=[st_b], writes=[st_b])
                                S.op("dve", lambda e, st=st: e.tensor_scalar(out=st[:], in0=st[:], scalar1=-1.0 / GLA_TAU, scalar2=None, op0=ALU.mult),
                                     reads=[st_b], writes=[st_b])
                                r0 = t0 + sub * 128
                                S.dma("pool", st_s, out=dst[r0:r0 + 128, :], in_=st[:], reads=[st_b], pwrites=[dst_b])
                    S.barrier()
                if stop_after == "P2":
                    return nc, S

                with ExitStack() as es:
                    sb, _ = mk(es)
                    kT, _ = sb("kT", [128, 2, T], BF16)
                    kT_b = [Buf("kT0"), Buf("kT1")]
                    Vs, V_b = sb("Vs", [128, NCH, 256], BF16)
                    cosS, cos_b = sb("cosS", [128, 4096], BF16)
                    sinS, sin_b = sb("sinS", [128, 4096], BF16)
                    qgS, qgS_b = sb("qgS", [128, 2], F32)
                    kgS, kgS_b = sb("kgS", [128, 2], F32)
                    qaR = Rot([sb("qa%d" % i, [128, 512], F32) for i in range(2)])
                    qbR = Rot([sb("qb%d" % i, [128, 512], F32) for i in range(2)])
                    qsqR = Rot([sb("qsq%d" % i, [128, 512], BF16) for i in range(2)])
                    rstR = Rot([sb("rst%d" % i, [128, 512], F32) for i in range(2)])
                    qrR = Rot([sb("qr%d" % i, [128, 512], BF16) for i in range(2)])
                    szR = Rot([sb("szr%d" % i, [128, 512], F32) for i in range(2)])
                    pTR = Rot([sb("pT%d" % i, [128, 512], BF16) for i in range(3)])
                    rDR = Rot([sb("rD%d" % i, [128, 512], F32) for i in range(2)])
                    ybR = Rot([sb("ybs%d" % i, [128, 512], BF16) + ("ybs%d" % i,) for i in range(2)])
                    (pA, pA_b), (pBk, pBk_b), (pSS, pSS_b), (pZ, pZ_b) = bank[0], bank[1], bank[2], bank[3]
                    SbR = Rot([bank[4], bank[5]])
                    (pO, pO_b), (pD, pD_b) = bank[6], bank[7]
                    S.dma("sp", "c1", out=cosS[:], in_=cos_d[:, :], writes=[cos_b])
                    S.dma("sp", "c2", out=sinS[:], in_=sin_d[:, :], writes=[sin_b])
                    S.dma("sp", "c3", out=qgS[:], in_=qg_d[l], writes=[qgS_b])
                    S.dma("sp", "c4", out=kgS[:], in_=kg_d[l], writes=[kgS_b])

                    def normrope_a(gS, gS_b, tile, rope):
                        t0, tw = tile_rng(tile)
                        qa, qa_b = qaR.next(); qb, qb_b = qbR.next(); qsq, qsq_b = qsqR.next(); rst, rst_b = rstR.next()
                        S.op("act", lambda e: e.activation(out=qa[:, 0:tw], in_=pA[:, 0:tw], func=AF.Copy, scale=gS[:, 0:1]),
                             reads=[pA_b, gS_b], writes=[qa_b])
                        S.op("act", lambda e: e.activation(out=qsq[:, 0:tw], in_=pA[:, 0:tw], func=AF.Square),
                             reads=[pA_b], writes=[qsq_b])
                        if rope:
                            S.op("act", lambda e: e.activation(out=qb[:, 0:tw], in_=pBk[:, 0:tw], func=AF.Copy, scale=gS[:, 1:2]),
                                 reads=[pBk_b, gS_b], writes=[qb_b])
                        S.op("pe", lambda e: e.matmul(pSS[:, 0:tw], lhsT=onesb[:], rhs=qsq[:, 0:tw], start=True, stop=True),
                             reads=[onesb_b, qsq_b], writes=[pSS_b])
                        S.op("dve", lambda e: e.tensor_scalar(out=rst[:, 0:tw], in0=pSS[:, 0:tw], scalar1=1.0 / 128, scalar2=EPS,
                                                              op0=ALU.mult, op1=ALU.add), reads=[pSS_b], writes=[rst_b])
                        return dict(t0=t0, tw=tw, rope=rope, qa=qa, qa_b=qa_b, qb=qb, qb_b=qb_b, rst=rst, rst_b=rst_b)

                    def normrope_b(cx, out_ap, out_w, out_pw):
                        t0, tw, rope = cx["t0"], cx["tw"], cx["rope"]
                        qa, qa_b, qb, qb_b, rst, rst_b = cx["qa"], cx["qa_b"], cx["qb"], cx["qb_b"], cx["rst"], cx["rst_b"]
                        S.op("act", lambda e: e.activation(out=rst[:, 0:tw], in_=rst[:, 0:tw], func=AF.Ln), reads=[rst_b], writes=[rst_b])
                        S.op("act", lambda e: e.activation(out=rst[:, 0:tw], in_=rst[:, 0:tw], func=AF.Exp, scale=-0.5), reads=[rst_b], writes=[rst_b])
                        if rope:
                            lt0 = t0 - 256
                            S.op("dve", lambda e: e.tensor_tensor(out=qa[:, 0:tw], in0=qa[:, 0:tw], in1=cosS[:, lt0:lt0 + tw], op=ALU.mult),
                                 reads=[qa_b, cos_b], writes=[qa_b])
                            S.op("pool", lambda e: e.tensor_tensor(out=qb[:, 0:tw], in0=qb[:, 0:tw], in1=sinS[:, lt0:lt0 + tw], op=ALU.mult),
                                 reads=[qb_b, sin_b], writes=[qb_b])
                            S.op("dve", lambda e: e.tensor_tensor(out=qa[:, 0:tw], in0=qa[:, 0:tw], in1=qb[:, 0:tw], op=ALU.add),
                                 reads=[qa_b, qb_b], writes=[qa_b])
                        S.op("dve", lambda e: e.tensor_tensor(out=out_ap, in0=qa[:, 0:tw], in1=rst[:, 0:tw], op=ALU.mult),
                             reads=[qa_b, rst_b], writes=out_w, pwrites=out_pw)

                    def normrope(gS, gS_b, tile, rope, out_ap, out_w, out_pw):
                        normrope_b(normrope_a(gS, gS_b, tile, rope), out_ap, out_w, out_pw)

                    def PRE_HEAD0():
                        wt_, wt_b_ = wbf_rot.next()
                        load_w(w_in[l, :, O_Q:O_Q + 128], 128, wst_rot, wt_, wt_b_, 0, True)
                        load_w(w_qp[l, :, 0:128], 128, wst_rot, wt_, wt_b_, 128, False)
                        load_w(w_in[l, :, O_ZA:O_ZA + 128], 128, wst_rot, wt_, wt_b_, 256, False)
                        return wt_, wt_b_

                    wt, wt_b = wbf_rot.next()
                    load_w(w_in[l, :, O_K:O_K + 256], 256, wst_rot, wt, wt_b, 0, True)
                    load_w(w_kp[l, :, :], 256, wst_rot, wt, wt_b, 256, False)
                    wtV, wtV_b = wbf_rot.next()
                    load_w(w_in[l, :, O_V:O_V + 256], 256, wst_rot, wtV, wtV_b, 0, True)
                    for kv in range(2):
                        for tile in range(9):
                            t0, tw = tile_rng(tile)
                            proj_fm(wt, wt_b, kv * 128, 128, tile, pA, pA_b)
                            if tile > 0:
                                proj_fm(wt, wt_b, 256 + kv * 128, 128, tile, pBk, pBk_b)
                            normrope(kgS, kgS_b, tile, tile > 0, kT[:, kv, t0:t0 + tw], [], [kT_b[kv]])
                    wt, wt_b = wtV, wtV_b
                    HW0 = PRE_HEAD0()
                    for i in range(NCH):
                        pb, pb_b = SbR.next()
                        proj_tm(wt, wt_b, 0, 256, i, pb, pb_b)
                        S.op("act", lambda e, pb=pb, i=i: e.activation(out=Vs[:, i, :], in_=pb[:, 0:256], func=AF.Copy),
                             reads=[pb_b], pwrites=[V_b])

                    q_tiles = list(range(1, 9)) if last else list(range(0, 9))
                    def load_head_w(h):
                        wt, wt_b = wbf_rot.next()
                        load_w(w_in[l, :, O_Q + h * 128:O_Q + (h + 1) * 128], 128, wst_rot, wt, wt_b, 0, True)
                        load_w(w_qp[l, :, h * 128:(h + 1) * 128], 128, wst_rot, wt, wt_b, 128, False)
                        load_w(w_in[l, :, O_ZA + h * 128:O_ZA + (h + 1) * 128], 128, wst_rot, wt, wt_b, 256, False)
                        return wt, wt_b

                    def make_head(h, wt, wt_b):
                        kv = h // 4

                        def prologue_a(tile):
                            t0, tw = tile_rng(tile)
                            qr, qr_b = qrR.next(); sz, sz_b = szR.next()
                            proj_fm(wt, wt_b, 0, 128, tile, pA, pA_b)
                            if tile > 0:
                                proj_fm(wt, wt_b, 128, 128, tile, pBk, pBk_b)
                            proj_fm(wt, wt_b, 256, 128, tile, pZ, pZ_b)
                            cx = normrope_a(qgS, qgS_b, tile, tile > 0)
                            return dict(tile=tile, tw=tw, qr=qr, qr_b=qr_b, sz=sz, sz_b=sz_b, cx=cx)

                        def prologue_b(P):
                            tw, qr, qr_b, sz, sz_b = P["tw"], P["qr"], P["qr_b"], P["sz"], P["sz_b"]
                            normrope_b(P["cx"], qr[:, 0:tw], [qr_b], [])
                            S.op("act", lambda e: e.activation(out=sz[:, 0:tw], in_=pZ[:, 0:tw], func=AF.Exp, scale=-1.0),
                                 reads=[pZ_b], writes=[sz_b])
                            S.op("dve", lambda e: e.tensor_scalar(out=sz[:, 0:tw], in0=sz[:, 0:tw], scalar1=1.0, scalar2=None, op0=ALU.add),
                                 reads=[sz_b], writes=[sz_b])
                            S.op("dve", lambda e: e.reciprocal(out=sz[:, 0:tw], in_=sz[:, 0:tw]), reads=[sz_b], writes=[sz_b])
                            S.op("dve", lambda e: e.tensor_tensor(out=sz[:, 0:tw], in0=pZ[:, 0:tw], in1=sz[:, 0:tw], op=ALU.mult),
                                 reads=[pZ_b, sz_b], writes=[sz_b])

                        def inner(tile, qr, qr_b, sz, sz_b, hook=None):
                            t0, tw = tile_rng(tile)
                            kcs = [0, 1] if tile == 0 else list(range(NCH))
                            LA = 2
                            nk = len(kcs)
                            pslots = {}

                            def emit_s(idx):
                                kc = kcs[idx]
                                sbk, sbk_b = SbR.next()
                                p_, p_b = pTR.next()
                                pslots[idx] = (p_, p_b)
                                S.op("pe", lambda e, sbk=sbk, kc=kc: e.matmul(
                                    sbk[:, 0:tw], lhsT=kT[:, kv, kc * 128:(kc + 1) * 128], rhs=qr[:, 0:tw], start=True, stop=True),
                                    reads=[kT_b[kv], qr_b], writes=[sbk_b])
                                S.op("act", lambda e, sbk=sbk, p_=p_: e.activation(out=p_[:, 0:tw], in_=sbk[:, 0:tw], func=AF.Exp, scale=ATTN_SCALE),
                                     reads=[sbk_b], writes=[p_b])

                            for idx in range(min(LA, nk)):
                                emit_s(idx)
                            for idx in range(nk):
                                kc = kcs[idx]
                                p_, p_b = pslots.pop(idx)
                                first, lastk = (idx == 0), (idx == nk - 1)
                                S.op("pe", lambda e, p_=p_, kc=kc, first=first, lastk=lastk: e.matmul(
                                    pO[:, 0:tw], lhsT=Vs[:, kc, kv * 128:(kv + 1) * 128], rhs=p_[:, 0:tw], start=first, stop=lastk),
                                    reads=[V_b, p_b], writes=[pO_b])
                                S.op("pe", lambda e, p_=p_, first=first, lastk=lastk: e.matmul(
                                    pD[:, 0:tw], lhsT=onesb[:], rhs=p_[:, 0:tw], start=first, stop=lastk),
                                    reads=[onesb_b, p_b], writes=[pD_b])
                                if idx + LA < nk:
                                    emit_s(idx + LA)
                                if hook is not None and idx == min(6, nk - 1):
                                    hook()
                            rD, rD_b = rDR.next(); ybs, ybs_b, ybs_s = ybR.next()
                            S.op("dve", lambda e: e.reciprocal(out=rD[:, 0:tw], in_=pD[:, 0:tw]), reads=[pD_b], writes=[rD_b])
                            S.op("dve", lambda e: e.tensor_tensor(out=rD[:, 0:tw], in0=pO[:, 0:tw], in1=rD[:, 0:tw], op=ALU.mult),
                                 reads=[pO_b, rD_b], writes=[rD_b])
                            S.op("dve", lambda e: e.tensor_tensor(out=ybs[:, 0:tw], in0=rD[:, 0:tw], in1=sz[:, 0:tw], op=ALU.mult),
                                 reads=[rD_b, sz_b], writes=[ybs_b])
                            S.dma("pool", ybs_s, out=ybT[h * 128:(h + 1) * 128, t0:t0 + tw], in_=ybs[:, 0:tw],
                                  reads=[ybs_b], pwrites=[B["ybT"]])

                        return prologue_a, prologue_b, inner

                    HW = {0: HW0}
                    HF = {}

                    def head_fns(h):
                        if h not in HF:
                            HF[h] = make_head(h, *HW[h])
                        return HF[h]

                    items = [(h, tile) for h in range(8) for tile in q_tiles]
                    cur = head_fns(0)[0](items[0][1])
                    head_fns(0)[1](cur)
                    for n, (h, tile) in enumerate(items):
                        if tile == q_tiles[-2] and h + 1 < 8:
                            HW[h + 1] = load_head_w(h + 1)
                        if n + 1 < len(items):
                            nh, nt = items[n + 1]
                            nxt = head_fns(nh)[0](nt)
                            hook = (lambda nh=nh, nxt=nxt: head_fns(nh)[1](nxt))
                        else:
                            nxt, hook = None, None
                        head_fns(h)[2](tile, cur["qr"], cur["qr_b"], cur["sz"], cur["sz_b"], hook=hook)
                        cur = nxt
                    S.barrier()
            if stop_after == "P3":
                return nc, S

            with ExitStack() as es:
                sb, _ = mk(es)
                tri, tri_b = sb("tri", [128, 4, 128], F32)
                maskF, maskF_b = sb("maskF", [128, 4, 128], F32)
                maskB, maskB_b = sb("maskB", [128, 4, 128], F32)
                Sst, Sst_b = sb("Sst", [128, 4, 256], F32)
                SbfR = Rot([sb("Sbf%d" % i, [128, 4, 256], BF16) for i in range(2)])
                gbc, gbc_b = sb("gbc", [128, 256], F32)
                qTl = Rot([sb("qTl%d" % i, [128, 4, 128], F32) + ("qTl%d" % i,) for i in range(2)])
                kTl = Rot([sb("kTl%d" % i, [128, 4, 128], F32) + ("kTl%d" % i,) for i in range(2)])
                ktml = Rot([sb("ktml%d" % i, [128, 512], F32) + ("ktml%d" % i,) for i in range(2)])
                vl = Rot([sb("vl%d" % i, [128, 1024], BF16) + ("vl%d" % i,) for i in range(3)])
                lal = Rot([sb("lal%d" % i, [128, 512], F32) + ("lal%d" % i,) for i in range(2)])
                ofl = Rot([sb("ofl%d" % i, [128, 1024], F32) + ("ofl%d" % i,) for i in range(3)])
                szl = Rot([sb("szl%d" % i, [128, 8, 128], BF16) + ("szl%d" % i,) for i in range(3)])
                EqR = Rot([sb("Eq%d" % i, [128, 4, 128], F32) for i in range(3)])
                EkR = Rot([sb("Ek%d" % i, [128, 4, 128], F32) for i in range(2)])
                EkkR = Rot([sb("Ekk%d" % i, [128, 512], F32) for i in range(2)])
                qtR = Rot([sb("qt%d" % i, [128, 4, 128], BF16) for i in range(3)])
                ktR = Rot([sb("kt%d" % i, [128, 4, 128], BF16) for i in range(2)])
                khR = Rot([sb("kh%d" % i, [128, 512], BF16) for i in range(3)])
                amR = Rot([sb("am%d" % i, [128, 4, 128], BF16) for i in range(2)])
                ofsR = Rot([sb("ofs%d" % i, [128, 1024], F32) + ("ofs%d" % i,) for i in range(2)])
                onR = Rot([sb("on%d" % i, [128, 1024], BF16) for i in range(2)])
                ss4R = Rot([sb("ss4%d" % i, [128, 4], F32) for i in range(2)])
                ycR = Rot([sb("ycs%d" % i, [128, 8, 128], BF16) + ("ycs%d" % i,) for i in range(2)])
                (pB1, pB1_b), (pB2, pB2_b) = bank[0], bank[1]
                ATs = [bank[2], bank[5]]
                pOb = [bank[3], bank[4]]
                pSU = [bank[6], bank[7]]
                S.dma("sp", "c1", out=tri[:], in_=tri_d.rearrange("k s t -> s k t"), writes=[tri_b])
                for hh in range(4):
                    S.dma("sp", "c2", out=maskF[:, hh, :], in_=tri_d[0], pwrites=[maskF_b])
                    S.dma("sp", "c3", out=maskB[:, hh, :], in_=tri_d[2], pwrites=[maskB_b])
                S.dma("sp", "c4", out=gbc[:], in_=gla_g[l].partition_broadcast(128), writes=[gbc_b])

                steps = [(0, c) for c in range(NCH)] + [(1, c) for c in [1, 0] + list(range(NCH - 1, 1, -1))]
                cur_S = [None]

                def stage_a(g, dr, c):
                    r0 = c * 128
                    pB1v = pB1[:].rearrange("p (h t) -> p h t", t=128)
                    cumM, restM = (tri[:, 0, :], tri[:, 1, :]) if dr == 0 else (tri[:, 2, :], tri[:, 3, :])
                    la_d, la_db = (laf, B["laf"]) if dr == 0 else (lab, B["lab"])
                    q_, q_b, q_s = qTl.next(); k_, k_b, k_s = kTl.next(); km, km_b, km_s = ktml.next()
                    v_, v_b, v_s = vl.next(); la, la_b, la_s = lal.next()
                    S.dma("sp", q_s, out=q_[:], in_=gqT[:, r0:r0 + 128].rearrange("(h d) t -> d h t", d=128), reads=[B["gqT"]], writes=[q_b])
                    S.dma("sp", k_s, out=k_[:], in_=gkT[:, r0:r0 + 128].rearrange("(h d) t -> d h t", d=128), reads=[B["gkT"]], writes=[k_b])
                    S.dma("sp", km_s, out=km[:], in_=gktm[r0:r0 + 128, :], reads=[B["gktm"]], writes=[km_b])
                    S.dma("sp", v_s, out=v_[:], in_=gv[r0:r0 + 128, :], reads=[B["gv"]], writes=[v_b])
                    S.dma("sp", la_s, out=la[:], in_=la_d[r0:r0 + 128, :], reads=[la_db], writes=[la_b])
                    Eq, Eq_b = EqR.next(); Ek, Ek_b = EkR.next(); Ekk, Ekk_b = EkkR.next()
                    qt, qt_b = qtR.next(); kt, kt_b = ktR.next(); kh, kh_b = khR.next()
                    for hh in range(4):
                        S.op("pe", lambda e, hh=hh: e.matmul(pB1v[:, hh, :], lhsT=la[:, hh * 128:(hh + 1) * 128], rhs=cumM, start=True, stop=True),
                             reads=[la_b, tri_b], writes=[pB1_b])
                    S.op("pe", lambda e: e.matmul(pB2[:, :], lhsT=restM, rhs=la[:, :], start=True, stop=True),
                         reads=[la_b, tri_b], writes=[pB2_b])
                    S.op("act", lambda e: e.activation(out=Eq[:], in_=pB1v, func=AF.Exp), reads=[pB1_b], writes=[Eq_b])
                    S.op("act", lambda e: e.activation(out=Ek[:], in_=pB1v, func=AF.Exp, scale=-1.0), reads=[pB1_b], writes=[Ek_b])
                    S.op("act", lambda e: e.activation(out=Ekk[:], in_=pB2[:, :], func=AF.Exp), reads=[pB2_b], writes=[Ekk_b])
                    S.op("dve", lambda e: e.tensor_tensor(out=qt[:], in0=q_[:], in1=Eq[:], op=ALU.mult), reads=[q_b, Eq_b], writes=[qt_b])
                    S.op("pool", lambda e: e.tensor_tensor(out=kt[:], in0=k_[:], in1=Ek[:], op=ALU.mult), reads=[k_b, Ek_b], writes=[kt_b])
                    S.op("pool", lambda e: e.tensor_tensor(out=kh[:], in0=km[:], in1=Ekk[:], op=ALU.mult), reads=[km_b, Ekk_b], writes=[kh_b])
                    cx = dict(g=g, dr=dr, c=c, r0=r0, v_=v_, v_b=v_b, Eq=Eq, Eq_b=Eq_b, qt=qt, qt_b=qt_b, kt=kt, kt_b=kt_b, kh=kh, kh_b=kh_b)
                    if dr == 1:
                        of_, of_b, of_s = ofl.next(); sz_, sz_b, sz_s = szl.next()
                        S.dma("sp", of_s, out=of_[:], in_=of_d[r0:r0 + 128, :], reads=[B["of"]], writes=[of_b])
                        S.dma("sp", sz_s, out=sz_[:], in_=szT[:, r0:r0 + 128].rearrange("(c p) t -> p c t", p=128),
                              reads=[B["szT"]], writes=[sz_b])
                        cx.update(of_=of_, of_b=of_b, sz_=sz_, sz_b=sz_b)
                    return cx

                def stage_b(cx):
                    g, dr = cx["g"], cx["dr"]
                    qt, qt_b, kt, kt_b = cx["qt"], cx["qt_b"], cx["kt"], cx["kt_b"]
                    pAT, pAT_b = ATs[g % 2]
                    pATv = pAT[:].rearrange("p (h t) -> p h t", t=128)
                    mask, mask_b = (maskF, maskF_b) if dr == 0 else (maskB, maskB_b)
                    am, am_b = amR.next()
                    for hh in range(4):
                        S.op("pe", lambda e, hh=hh: e.matmul(pATv[:, hh, :], lhsT=kt[:, hh, :], rhs=qt[:, hh, :], start=True, stop=True),
                             reads=[kt_b, qt_b], writes=[pAT_b])
                    S.op("dve", lambda e: e.tensor_tensor(out=am[:], in0=pATv, in1=mask[:], op=ALU.mult), reads=[pAT_b, mask_b], writes=[am_b])
                    cx.update(am=am, am_b=am_b)
                    return cx

                def back(cx):
                    g, dr, c, r0 = cx["g"], cx["dr"], cx["c"], cx["r0"]
                    v_, v_b, Eq, Eq_b, qt, qt_b = cx["v_"], cx["v_b"], cx["Eq"], cx["Eq_b"], cx["qt"], cx["qt_b"]
                    kh, kh_b, am, am_b = cx["kh"], cx["kh_b"], cx["am"], cx["am_b"]
                    pAT, pAT_b = ATs[g % 2]
                    pTB_b = pAT_b
                    pTBv = pAT[:].bitcast(BF16).rearrange("p (c t) -> p c t", t=128)
                    tl = 127 if dr == 0 else 0
                    if (dr, c) in ((0, 0), (1, 1)):
                        S.op("pool", lambda e: e.memset(Sst[:], 0.0), writes=[Sst_b])
                        Sbf0, Sbf0_b = SbfR.next()
                        S.op("pool", lambda e: e.memset(Sbf0[:], 0.0), writes=[Sbf0_b])
                        cur_S[0] = (Sbf0, Sbf0_b)
                    Sbf, Sbf_b = cur_S[0]
                    for hh in range(4):
                        su, su_b = pSU[hh // 2]
                        oc = (hh % 2) * 256
                        S.op("pe", lambda e, hh=hh, su=su, oc=oc: e.matmul(su[:, oc:oc + 256], lhsT=kh[:, hh * 128:(hh + 1) * 128], rhs=v_[:, hh * 256:(hh + 1) * 256],
                                                                        start=True, stop=True), reads=[kh_b, v_b], writes=[su_b])
                    for hh in range(4):
                        ob, ob_b = pOb[hh // 2]
                        oc = (hh % 2) * 256
                        S.op("pe", lambda e, hh=hh, ob=ob, oc=oc: e.matmul(ob[:, oc:oc + 256], lhsT=am[:, hh, :], rhs=v_[:, hh * 256:(hh + 1) * 256],
                                                                        start=True, stop=False), reads=[am_b, v_b], writes=[ob_b])
                        S.op("pe", lambda e, hh=hh, ob=ob, oc=oc: e.matmul(ob[:, oc:oc + 256], lhsT=qt[:, hh, :], rhs=Sbf[:, hh, :],
                                                                        start=False, stop=True), reads=[qt_b, Sbf_b], writes=[ob_b])
                    for hh in range(4):
                        su, su_b = pSU[hh // 2]
                        oc = (hh % 2) * 256
                        S.op("dve", lambda e, hh=hh, su=su, oc=oc: e.scalar_tensor_tensor(
                            out=Sst[:, hh, :], in0=Sst[:, hh, :], scalar=Eq[:, hh, tl:tl + 1], in1=su[:, oc:oc + 256],
                            op0=ALU.mult, op1=ALU.add), reads=[Sst_b, Eq_b, su_b], writes=[Sst_b])
                    Sbfn, Sbfn_b = SbfR.next()
                    S.op("act", lambda e: e.activation(out=Sbfn[:], in_=Sst[:], func=AF.Copy), reads=[Sst_b], writes=[Sbfn_b])
                    cur_S[0] = (Sbfn, Sbfn_b)
                    ofs, ofs_b, ofs_s = ofsR.next()
                    if dr == 0:
                        for hf in range(2):
                            ob, ob_b = pOb[hf]
                            S.op("act", lambda e, ob=ob, hf=hf: e.activation(out=ofs[:, hf * 512:(hf + 1) * 512], in_=ob[:, :], func=AF.Copy),
                                 reads=[ob_b], pwrites=[ofs_b])
                        S.dma("pool", ofs_s, out=of_d[r0:r0 + 128, :], in_=ofs[:], reads=[ofs_b], pwrites=[B["of"]])
                    elif not (last and c < 2):
                        of_, of_b, sz_, sz_b = cx["of_"], cx["of_b"], cx["sz_"], cx["sz_b"]
                        on, on_b = onR.next(); ss4, ss4_b = ss4R.next(); ycs, ycs_b, ycs_s = ycR.next()
                        for hf in range(2):
                            ob, ob_b = pOb[hf]
                            S.op("dve", lambda e, ob=ob, hf=hf: e.tensor_tensor(out=ofs[:, hf * 512:(hf + 1) * 512], in0=ob[:, :],
                                                                             in1=of_[:, hf * 512:(hf + 1) * 512], op=ALU.add),
                                 reads=[ob_b, of_b], pwrites=[ofs_b])
                        for hh in range(4):
                            S.op("act", lambda e, hh=hh: e.activation(out=on[:, hh * 256:(hh + 1) * 256], in_=ofs[:, hh * 256:(hh + 1) * 256],
                                                                      func=AF.Square, accum_out=ss4[:, hh:hh + 1]),
                                 reads=[ofs_b], pwrites=[on_b, ss4_b])
                        S.op("dve", lambda e: e.tensor_scalar(out=ss4[:], in0=ss4[:], scalar1=1.0 / 256, scalar2=EPS, op0=ALU.mult, op1=ALU.add),
                             reads=[ss4_b], writes=[ss4_b])
                        S.op("act", lambda e: e.activation(out=ss4[:], in_=ss4[:], func=AF.Ln), reads=[ss4_b], writes=[ss4_b])
                        S.op("act", lambda e: e.activation(out=ss4[:], in_=ss4[:], func=AF.Exp, scale=-0.5), reads=[ss4_b], writes=[ss4_b])
                        for hh in range(4):
                            S.op("dve", lambda e, hh=hh: e.scalar_tensor_tensor(
                                out=on[:, hh * 256:(hh + 1) * 256], in0=ofs[:, hh * 256:(hh + 1) * 256], scalar=ss4[:, hh:hh + 1],
                                in1=gbc[:], op0=ALU.mult, op1=ALU.mult), reads=[ofs_b, ss4_b, gbc_b, on_b], writes=[on_b])
                        for cc in range(8):
                            S.op("pe", lambda e, cc=cc: e.transpose(out=pTBv[:, cc, :], in_=on[:, cc * 128:(cc + 1) * 128], identity=identb[:]),
                                 reads=[on_b, identb_b], writes=[pTB_b])
                        S.op("dve", lambda e: e.tensor_tensor(out=ycs[:], in0=pTBv, in1=sz_[:], op=ALU.mult),
                             reads=[pTB_b, sz_b], writes=[ycs_b])
                        S.dma("pool", ycs_s, out=ycT[:, r0:r0 + 128].rearrange("(c p) t -> p c t", p=128), in_=ycs[:],
                              reads=[ycs_b], pwrites=[B["ycT"]])

                NS = len(steps)
                cxa = {0: stage_a(0, *steps[0])}
                if NS > 1:
                    cxa[1] = stage_a(1, *steps[1])
                cxb = {0: stage_b(cxa.pop(0))}
                for g in range(NS):
                    if g + 2 < NS:
                        cxa[g + 2] = stage_a(g + 2, *steps[g + 2])
                    if g + 1 < NS:
                        cxb[g + 1] = stage_b(cxa.pop(g + 1))
                    back(cxb.pop(g))
                S.barrier()
            if stop_after == "P4":
                return nc, S

            with ExitStack() as es:
                sb, _ = mk(es)
                wst = [sb("wstO%d" % i, [128, KC, 256], F32) for i in range(3)]
                wst_rot = Rot([(wst[i][0], wst[i][1], "wstO%d" % i) for i in range(3)])
                wA, wA_b = sb("wA", [128, KC, 1024], BF16)
                wB, wB_b = sb("wB", [128, KC, 1024], BF16)
                wC, wC_b = sb("wC", [128, KC, 1024], BF16)
                wO, wO_b = sb("wO", [128, KC, 1024], BF16)
                gtg, gtg_b = sb("gtgS", [128, 2, 1024], F32)
                yaR = Rot([sb("yal%d" % i, [128, 8, 512], BF16) + ("yal%d" % i,) for i in range(2)])
                ybR2 = Rot([sb("ybl%d" % i, [128, 8, 512], BF16) + ("ybl%d" % i,) for i in range(2)])
                ycR2 = Rot([sb("ycl%d" % i, [128, 8, 512], BF16) + ("ycl%d" % i,) for i in range(2)])
                mgR = Rot([sb("mgl%d" % i, [128, 3, 512], BF16) + ("mgl%d" % i,) for i in range(2)])
                mrgR = Rot([sb("mrg%d" % i, [128, 8, 512], BF16) for i in range(2)])
                m1R = Rot([sb("m1%d" % i, [128, 512], F32) for i in range(2)])
                m2R = Rot([sb("m2%d" % i, [128, 512], F32) for i in range(2)])
                m3R = Rot([sb("m3%d" % i, [128, 512], F32) for i in range(2)])
                xlR = Rot([sb("xl%d" % i, [128, 1024], F32) + ("xl%d" % i,) for i in range(2)])
                ostR = Rot([sb("ost%d" % i, [128, 1024], F32) + ("ost%d" % i,) for i in range(2)])
                ttR = Rot([sb("tt%d" % i, [128, 1024], F32) for i in range(2)])
                ss2R = Rot([sb("ss2%d" % i, [128, 2], F32) for i in range(2)])
                rsR = Rot([sb("rso%d" % i, [128, 1], F32) for i in range(2)])
                brR = Rot(bank[0:6])
                (pO0, pO0_b), (pO1, pO1_b) = bank[6], bank[7]
                load_w(w_bra[l], 1024, wst_rot, wA, wA_b, 0, True, cast_engs=("act", "dve"), dma_engs=("sp", "act"))
                load_w(w_brb[l], 1024, wst_rot, wB, wB_b, 0, True, cast_engs=("act", "dve"), dma_engs=("sp", "act"))
                load_w(w_brc[l], 1024, wst_rot, wC, wC_b, 0, True, cast_engs=("act", "dve"), dma_engs=("sp", "act"))
                load_w(w_out[l], 1024, wst_rot, wO, wO_b, 0, True, cast_engs=("act", "dve"), dma_engs=("sp", "act"))
                for j in range(2):
                    S.dma("sp", "c1", out=gtg[:, j, :], in_=gtg_d[l, j].partition_broadcast(128), reads=[B["gtg"]], pwrites=[gtg_b])
                mgTv = mgT.rearrange("(g j p) t -> j p g t", g=3, j=8, p=128)
                tiles = list(range(1, 9)) if last else list(range(0, 9))
                for tile in tiles:
                    t0, tw = tile_rng(tile)
                    jm = 1 if tile == 0 else 0
                    ya, ya_b, ya_s = yaR.next(); yb, yb_b, yb_s = ybR2.next(); yc, yc_b, yc_s = ycR2.next()
                    for (dst, dst_b, sem, src, src_b) in ((ya, ya_b, ya_s, yaT, B["yaT"]), (yb, yb_b, yb_s, ybT, B["ybT"]), (yc, yc_b, yc_s, ycT, B["ycT"])):
                        S.dma("sp", sem, out=dst[:, :, 0:tw], in_=src[:, t0:t0 + tw].rearrange("(c p) t -> p c t", p=128),
                              reads=[src_b], writes=[dst_b])
                    mrg, mrg_b = mrgR.next()
                    for j in range(8):
                        mg, mg_b, mg_s = mgR.next()
                        S.dma("sp", mg_s, out=mg[:, :, 0:tw], in_=mgTv[j][:, :, t0:t0 + tw], reads=[B["mgT"]], writes=[mg_b])
                        pbs = []
                        for (wt, wt_b, yy, yy_b) in ((wA, wA_b, ya, ya_b), (wB, wB_b, yb, yb_b), (wC, wC_b, yc, yc_b)):
                            pb, pb_b = brR.next()
                            for kc in range(KC):
                                S.op("pe", lambda e, pb=pb, wt=wt, yy=yy, kc=kc, j=j: e.matmul(
                                    pb[:, 0:tw], lhsT=wt[:, kc, j * 128:(j + 1) * 128], rhs=yy[:, kc, 0:tw],
                                    start=(kc == 0), stop=(kc == KC - 1)), reads=[wt_b, yy_b], writes=[pb_b])
                            pbs.append((pb, pb_b))
                        m1, m1_b = m1R.next(); m2, m2_b = m2R.next(); m3, m3_b = m3R.next()
                        for n, (mm, mm_b) in enumerate(((m1, m1_b), (m2, m2_b), (m3, m3_b))):
                            pb, pb_b = pbs[n]
                            S.op("dve", lambda e, mm=mm, pb=pb, n=n, mg=mg: e.tensor_tensor(out=mm[:, 0:tw], in0=pb[:, 0:tw], in1=mg[:, n, 0:tw], op=ALU.mult),
                                 reads=[pb_b, mg_b], writes=[mm_b])
                        S.op("pool", lambda e, m1=m1, m2=m2: e.tensor_tensor(out=m1[:, 0:tw], in0=m1[:, 0:tw], in1=m2[:, 0:tw], op=ALU.add),
                             reads=[m1_b, m2_b], writes=[m1_b])
                        S.op("pool", lambda e, m1=m1, m3=m3, mrg=mrg, j=j: e.tensor_tensor(out=mrg[:, j, 0:tw], in0=m1[:, 0:tw], in1=m3[:, 0:tw], op=ALU.add),
                             reads=[m1_b, m3_b], pwrites=[mrg_b])
                    for s_ in range(tw // 128):
                        tok0 = t0 + s_ * 128
                        src = xs_ctx[tok0:tok0 + 128, :] if tile == 0 else xs_lat[tok0 - 256:tok0 - 128, :]
                        dstd = xc1[tok0:tok0 + 128, :] if tile == 0 else xd_lat[tok0 - 256:tok0 - 128, :]
                        dstd_b = B["xc1"] if tile == 0 else xd_lat_b
                        xl, xl_b, xl_s = xlR.next(); ost, ost_b, ost_s = ostR.next(); tt, tt_b = ttR.next()
                        ss2, ss2_b = ss2R.next(); rs, rs_b = rsR.next()
                        S.dma("sp", xl_s, out=xl[:], in_=src, reads=xs_bufs, writes=[xl_b])
                        for hf, (po, po_b) in enumerate(((pO0, pO0_b), (pO1, pO1_b))):
                            for kc in range(KC):
                                S.op("pe", lambda e, po=po, kc=kc, hf=hf: e.matmul(
                                    po[:, :], lhsT=mrg[:, kc, s_ * 128:(s_ + 1) * 128], rhs=wO[:, kc, hf * 512:(hf + 1) * 512],
                                    start=(kc == 0), stop=(kc == KC - 1)), reads=[mrg_b, wO_b], writes=[po_b])
                        for hf, (po, po_b) in enumerate(((pO0, pO0_b), (pO1, pO1_b))):
                            S.op("act", lambda e, po=po, hf=hf: e.activation(out=tt[:, hf * 512:(hf + 1) * 512], in_=po[:, :], func=AF.Square,
                                                                            accum_out=ss2[:, hf:hf + 1]),
                                 reads=[po_b], pwrites=[tt_b, ss2_b])
                        S.op("dve", lambda e: e.tensor_tensor(out=rs[:], in0=ss2[:, 0:1], in1=ss2[:, 1:2], op=ALU.add), reads=[ss2_b], writes=[rs_b])
                        S.op("dve", lambda e: e.tensor_scalar(out=rs[:], in0=rs[:], scalar1=1.0 / D, scalar2=EPS, op0=ALU.mult, op1=ALU.add),
                             reads=[rs_b], writes=[rs_b])
                        S.op("act", lambda e: e.activation(out=rs[:], in_=rs[:], func=AF.Sqrt), reads=[rs_b], writes=[rs_b])
                        S.op("dve", lambda e: e.reciprocal(out=rs[:], in_=rs[:]), reads=[rs_b], writes=[rs_b])
                        for hf, (po, po_b) in enumerate(((pO0, pO0_b), (pO1, pO1_b))):
                            S.op("dve", lambda e, po=po, hf=hf: e.tensor_tensor(out=tt[:, hf * 512:(hf + 1) * 512], in0=po[:, :],
                                                                             in1=gtg[:, jm, hf * 512:(hf + 1) * 512], op=ALU.mult),
                                 reads=[po_b, gtg_b, tt_b], pwrites=[tt_b])
                        S.op("dve", lambda e: e.scalar_tensor_tensor(out=ost[:], in0=tt[:], scalar=rs[:], in1=xl[:], op0=ALU.mult, op1=ALU.add),
                             reads=[tt_b, rs_b, xl_b], writes=[ost_b])
                        S.dma("pool", ost_s, out=dstd, in_=ost[:], reads=[ost_b], pwrites=[dstd_b])
                S.barrier()
    S.barrier()
    return nc, S


def _lay(v):
    return np.ascontiguousarray(np.asarray(v).reshape(-1, 128).T)


def _rope_perm():
    d = np.arange(128)
    return np.where((d % 64) < 32, d + 32, d - 32)


def _consts():
    f32 = np.float32
    s = np.arange(128)[:, None]; t = np.arange(128)[None, :]
    tri4 = np.stack([(s <= t), (s > t), (s >= t), (s < t)]).astype(f32)
    tok = np.arange(4096)
    row = (tok // 64).astype(f32); col = (tok % 64).astype(f32)
    freqs = (f32(10000.0) ** (-np.arange(32, dtype=f32) * f32(2.0) / f32(64))).astype(f32)
    d = np.arange(128)
    axis = d // 64; half = (d % 64) // 32; p = d % 32
    pos = np.where(axis[:, None] == 0, row[None, :], col[None, :]).astype(f32)
    ang = (pos * freqs[p][:, None]).astype(f32)
    cosT = np.cos(ang).astype(f32)
    sinT = (np.sin(ang) * np.where(half == 0, -1.0, 1.0)[:, None]).astype(f32)
    return dict(ident=np.eye(128, dtype=f32), tri4=tri4,
                cosT=cosT.astype(ml_dtypes.bfloat16), sinT=sinT.astype(ml_dtypes.bfloat16))


def prep_inputs(inp):
    f32 = np.float32
    perm = _rope_perm()
    A = lambda v: np.ascontiguousarray(np.asarray(v, dtype=f32))
    w_in = A(inp["w_in"])
    qcols = np.concatenate([O_Q + h * 128 + perm for h in range(8)])
    kcols = np.concatenate([O_K + h * 128 + perm for h in range(2)])
    shared = dict(
        w_ada=A(inp["w_ada"]), b_adaT=np.stack([_lay(inp["b_ada"][i]) for i in range(2)]),
        g_preT=np.stack([_lay(inp["g_pre"][i]) for i in range(2)]),
        g_postT=np.stack([_lay(inp["g_post"][i]) for i in range(2)]),
        w_in=w_in, w_qp=np.ascontiguousarray(w_in[:, :, qcols]), w_kp=np.ascontiguousarray(w_in[:, :, kcols]),
        conv_wT=np.ascontiguousarray(np.stack([A(inp["conv_w"][i]).T.reshape(8, 128, 3).transpose(1, 0, 2) for i in range(2)])),
        qg=np.ascontiguousarray(np.stack([np.stack([A(inp["q_norm_g"][i]), A(inp["q_norm_g"][i])[perm]], -1) for i in range(2)])),
        kg=np.ascontiguousarray(np.stack([np.stack([A(inp["k_norm_g"][i]), A(inp["k_norm_g"][i])[perm]], -1) for i in range(2)])),
        wd_f=np.ascontiguousarray(np.concatenate([A(inp["w_decay_fwd"]), A(inp["b_decay_fwd"])[:, None, :]], 1)),
        wd_b=np.ascontiguousarray(np.concatenate([A(inp["w_decay_bwd"]), A(inp["b_decay_bwd"])[:, None, :]], 1)),
        gla_g=A(inp["gla_norm_g"]),
        w_br_conv=A(inp["w_br_conv"]), w_br_attn=A(inp["w_br_attn"]), w_br_gla=A(inp["w_br_gla"]),
        b_gateT=np.stack([_lay(inp["b_gate"][i]) for i in range(2)]), w_out=A(inp["w_out"]),
        **_consts())
    in_maps = []
    for b in range(8):
        m = dict(shared)
        m["x"] = A(inp["x"][b]); m["ctx"] = A(inp["ctx"][b])
        m["c2_d"] = np.ascontiguousarray(np.stack([_lay(inp["c"][b]), _lay(inp["c_ctx"])], -1).astype(f32))
        in_maps.append(m)
    return in_maps


def kernel(**inputs):
    in_maps = prep_inputs(inputs)
    nc, _ = build(2)
    res = run_bass_kernel_spmd(nc, in_maps, core_ids=list(range(8)))
    return np.stack([np.asarray(r["y"], dtype=np.float32) for r in res.results], 0)
```
